# Optimizing a Trainium2 kernel written in Bass

```python
import math
import jax
import jax.numpy as jnp
from jax import lax
import numpy as np

D_MODEL = 1024
BATCH = 4
SEQ = 8192
DEPTH = 2

HEAD_DIM = 64
GRID_W = 64
NEG_INF = -1e30

MLA_HEADS = 4
MLA_NOPE = 64
MLA_ROPE = 32
MLA_V = 64
Q_LORA = 256
KV_LORA = 128
ROPE_THETA = 10000.0
Q_BLOCK = 128

SWA_HEADS = 4
SWA_KV_HEADS = 2
SWA_HALF = 128

NA_HEADS = 4
NA_KH_MAX = 8
NA_KW = 16

DIL_CONFIGS = ((128, 1), (512, 4), (2048, 16))
DIL_GROUPS = 3
DIL_HEADS = 4

T5_BUCKETS = 32
T5_MAX_DIST = 1024
T5_HEADS = SWA_HEADS + DIL_GROUPS * DIL_HEADS

N_BRANCH = 4
BRANCH_W = 256
D_FF = 4 * D_MODEL
PLE_DIM = 256

SPLIT_WIDTHS = (Q_LORA, KV_LORA, MLA_ROPE,
                SWA_HEADS * HEAD_DIM, SWA_KV_HEADS * HEAD_DIM, SWA_KV_HEADS * HEAD_DIM,
                3 * NA_HEADS * HEAD_DIM, 3 * DIL_GROUPS * DIL_HEADS * HEAD_DIM,
                N_BRANCH * D_MODEL)
IN_COLS = sum(SPLIT_WIDTHS)

kernel_name = 'hybrid_gated_parallel_encoder'


def layer_norm(x, g, b, eps=1e-5):
    xf = x.astype(jnp.float32)
    mu = xf.mean(-1, keepdims=True)
    var = jnp.square(xf - mu).mean(-1, keepdims=True)
    y = (xf - mu) * lax.rsqrt(var + eps) * g.astype(jnp.float32) + b.astype(jnp.float32)
    return y.astype(x.dtype)


def rms_norm(x, g, eps=1e-6):
    xf = x.astype(jnp.float32)
    y = xf * lax.rsqrt(jnp.mean(jnp.square(xf), -1, keepdims=True) + eps) * g.astype(jnp.float32)
    return y.astype(x.dtype)


def rope(x, positions):
    half = x.shape[-1] // 2
    inv = ROPE_THETA ** (-jnp.arange(half, dtype=jnp.float32) / half)
    ang = positions.astype(jnp.float32)[:, None] * inv[None, :]
    shape = (ang.shape[0],) + (1,) * (x.ndim - 3) + (half,)
    cos, sin = jnp.cos(ang).reshape(shape), jnp.sin(ang).reshape(shape)
    x1, x2 = x[..., :half].astype(jnp.float32), x[..., half:].astype(jnp.float32)
    return jnp.concatenate([x1 * cos - x2 * sin, x1 * sin + x2 * cos], -1).astype(x.dtype)


def t5_bucket(rel):
    nb = T5_BUCKETS // 2
    max_exact = nb // 2
    n = jnp.abs(rel)
    large = max_exact + (jnp.log(jnp.maximum(n, 1).astype(jnp.float32) / max_exact)
                         / math.log(T5_MAX_DIST / max_exact) * (nb - max_exact)).astype(jnp.int32)
    large = jnp.minimum(large, nb - 1)
    return jnp.where(rel > 0, nb, 0) + jnp.where(n < max_exact, n, large)


def band_offsets(half):
    return jnp.arange(3 * half)[None, :] - half - jnp.arange(half)[:, None]


def banded_attention(q, k, v, half, bias, sink=None):
    bsz, L, hk, g, dh = q.shape
    nb = -(-L // half)
    lp = nb * half
    q = jnp.pad(q, ((0, 0), (0, lp - L), (0, 0), (0, 0), (0, 0)))
    kv_pad = ((0, 0), (half, lp - L + half), (0, 0), (0, 0))

    def windows(t):
        tb = jnp.pad(t, kv_pad).reshape(bsz, nb + 2, half, hk, dh)
        return jnp.concatenate([tb[:, :-2], tb[:, 1:-1], tb[:, 2:]], axis=2)

    kw, vw = windows(k), windows(v)
    qb = q.reshape(bsz, nb, half, hk, g, dh)
    s = jnp.einsum('bnqhgd,bnkhd->bnhgqk', qb, kw).astype(jnp.float32) * dh ** -0.5
    s = s + bias.astype(jnp.float32)
    rel = band_offsets(half)
    kpos = jnp.arange(nb)[:, None] * half - half + jnp.arange(3 * half)[None, :]
    mask = (jnp.abs(rel) <= half)[None] & ((kpos >= 0) & (kpos < L))[:, None, :]
    s = jnp.where(mask[None, :, None, None], s, NEG_INF)
    m = s.max(-1)
    if sink is not None:
        sk = sink.astype(jnp.float32)[:, :, None]
        m = jnp.maximum(m, sk)
    e = jnp.exp(s - m[..., None])
    den = e.sum(-1)
    if sink is not None:
        den = den + jnp.exp(sk - m)
    pr = (e / den[..., None]).astype(v.dtype)
    o = jnp.einsum('bnhgqk,bnkhd->bnqhgd', pr, vw).reshape(bsz, lp, hk, g, dh)[:, :L]
    lse = (m + jnp.log(den)).transpose(0, 1, 4, 2, 3).reshape(bsz, lp, hk, g)[:, :L]
    return o, lse


def mla_attention(cq_in, ckv_in, kr_in, g_q, w_uq, g_kv, w_ukv, positions):
    bsz, S, _ = cq_in.shape
    q = (rms_norm(cq_in, g_q) @ w_uq).reshape(bsz, S, MLA_HEADS, MLA_NOPE + MLA_ROPE)
    q_nope, q_rope = q[..., :MLA_NOPE], rope(q[..., MLA_NOPE:], positions)
    kv = (rms_norm(ckv_in, g_kv) @ w_ukv).reshape(bsz, S, MLA_HEADS, MLA_NOPE + MLA_V)
    k_nope, v = kv[..., :MLA_NOPE], kv[..., MLA_NOPE:]
    k_rope = rope(kr_in, positions)
    scale = (MLA_NOPE + MLA_ROPE) ** -0.5
    nq = S // Q_BLOCK

    def to_blocks(t):
        return jnp.moveaxis(t.reshape(bsz, nq, Q_BLOCK, *t.shape[2:]), 1, 0)

    def block(args):
        qn, qr = args
        s = (jnp.einsum('bqhd,bkhd->bhqk', qn, k_nope)
             + jnp.einsum('bqhr,bkr->bhqk', qr, k_rope)).astype(jnp.float32) * scale
        pr = jax.nn.softmax(s, axis=-1).astype(v.dtype)
        return jnp.einsum('bhqk,bkhd->bqhd', pr, v)

    o = lax.map(block, (to_blocks(q_nope), to_blocks(q_rope)))
    return jnp.moveaxis(o, 0, 1).reshape(bsz, S, MLA_HEADS * MLA_V)


def neighborhood_attention(q, k, v, rpb):
    bsz, S, H, dh = q.shape
    rows = S // GRID_W
    kh = min(NA_KH_MAX, rows)
    q = q.reshape(bsz, rows, GRID_W, H, dh)
    k = k.reshape(bsz, rows, GRID_W, H, dh)
    v = v.reshape(bsz, rows, GRID_W, H, dh)
    r = jnp.arange(rows)
    row_idx = jnp.clip(r - kh // 2, 0, rows - kh)[:, None] + jnp.arange(kh)[None, :]
    kr = jnp.take(k, row_idx, axis=1)
    vr = jnp.take(v, row_idx, axis=1)
    c = jnp.arange(GRID_W)
    col_start = jnp.clip(c - NA_KW // 2, 0, GRID_W - NA_KW)
    col_ok = (c[None, :] >= col_start[:, None]) & (c[None, :] < col_start[:, None] + NA_KW)
    dr = row_idx - r[:, None] + (NA_KH_MAX - 1)
    dc = jnp.clip(c[None, :] - c[:, None], -(NA_KW - 1), NA_KW - 1) + (NA_KW - 1)
    bias = rpb[:, dr[:, None, :, None], dc[None, :, None, :]]
    s = jnp.einsum('brchd,brawhd->brhcaw', q, kr).astype(jnp.float32) * dh ** -0.5
    s = s + bias.astype(jnp.float32).transpose(1, 0, 2, 3, 4)[None]
    s = jnp.where(col_ok[:, None, :], s, NEG_INF)
    pr = jax.nn.softmax(s.reshape(bsz, rows, H, GRID_W, kh * GRID_W), axis=-1)
    pr = pr.reshape(s.shape).astype(v.dtype)
    o = jnp.einsum('brhcaw,brawhd->brchd', pr, vr)
    return o.reshape(bsz, S, H * dh)


def dilated_attention(q, k, v, rel_bias):
    bsz, S, _, H, dh = q.shape
    outs, lses = [], []
    for g, (window, r) in enumerate(DIL_CONFIGS):
        half = window // (2 * r)
        L = S // r

        def fold(t):
            return t.reshape(bsz, L, r, *t.shape[2:]).swapaxes(1, 2).reshape(bsz * r, L, *t.shape[2:])

        def unfold(t):
            return t.reshape(bsz, r, L, *t.shape[2:]).swapaxes(1, 2).reshape(bsz, S, *t.shape[2:])

        lo = SWA_HEADS + g * DIL_HEADS
        bias = rel_bias[t5_bucket(band_offsets(half) * r)][..., lo:lo + DIL_HEADS]
        bias = bias.transpose(2, 0, 1)[:, None]
        o, lse = banded_attention(fold(q[:, :, g])[:, :, :, None], fold(k[:, :, g]),
                                  fold(v[:, :, g]), half, bias)
        outs.append(unfold(o[:, :, :, 0]))
        lses.append(unfold(lse[:, :, :, 0]))
    w = jax.nn.softmax(jnp.stack(lses), axis=0)
    o = jnp.einsum('gbsh,gbshd->bshd', w, jnp.stack(outs).astype(jnp.float32))
    return o.reshape(bsz, S, H * dh).astype(q.dtype)


def setup_inputs(seed: int = 0) -> dict:
    key = jax.random.key(seed)
    ks = jax.random.split(key, 22)
    beta = (8 * DEPTH) ** -0.25

    def nrm(k, shape, scale):
        return jax.random.normal(k, shape, jnp.float32) * scale

    L = DEPTH
    return {
        'x': nrm(ks[0], (BATCH, SEQ, D_MODEL), 1.0),
        'p': nrm(ks[1], (DEPTH, BATCH, SEQ, PLE_DIM), 1.0),
        'ln_emb_g': 1.0 + nrm(ks[2], (D_MODEL,), 0.02),
        'ln_emb_b': nrm(ks[3], (D_MODEL,), 0.02),
        'rel_bias': nrm(ks[4], (T5_BUCKETS, T5_HEADS), 0.2),
        'w_in': nrm(ks[5], (L, D_MODEL, IN_COLS), D_MODEL ** -0.5),
        'mla_q_norm': 1.0 + nrm(ks[6], (L, Q_LORA), 0.02),
        'mla_w_uq': nrm(ks[7], (L, Q_LORA, MLA_HEADS * (MLA_NOPE + MLA_ROPE)), Q_LORA ** -0.5),
        'mla_kv_norm': 1.0 + nrm(ks[8], (L, KV_LORA), 0.02),
        'mla_w_ukv': nrm(ks[9], (L, KV_LORA, MLA_HEADS * (MLA_NOPE + MLA_V)), KV_LORA ** -0.5),
        'swa_sink': nrm(ks[10], (L, SWA_HEADS), 1.0),
        'na_rpb': nrm(ks[11], (L, NA_HEADS, 2 * NA_KH_MAX - 1, 2 * NA_KW - 1), 0.2),
        'w_branch': nrm(ks[12], (L, N_BRANCH, BRANCH_W, D_MODEL), BRANCH_W ** -0.5),
        'w_out': nrm(ks[13], (L, D_MODEL, D_MODEL), beta * D_MODEL ** -0.5),
        'ln1_g': 1.0 + nrm(ks[14], (L, D_MODEL), 0.02),
        'ln1_b': nrm(ks[15], (L, D_MODEL), 0.02),
        'w_ff1': nrm(ks[16], (L, D_MODEL, D_FF), beta * D_MODEL ** -0.5),
        'w_ff2': nrm(ks[17], (L, D_FF, D_MODEL), beta * D_FF ** -0.5),
        'w_ple': nrm(ks[18], (L, PLE_DIM, D_MODEL), beta * PLE_DIM ** -0.5),
        'w_ple_gate': nrm(ks[19], (L, D_MODEL, D_MODEL), D_MODEL ** -0.5),
        'ln2_g': 1.0 + nrm(ks[20], (L, D_MODEL), 0.02),
        'ln2_b': nrm(ks[21], (L, D_MODEL), 0.02),
    }


def reference(x, p, ln_emb_g, ln_emb_b, rel_bias, w_in, mla_q_norm, mla_w_uq, mla_kv_norm,
              mla_w_ukv, swa_sink, na_rpb, w_branch, w_out, ln1_g, ln1_b, w_ff1, w_ff2,
              w_ple, w_ple_gate, ln2_g, ln2_b):
    bsz, S, _ = x.shape
    positions = jnp.arange(S)
    alpha = (2 * DEPTH) ** 0.25
    offsets = [int(o) for o in np.cumsum(SPLIT_WIDTHS[:-1])]
    g_swa = SWA_HEADS // SWA_KV_HEADS
    bias_swa = rel_bias[t5_bucket(band_offsets(SWA_HALF))][..., :SWA_HEADS]
    bias_swa = bias_swa.transpose(2, 0, 1).reshape(SWA_KV_HEADS, g_swa, SWA_HALF, 3 * SWA_HALF)

    h = layer_norm(x, ln_emb_g, ln_emb_b)
    for i in range(DEPTH):
        z = h @ w_in[i]
        a_q, a_kv, a_kr, b_q, b_k, b_v, c_qkv, d_qkv, gates = jnp.split(z, offsets, axis=-1)
        y_a = mla_attention(a_q, a_kv, a_kr, mla_q_norm[i], mla_w_uq[i], mla_kv_norm[i],
                            mla_w_ukv[i], positions)
        o_b, _ = banded_attention(b_q.reshape(bsz, S, SWA_KV_HEADS, g_swa, HEAD_DIM),
                                  b_k.reshape(bsz, S, SWA_KV_HEADS, HEAD_DIM),
                                  b_v.reshape(bsz, S, SWA_KV_HEADS, HEAD_DIM),
                                  SWA_HALF, bias_swa, swa_sink[i].reshape(SWA_KV_HEADS, g_swa))
        y_b = o_b.reshape(bsz, S, SWA_HEADS * HEAD_DIM)
        c = c_qkv.reshape(bsz, S, 3, NA_HEADS, HEAD_DIM)
        y_c = neighborhood_attention(c[:, :, 0], c[:, :, 1], c[:, :, 2], na_rpb[i])
        d = d_qkv.reshape(bsz, S, 3, DIL_GROUPS, DIL_HEADS, HEAD_DIM)
        y_d = dilated_attention(d[:, :, 0], d[:, :, 1], d[:, :, 2], rel_bias)
        gate = jax.nn.sigmoid(gates.reshape(bsz, S, N_BRANCH, D_MODEL))
        branches = (y_a, y_b, y_c, y_d)
        merged = gate[:, :, 0] * (branches[0] @ w_branch[i, 0])
        for n in range(1, N_BRANCH):
            merged = merged + gate[:, :, n] * (branches[n] @ w_branch[i, n])
        h = layer_norm(alpha * h + merged @ w_out[i], ln1_g[i], ln1_b[i])
        ff = jnp.square(jax.nn.relu(h @ w_ff1[i])) @ w_ff2[i]
        ple = (p[i] @ w_ple[i]) * jax.nn.sigmoid(h @ w_ple_gate[i])
        h = layer_norm(alpha * h + ff + ple, ln2_g[i], ln2_b[i])
    return h
```

```python
import math
from contextlib import ExitStack

import numpy as np
import concourse.bass as bass
import concourse.mybir as mybir
from concourse.bass_utils import run_bass_kernel_spmd

F32 = mybir.dt.float32
BF16 = mybir.dt.bfloat16
AF = mybir.ActivationFunctionType
ALU = mybir.AluOpType

D = 1024
T = 4096
SEQ = 8192
NL = 2
ALPHA = (2 * NL) ** 0.25
IN_COLS = 8096
NDMA_SEM = 8


class Buf:
    def __init__(self, name="", multi=False):
        self.name = name
        self.multi = multi
        self.writers = {}
        self.readers = {}


class Op:
    __slots__ = ("eng", "emit", "deps", "needed", "sem", "val", "is_dma", "key", "idx")


class Sched:
    ENGS = ("pe", "act", "dve", "pool", "sp")

    def __init__(self, nc, es):
        self.nc = nc
        self.sem = {e: nc.alloc_semaphore(name="s_" + e) for e in self.ENGS}
        self.dsem = {e: [nc.alloc_semaphore(name="d_%s%d" % (e, i)) for i in range(NDMA_SEM)]
                     for e in ("sp", "pool", "act")}
        self.all_sems = list(self.sem.values()) + [x for v in self.dsem.values() for x in v]
        self.cnt = {e: 0 for e in self.ENGS}
        self.dcnt = {e: 0 for e in self.dsem}
        self.dlast = {e: [None] * NDMA_SEM for e in self.dsem}
        self.waited = {e: {} for e in self.ENGS}
        self.ops = {e: [] for e in self.ENGS}
        self.same_engine_sync = True
        self.nops = 0

    def _mk(self, eng, emit, reads, writes, is_dma):
        op = Op()
        op.eng, op.emit, op.is_dma, op.needed = eng, emit, is_dma, False
        op.sem = op.val = None
        op.idx = self.nops
        self.nops += 1
        deps = {}

        def add(o):
            if o is None:
                return
            if (not o.is_dma) and o.eng == eng:
                if eng == "pe" or not self.same_engine_sync:
                    return
            deps[id(o)] = o

        for b in reads:
            for o in b.writers.values():
                add(o)
        for b in writes:
            for o in b.readers.values():
                add(o)
            for o in b.writers.values():
                add(o)
        if is_dma:
            k = self.dcnt[eng]
            slot = k % NDMA_SEM
            self.dcnt[eng] = k + 1
            op.sem = self.dsem[eng][slot]
            op.val = 16 * (k // NDMA_SEM + 1)
            op.key = ("d", eng, slot)
            add(self.dlast[eng][slot])
            self.dlast[eng][slot] = op
            op.needed = True
        else:
            op.key = ("e", eng)
        op.deps = list(deps.values())
        for o in op.deps:
            o.needed = True
        for b in reads:
            b.readers[op.key] = op
        for b in writes:
            if b.multi:
                b.writers[op.key] = op
            else:
                b.writers = {op.key: op}
                b.readers = {}
        self.ops[eng].append(op)
        return op

    def op(self, eng, emit, reads=(), writes=()):
        return self._mk(eng, emit, reads, writes, False)

    def dma(self, eng, out, in_, reads=(), writes=()):
        return self._mk(eng, lambda e: e.dma_start(out=out, in_=in_), reads, writes, True)

    def collective(self, emit, sem, val, reads=(), writes=()):
        op = self._mk("pool", emit, reads, writes, False)
        op.is_dma = True
        op.needed = True
        op.sem, op.val = sem, val
        op.key = ("cc", id(sem))
        for b in reads:
            b.readers.pop(("e", "pool"), None)
            b.readers[op.key] = op
        for b in writes:
            b.writers = {op.key: op}
        return op

    def emit(self):
        nc = self.nc
        for e in self.ENGS:
            for op in self.ops[e]:
                if not op.is_dma and op.needed and op.sem is None:
                    self.cnt[e] += 1
                    op.sem = self.sem[e]
                    op.val = self.cnt[e]
        with nc.Block() as block:
            hooks = {"pe": block.tensor, "act": block.scalar, "dve": block.vector,
                     "pool": block.gpsimd, "sp": block.sync}
            for e in self.ENGS:
                ops = self.ops[e]
                if not ops:
                    continue
                waited = self.waited[e]

                def body(eng, ops=ops, waited=waited):
                    for op in ops:
                        for d in op.deps:
                            if d.val is None:
                                continue
                            k = id(d.sem)
                            if waited.get(k, 0) < d.val:
                                eng.wait_ge(d.sem, d.val)
                                waited[k] = d.val
                        inst = op.emit(eng)
                        if op.key[0] == "cc":
                            inst.then_inc(op.sem)
                        elif op.is_dma:
                            inst.then_inc(op.sem, 16)
                        elif op.needed:
                            inst.then_inc(op.sem, 1)

                hooks[e](body)
        self.ops = {e: [] for e in self.ENGS}

    def final_wait(self, eng, bufs):
        self._mk(eng, lambda e: e.nop(), bufs, (), False)


def _bcast_rows(ap1d, n=128):
    return ap1d.partition_broadcast(n)


class K:
    def __init__(self, debug=()):
        self.debug = set(debug)
        self.nc = bass.Bass("TRN2", target_bir_lowering=False)
        self.es = ExitStack()
        self.cc_sems = [self.nc.alloc_semaphore(name="cc%d" % i) for i in range(NL)]
        self.cm = self.nc.cleanup_on_exit()
        self.cm.__enter__()
        self.S = Sched(self.nc, self.es)
        self.dram = {}
        self.bufs = {}

    def sbt(self, name, shape, dt):
        self.uid = getattr(self, "uid", 0) + 1
        return self.nc.sbuf_tensor("%s_%d" % (name, self.uid), shape, dt)

    def pst(self, name, shape, dt):
        self.uid = getattr(self, "uid", 0) + 1
        return self.nc.psum_tensor("%s_%d" % (name, self.uid), shape, dt)

    def ext_in(self, name, shape, dt=F32):
        t = self.nc.dram_tensor(name, list(shape), dt, kind="ExternalInput")
        self.dram[name] = t
        self.bufs[name] = Buf(name, multi=True)
        return t

    def ext_out(self, name, shape, dt=F32):
        t = self.nc.dram_tensor(name, list(shape), dt, kind="ExternalOutput")
        self.dram[name] = t
        self.bufs[name] = Buf(name, multi=True)
        return t

    def scratch(self, name, shape, dt):
        t = self.nc.dram_tensor(name, list(shape), dt)
        self.shapes = getattr(self, "shapes", {})
        self.shapes[name] = (list(shape), dt)
        self.dram[name] = t
        self.bufs[name] = Buf(name, multi=True)
        return t

    def B(self, name):
        return self.bufs[name]

    def ln_ops(self, tag, src, src_b, dst, dst_b, g_bc, b_bc, tmp, eps=1e-5):
        S = self.S
        st, mv, sd = tmp["st"], tmp["mv"], tmp["sd"]
        bst = tmp["b"]
        def f_stats(e):
            e.bn_stats(out=st[:, 0, :], in_=src[:, 0:512])
            return e.bn_stats(out=st[:, 1, :], in_=src[:, 512:1024])
        S.op("dve", f_stats, reads=[src_b], writes=[tmp["b0"]])
        S.op("dve", lambda e: e.bn_aggr(out=mv[:, :], in_=st[:, :, :]), reads=[tmp["b0"]], writes=[bst])
        S.op("act", lambda e: e.activation(out=sd[:, 0:1], in_=mv[:, 1:2], func=AF.Sqrt, bias=tmp["eps"][:, 0:1], scale=1.0),
             reads=[bst], writes=[tmp["b2"]])
        S.op("dve", lambda e: e.reciprocal(out=sd[:, 1:2], in_=sd[:, 0:1]), reads=[tmp["b2"]], writes=[tmp["b3"]])
        S.op("dve", lambda e: e.tensor_scalar(out=dst[:, :], in0=src[:, :], scalar1=mv[:, 0:1], scalar2=sd[:, 1:2],
                                              op0=ALU.subtract, op1=ALU.mult),
             reads=[src_b, bst, tmp["b3"]], writes=[dst_b])
        S.op("pool", lambda e: e.tensor_tensor(out=dst[:, :], in0=dst[:, :], in1=g_bc[:, :], op=ALU.mult),
             reads=[dst_b, tmp["gb"]], writes=[dst_b])
        S.op("dve", lambda e: e.tensor_tensor(out=dst[:, :], in0=dst[:, :], in1=b_bc[:, :], op=ALU.add),
             reads=[dst_b, tmp["gb"]], writes=[dst_b])

    def transpose_ops(self, src_bf, src_b, psT, psT_b, dstT, dstT_b, col0, ident, ident_b, nk=8):
        S = self.S
        def f_tr(e):
            for kc in range(nk):
                i = e.transpose(out=psT[:, kc, :], in_=src_bf[:, kc * 128:(kc + 1) * 128], identity=ident[:, :])
            return i
        S.op("pe", f_tr, reads=[src_b, ident_b], writes=[psT_b])
        S.op("dve", lambda e: e.tensor_copy(out=dstT[:, 0:nk, col0:col0 + 128], in_=psT[:, 0:nk, :]),
             reads=[psT_b], writes=[dstT_b])

    def ln_consts(self, es, g_ap, b_ap, tagp):
        nc, S = self.nc, self.S
        sb = lambda n, sh, dt: es.enter_context(self.sbt(tagp + n, sh, dt))
        c = {}
        c["g"] = sb("g_bc", [128, D], F32); c["bb"] = sb("b_bc", [128, D], F32)
        c["eps"] = sb("eps", [128, 1], F32)
        c["ident"] = sb("identb", [128, 128], BF16); c["identf"] = sb("identf", [128, 128], F32)
        c["gb"] = Buf("gb"); c["ident_b"] = Buf("ident")
        S.dma("sp", c["g"][:, :], g_ap.partition_broadcast(128), writes=[c["gb"]])
        S.dma("sp", c["bb"][:, :], b_ap.partition_broadcast(128), writes=[c["gb"]])
        S.dma("sp", c["identf"][:, :], self.ident_f[:, :], writes=[c["ident_b"]])
        S.op("dve", lambda e: e.tensor_copy(out=c["ident"][:, :], in_=c["identf"][:, :]), reads=[c["ident_b"]], writes=[c["ident_b"]])
        S.op("dve", lambda e: e.memset(c["eps"][:, :], 1e-5), writes=[c["gb"]])
        c["tmp"] = []
        for i in range(2):
            t = {"st": sb("st%d" % i, [128, 2, 6], F32), "mv": sb("mv%d" % i, [128, 2], F32),
                 "sd": sb("sd%d" % i, [128, 2], F32), "b": Buf(), "b0": Buf(), "b2": Buf(), "b3": Buf(),
                 "gb": c["gb"], "eps": c["eps"]}
            c["tmp"].append(t)
        return c

    def phase_ln_emb(self):
        nc, S = self.nc, self.S
        self.hTv = self.hT.ap().rearrange("(kc p) t -> p kc t", p=128)
        with ExitStack() as es:
            sb = lambda n, sh, dt: es.enter_context(self.sbt("le_" + n, sh, dt))
            c = self.ln_consts(es, self.ln_emb_g.ap(), self.ln_emb_b.ap(), "le_")
            xt = [sb("xt%d" % i, [128, D], F32) for i in range(2)]; xt_b = [Buf() for _ in range(2)]
            yt = [sb("yt%d" % i, [128, D], F32) for i in range(2)]; yt_b = [Buf() for _ in range(2)]
            yb = [sb("yb%d" % i, [128, D], BF16) for i in range(2)]; yb_b = [Buf() for _ in range(2)]
            hts = [sb("hts%d" % i, [128, 8, 512], BF16) for i in range(2)]; hts_b = [Buf() for _ in range(2)]
            psT = [es.enter_context(self.pst("le_psT%d" % i, [128, 8, 128], BF16)) for i in range(2)]
            psT_b = [Buf() for _ in range(2)]
            hTv = self.hT.ap().rearrange("(kc p) t -> p kc t", p=128)
            for tt in range(T // 128):
                i = tt % 2
                S.dma("sp", xt[i][:, :], self.x[tt * 128:(tt + 1) * 128, :], reads=[self.B("x")], writes=[xt_b[i]])
                self.ln_ops("le", xt[i], xt_b[i], yt[i], yt_b[i], c["g"], c["bb"], c["tmp"][i])
                S.dma("pool", self.hA[tt * 128:(tt + 1) * 128, :], yt[i][:, :], reads=[yt_b[i]], writes=[self.B("hA")])
                S.op("act", lambda e, i=i: e.activation(out=yb[i][:, :], in_=yt[i][:, :], func=AF.Copy),
                     reads=[yt_b[i]], writes=[yb_b[i]])
                j = (tt // 4) % 2
                self.transpose_ops(yb[i], yb_b[i], psT[i], psT_b[i], hts[j], hts_b[j], (tt % 4) * 128, c["ident"], c["ident_b"])
                if tt % 4 == 3:
                    c0 = (tt // 4) * 512
                    S.dma("pool", hTv[:, :, c0:c0 + 512], hts[j][:, :, :], reads=[hts_b[j]], writes=[self.B("hT")])
            S.emit()

    def dump(self, names):
        outs = []
        for n in names:
            shape, dt = self.shapes[n]
            o = self.ext_out("dbg_" + n, shape, dt)
            src, dst = self.dram[n].ap(), o.ap()
            R = shape[0]
            step = max(1, R // 4)
            for r0 in range(0, R, step):
                self.S.dma("sp", dst[r0:min(R, r0 + step)], src[r0:min(R, r0 + step)], reads=[self.B(n)], writes=[self.B("dbg_" + n)])
            outs.append(self.B("dbg_" + n))
        return outs

    def finish(self, out_bufs):
        self.S.final_wait("sp", out_bufs)
        self.S.final_wait("pool", out_bufs)
        self.S.emit()
        self.es.close()
        self.cm.__exit__(None, None, None)


    LAYER_W = (("w_in", [D, IN_COLS]), ("mla_q_norm", [256]), ("mla_w_uq", [256, 384]), ("mla_kv_norm", [128]),
               ("mla_w_ukv", [128, 512]), ("swa_sink", [4]), ("w_branch", [4 * 256, D]), ("w_out", [D, D]),
               ("ln1_g", [D]), ("ln1_b", [D]), ("w_ff1", [D, 4 * D]), ("w_ff2", [4 * D, D]), ("w_ple", [256, D]),
               ("w_ple_gate", [D, D]), ("ln2_g", [D]), ("ln2_b", [D]), ("p", [T, 256]), ("tabC", [4 * 33, 128, 128]))

    def declare_F(self):
        e = self.ext_in
        self.x = e("x", [T, D]); self.ident_f = e("ident", [128, 128])
        self.ln_emb_g = e("ln_emb_g", [D]); self.ln_emb_b = e("ln_emb_b", [D])
        self.ropeq = e("ropeq", [2, 32, T]); self.ropek = e("ropek", [2, 32, SEQ])
        self.tabB = e("tabB", [4 * 9, 128, 128])
        self.tabD = [e("tabD%d" % g, [4 * 6, 128, 128]) for g in range(3)]
        self.LW = [{n: e("%s_%d" % (n, l), sh) for n, sh in self.LAYER_W} for l in range(NL)]
        self.out = self.ext_out("out", [T, D])
        s = self.scratch
        self.hA_l = [s("hA%d" % l, [T, D], F32) for l in range(NL)]
        self.hT_l = [s("hT%d" % l, [D, T], BF16) for l in range(NL)]
        self.G_l = [s("G%d" % l, [8, 256, T], BF16) for l in range(NL)]
        self.hTd = s("hTd", [D, T], BF16)
        self.h1T = s("h1T", [D, T], BF16); self.r2 = s("r2", [T, D], F32)
        self.yT = [s("yT%d" % n, [256, T], BF16) for n in range(4)]
        self.Og = {n: s("Og" + n, [T, 260], F32) for n in ("A", "B", "C", "D0", "D1", "D2")}
        self.vS = s("vS", [T + 2048, 260], BF16)
        self.hA, self.hT = self.hA_l[0], self.hT_l[0]
        self.bufs["hA"] = self.bufs["hA0"]; self.bufs["hT"] = self.bufs["hT0"]
        self.init_sems()

    def set_layer(self, l):
        W = self.LW[l]
        for n, _ in self.LAYER_W:
            setattr(self, n, W[n])
            self.bufs[n] = self.bufs["%s_%d" % (n, l)]
        self.hA, self.hTo, self.G = self.hA_l[l], self.hT_l[l], self.G_l[l]
        self.bufs["hA"] = self.bufs["hA%d" % l]; self.bufs["hTo"] = self.bufs["hT%d" % l]; self.bufs["G"] = self.bufs["G%d" % l]
        if l < NL - 1:
            self.hA_o, self.hT_o = self.hA_l[l + 1], self.hT_l[l + 1]
            self.bufs["hA_out"] = self.bufs["hA%d" % (l + 1)]; self.bufs["hT_out"] = self.bufs["hT%d" % (l + 1)]
        else:
            self.hA_o, self.hT_o = self.out, self.hTd
            self.bufs["hA_out"] = self.bufs["out"]; self.bufs["hT_out"] = self.bufs["hTd"]
        self.hTo_v = self.hTo.ap().rearrange("(kc p) t -> p kc t", p=128)
        self.G_v = [self.G[:, r * 128:(r + 1) * 128, :].rearrange("kc p t -> p kc t") for r in range(2)]

    def allgather(self, l):
        S = self.S
        hT, G = self.hT_l[l], self.G_l[l]
        for j in range(8):
            S.collective(lambda e, j=j: e.collective_compute("AllGather", ALU.bypass,
                                                             replica_groups=[[0, 1], [2, 3], [4, 5], [6, 7]],
                                                             ins=[hT[j * 128:(j + 1) * 128, :].opt()], outs=[G[j].opt()]),
                         self.cc_sems[l], j + 1, reads=[self.B("hT%d" % l)], writes=[self.B("G%d" % l)])
        S.op("pool", lambda e: e.nop(), reads=[self.B("G%d" % l)])
        S.emit()

    def init_sems(self):
        with self.nc.Block() as block:
            @block.gpsimd
            def _(g):
                for sm in self.S.all_sems:
                    g.sem_clear(sm)

    def declare_A(self):
        e = self.ext_in
        self.x = e("x", [T, D]); self.ident_f = e("ident", [128, 128])
        self.ln_emb_g = e("ln_emb_g", [D]); self.ln_emb_b = e("ln_emb_b", [D])
        self.hA = self.ext_out("hA_out", [T, D]); self.hT = self.ext_out("hT_out", [D, T], BF16)
        self.bufs["hA"] = self.bufs["hA_out"]; self.bufs["hT"] = self.bufs["hT_out"]
        self.init_sems()

    def declare_B(self):
        e = self.ext_in
        self.hA = e("hA", [T, D]); self.hTo = e("hTo", [D, T], BF16); self.G = e("G", [2 * D, T], BF16)
        self.p = e("p", [T, 256]); self.ident_f = e("ident", [128, 128])
        self.w_in = e("w_in", [D, IN_COLS])
        self.mla_q_norm = e("mla_q_norm", [256]); self.mla_w_uq = e("mla_w_uq", [256, 384])
        self.mla_kv_norm = e("mla_kv_norm", [128]); self.mla_w_ukv = e("mla_w_ukv", [128, 512])
        self.swa_sink = e("swa_sink", [4])
        self.w_branch = e("w_branch", [4 * 256, D]); self.w_out = e("w_out", [D, D])
        self.ln1_g = e("ln1_g", [D]); self.ln1_b = e("ln1_b", [D])
        self.w_ff1 = e("w_ff1", [D, 4 * D]); self.w_ff2 = e("w_ff2", [4 * D, D])
        self.w_ple = e("w_ple", [256, D]); self.w_ple_gate = e("w_ple_gate", [D, D])
        self.ln2_g = e("ln2_g", [D]); self.ln2_b = e("ln2_b", [D])
        self.ropeq = e("ropeq", [2, 32, T]); self.ropek = e("ropek", [2, 32, SEQ])
        self.tabB = e("tabB", [4 * 9, 128, 128]); self.tabC = e("tabC", [4 * 33, 128, 128])
        self.tabD = [e("tabD%d" % g, [4 * 6, 128, 128]) for g in range(3)]
        self.hA_o = self.ext_out("hA_out", [T, D]); self.hT_o = self.ext_out("hT_out", [D, T], BF16)
        s = self.scratch
        self.h1T = s("h1T", [D, T], BF16); self.r2 = s("r2", [T, D], F32)
        self.yT = [s("yT%d" % n, [256, T], BF16) for n in range(4)]
        self.Og = {n: s("Og" + n, [T, 260], F32) for n in ("A", "B", "C", "D0", "D1", "D2")}
        self.vS = s("vS", [T + 2048, 260], BF16)
        self.hTo_v = self.hTo.ap().rearrange("(kc p) t -> p kc t", p=128)
        self.G_v = [self.G[r * D:(r + 1) * D, :].rearrange("(kc p) t -> p kc t", p=128) for r in range(2)]
        self.init_sems()

    def wload(self, dst, src, c0=0, eng="pool", wb=None):
        n = src.shape[1]
        kc = src.shape[0] // 128
        v = src.rearrange("(kc p) n -> p kc n", p=128)
        for a in range(0, n, 2048):
            b = min(n, a + 2048)
            self.S.dma(eng, dst[:, 0:kc, c0 + a:c0 + b], v[:, :, a:b], writes=[wb])

    def hT_src(self, e0, n, H):
        t0 = e0 - H
        if t0 < 0:
            return self.G_v[0][:, :, T + t0:T + t0 + n], self.B("G")
        if t0 >= T:
            return self.G_v[1][:, :, t0 - T:t0 - T + n], self.B("G")
        return self.hTo_v[:, :, t0:t0 + n], self.B("hTo")

    def phase_banded(self, tag, qcols, kcols, vcols, nkv, r, halo_f, blocks_fn, nslots, tab, og, dup_k):
        nc, S = self.nc, self.S
        H = r * halo_f
        E = T + 2 * H
        L = T // r
        nb = L // 128
        nkb = (L + 2 * halo_f) // 128
        with ExitStack() as es:
            sb = lambda n, sh, dt: es.enter_context(self.sbt(tag + n, sh, dt))
            wq = sb("wq", [128, 8, 256], BF16); wk = sb("wk", [128, 8, 256], BF16); wv = sb("wv", [128, 8, 256], BF16)
            w_b = Buf()
            self.wload(wq, self.w_in[:, qcols:qcols + 256], wb=w_b)
            if dup_k:
                for j in range(2):
                    for d in range(2):
                        self.wload(wk, self.w_in[:, kcols + j * 64:kcols + j * 64 + 64], c0=j * 128 + d * 64, wb=w_b)
            else:
                self.wload(wk, self.w_in[:, kcols:kcols + 256], wb=w_b)
            self.wload(wv, self.w_in[:, vcols:vcols + nkv * 64], wb=w_b)
            qT = [sb("qT%d" % m, [128, T], BF16) for m in range(2)]
            kT = [sb("kT%d" % m, [128, E], BF16) for m in range(2)]
            qk_b = Buf()
            tabs = sb("tab", [128, 4 * nslots, 128], F32); tab_b = Buf()
            for h in range(4):
                S.dma("sp", tabs[:, h * nslots:(h + 1) * nslots, :],
                      tab[h * nslots:(h + 1) * nslots, :, :].rearrange("s k q -> k s q"), writes=[tab_b])
            hc = [sb("hc%d" % i, [128, 8, 512], BF16) for i in range(2)]; hc_b = [Buf() for _ in range(2)]
            vst = [sb("vst%d" % i, [128, nkv, 65], BF16) for i in range(2)]; vst_b = [Buf() for _ in range(2)]
            for i in range(2):
                S.op("pool", lambda e, i=i: e.memset(vst[i][:, :, :], 1.0), writes=[vst_b[i]])
            es2 = ExitStack()
            pp = [es2.enter_context(self.pst(tag + "pp%d" % i, [128, 512], F32)) for i in range(2)]
            pp_b = [Buf() for _ in range(2)]
            ppi = [0]

            def proj_fm(wt, c0, hci, n, dst, d0):
                i = ppi[0] % 2; ppi[0] += 1
                def f(e):
                    for kc in range(8):
                        ins = e.matmul(pp[i][:, 0:n], lhsT=wt[:, kc, c0:c0 + 128], rhs=hc[hci][:, kc, 0:n],
                                       start=(kc == 0), stop=(kc == 7))
                    return ins
                S.op("pe", f, reads=[w_b, hc_b[hci]], writes=[pp_b[i]])
                S.op("act", lambda e: e.activation(out=dst[:, d0:d0 + n], in_=pp[i][:, 0:n], func=AF.Copy),
                     reads=[pp_b[i]], writes=[qk_b])

            chunks = []
            e0 = 0
            while e0 < E:
                t0 = e0 - H
                if t0 < 0:
                    n = min(512, -t0)
                elif t0 >= T:
                    n = min(512, E - e0)
                else:
                    n = min(512, T - t0)
                chunks.append((e0, n)); e0 += n
            vcnt = 0
            for ci, (e0, n) in enumerate(chunks):
                hci = ci % 2
                src, src_b = self.hT_src(e0, n, H)
                S.dma("sp", hc[hci][:, :, 0:n], src, reads=[src_b], writes=[hc_b[hci]])
                own = 0 <= e0 - H < T
                for m in range(2):
                    proj_fm(wk, m * 128, hci, n, kT[m], e0)
                    if own:
                        proj_fm(wq, m * 128, hci, n, qT[m], e0 - H)
                for s0 in range(0, n, 128):
                    ns = min(128, n - s0)
                    i = ppi[0] % 2; ppi[0] += 1
                    vi = vcnt % 2; vcnt += 1
                    def f(e, i=i, s0=s0, ns=ns, hci=hci):
                        for kc in range(8):
                            ins = e.matmul(pp[i][0:ns, 0:nkv * 64], lhsT=hc[hci][:, kc, s0:s0 + ns],
                                           rhs=wv[:, kc, 0:nkv * 64], start=(kc == 0), stop=(kc == 7))
                        return ins
                    S.op("pe", f, reads=[w_b, hc_b[hci]], writes=[pp_b[i]])
                    S.op("dve", lambda e, i=i, vi=vi, ns=ns: e.tensor_copy(
                        out=vst[vi][0:ns, :, 0:64], in_=pp[i][0:ns, 0:nkv * 64].rearrange("p (h d) -> p h d", d=64)),
                        reads=[pp_b[i]], writes=[vst_b[vi]])
                    r0 = e0 + s0
                    S.dma("pool", self.vS[r0:r0 + ns, 0:nkv * 65], vst[vi][0:ns, :, :].rearrange("p h d -> p (h d)"),
                          reads=[vst_b[vi]], writes=[self.B("vS")])
            vx = sb("vx", [128, r * nkb, nkv * 65], BF16); vx_b = Buf()
            for rho in range(r):
                for m in range(nkb):
                    row0 = H + rho + r * (128 * m - halo_f)
                    srcv = self.vS[row0:row0 + 127 * r + 1:r, 0:nkv * 65] if r > 1 else self.vS[row0:row0 + 128, 0:nkv * 65]
                    S.dma("sp", vx[:, rho * nkb + m, :], srcv, reads=[self.B("vS")], writes=[vx_b])
            S.emit()
            es2.close()
            nkmax = max(len(blocks_fn(b, nb)[0]) for b in range(nb))
            NSET = 2 if nkmax <= 4 else 1
            ps = [[es.enter_context(self.pst(tag + "ps%d_%d" % (a, j), [128, 512 if nkmax <= 4 else 1024], F32)) for j in range(2)] for a in range(NSET)]
            ps_b = [[Buf() for j in range(2)] for a in range(NSET)]
            pob = [[es.enter_context(self.pst(tag + "po%d_%d" % (a, j), [128, 512], F32)) for j in range(2)] for a in range(2)]
            po_b = [[Buf() for j in range(2)] for a in range(2)]
            sbt = [[sb("sbt%d_%d" % (a, j), [128, nkmax * 128], F32) for j in range(2)] for a in range(2)]
            sbt_b = [[Buf() for j in range(2)] for a in range(2)]
            et = [[sb("et%d_%d" % (a, j), [128, nkmax * 128], BF16) for j in range(2)] for a in range(2)]
            et_b = [[Buf() for j in range(2)] for a in range(2)]
            ost = [sb("ost%d" % i, [128, 4, 65], F32) for i in range(2)]
            ost_b = [[Buf() for h in range(4)] for i in range(2)]
            units = [(rho, b, hp) for rho in range(r) for b in range(nb) for hp in range(2)]

            def stage_f(u):
                rho, b, hp = units[u]
                a = u % NSET
                ms, slot0 = blocks_fn(b, nb)
                q0 = rho + r * 128 * b
                for j in range(2):
                    h = 2 * hp + j
                    mq = h // 2
                    pr = slice((h % 2) * 64, (h % 2) * 64 + 64)
                    qap = qT[mq][pr, q0:q0 + 127 * r + 1:r] if r > 1 else qT[mq][pr, q0:q0 + 128]
                    def f(e, a=a, j=j, ms=ms, mq=mq, pr=pr, qap=qap, rho=rho):
                        for x, m in enumerate(ms):
                            k0 = H + rho + r * (128 * m - halo_f)
                            kap = kT[mq][pr, k0:k0 + 127 * r + 1:r] if r > 1 else kT[mq][pr, k0:k0 + 128]
                            ins = e.matmul(ps[a][j][:, x * 128:(x + 1) * 128], lhsT=kap, rhs=qap, start=True, stop=True)
                        return ins
                    S.op("pe", f, reads=[qk_b], writes=[ps_b[a][j]])

            def stage_m(u):
                rho, b, hp = units[u]
                a = u % NSET
                a2 = u % 2
                ms, slot0 = blocks_fn(b, nb)
                nk = len(ms)
                oi = (u // 2) % 2
                q0 = rho + r * 128 * b
                for j in range(2):
                    h = 2 * hp + j
                    bias = tabs[:, h * nslots + slot0:h * nslots + slot0 + nk, :].rearrange("k s q -> k (s q)")
                    S.op("dve", lambda e, a=a, a2=a2, j=j, nk=nk, bias=bias: e.scalar_tensor_tensor(
                        out=sbt[a2][j][:, 0:nk * 128], in0=ps[a][j][:, 0:nk * 128], scalar=0.125, in1=bias,
                        op0=ALU.mult, op1=ALU.add), reads=[ps_b[a][j], tab_b], writes=[sbt_b[a2][j]])
                for j in range(2):
                    S.op("act", lambda e, a2=a2, j=j, nk=nk: e.activation(out=et[a2][j][:, 0:nk * 128], in_=sbt[a2][j][:, 0:nk * 128], func=AF.Exp),
                         reads=[sbt_b[a2][j]], writes=[et_b[a2][j]])
                for j in range(2):
                    h = 2 * hp + j
                    hv = (h // 2) if dup_k else h
                    def g(e, a2=a2, j=j, ms=ms, hv=hv, rho=rho, nk=nk):
                        for x, m in enumerate(ms):
                            ins = e.matmul(pob[a2][j][:, 0:65], lhsT=et[a2][j][:, x * 128:(x + 1) * 128],
                                           rhs=vx[:, rho * nkb + m, hv * 65:(hv + 1) * 65], start=(x == 0), stop=(x == nk - 1))
                        return ins
                    S.op("pe", g, reads=[et_b[a2][j], vx_b], writes=[po_b[a2][j]])
                for j in range(2):
                    h = 2 * hp + j
                    S.op("act", lambda e, a2=a2, j=j, h=h, oi=oi: e.activation(out=ost[oi][:, h, :], in_=pob[a2][j][:, 0:65], func=AF.Copy),
                         reads=[po_b[a2][j]], writes=[ost_b[oi][h]])
                if hp == 1:
                    dst = og[q0:q0 + 127 * r + 1:r, :] if r > 1 else og[q0:q0 + 128, :]
                    S.dma("pool", dst, ost[oi][:, :, :].rearrange("p h d -> p (h d)"), reads=ost_b[oi], writes=[self.B(og.name)])

            if NSET == 2:
                stage_f(0)
            for u in range(len(units)):
                if NSET == 2:
                    if u + 1 < len(units):
                        stage_f(u + 1)
                else:
                    stage_f(u)
                stage_m(u)
            S.emit()

    def phase_combine(self, tag, ogs, yT, sink=False):
        nc, S = self.nc, self.S
        ng = len(ogs)
        with ExitStack() as es:
            sb = lambda n, sh, dt: es.enter_context(self.sbt(tag + n, sh, dt))
            identf = sb("idf", [128, 128], F32); ident = sb("idb", [128, 128], BF16); id_b = Buf()
            S.dma("sp", identf[:, :], self.ident_f[:, :], writes=[id_b])
            S.op("dve", lambda e: e.tensor_copy(out=ident[:, :], in_=identf[:, :]), reads=[id_b], writes=[id_b])
            esk = sb("esk", [128, 4], F32); esk_b = Buf()
            if sink:
                S.dma("sp", esk[:, :], self.swa_sink.ap().partition_broadcast(128), writes=[esk_b])
                S.op("act", lambda e: e.activation(out=esk[:, :], in_=esk[:, :], func=AF.Exp), reads=[esk_b], writes=[esk_b])
            NT4 = 4
            ot = [[sb("ot%d_%d" % (i, g), [128, 4, 65], F32) for g in range(ng)] for i in range(NT4)]
            ot_b = [[Buf() for g in range(ng)] for i in range(NT4)]
            den = [sb("den%d" % i, [128, 4], F32) for i in range(NT4)]; den_b = [Buf() for _ in range(NT4)]
            y = [sb("y%d" % i, [128, 256], BF16) for i in range(NT4)]; y_b = [[Buf() for h in range(4)] for _ in range(NT4)]
            psT = [es.enter_context(self.pst(tag + "psT%d" % i, [128, 8, 128], BF16)) for i in range(NT4)]
            psT_b = [Buf() for _ in range(NT4)]
            yts = [sb("yts%d" % i, [128, 2, 512], BF16) for i in range(2)]; yts_b = [[Buf() for k in range(4)] for _ in range(2)]
            yTv = yT.ap().rearrange("(kc p) t -> p kc t", p=128)
            for grp in range(T // 512):
                j = grp % 2
                tts = [grp * 4 + k for k in range(4)]
                for k, tt in enumerate(tts):
                    for g in range(ng):
                        S.dma("sp", ot[k][g][:, :, :].rearrange("p h d -> p (h d)"), ogs[g][tt * 128:(tt + 1) * 128, :],
                              reads=[self.B(ogs[g].name)], writes=[ot_b[k][g]])
                for g in range(1, ng):
                    for k in range(4):
                        S.op("dve", lambda e, k=k, g=g: e.tensor_tensor(out=ot[k][0][:, :, :], in0=ot[k][0][:, :, :], in1=ot[k][g][:, :, :], op=ALU.add),
                             reads=[ot_b[k][0], ot_b[k][g]], writes=[ot_b[k][0]])
                if sink:
                    for k in range(4):
                        S.op("dve", lambda e, k=k: e.tensor_tensor(out=den[k][:, :], in0=ot[k][0][:, :, 64], in1=esk[:, :], op=ALU.add),
                             reads=[ot_b[k][0], esk_b], writes=[den_b[k]])
                    for k in range(4):
                        S.op("dve", lambda e, k=k: e.reciprocal(out=den[k][:, :], in_=den[k][:, :]), reads=[den_b[k]], writes=[den_b[k]])
                else:
                    for k in range(4):
                        S.op("dve", lambda e, k=k: e.reciprocal(out=den[k][:, :], in_=ot[k][0][:, :, 64]), reads=[ot_b[k][0]], writes=[den_b[k]])
                for h in range(4):
                    for k in range(4):
                        S.op("dve", lambda e, k=k, h=h: e.tensor_scalar(out=y[k][:, h * 64:(h + 1) * 64], in0=ot[k][0][:, h, 0:64],
                                                                       scalar1=den[k][:, h:h + 1], scalar2=None, op0=ALU.mult),
                             reads=[ot_b[k][0], den_b[k]], writes=[y_b[k][h]])
                for k in range(4):
                    def f_tr(e, k=k):
                        e.transpose(out=psT[k][:, 0, :], in_=y[k][:, 0:128], identity=ident[:, :])
                        return e.transpose(out=psT[k][:, 1, :], in_=y[k][:, 128:256], identity=ident[:, :])
                    S.op("pe", f_tr, reads=y_b[k] + [id_b], writes=[psT_b[k]])
                for k in range(4):
                    S.op("dve", lambda e, k=k, j=j: e.tensor_copy(out=yts[j][:, 0:2, k * 128:(k + 1) * 128], in_=psT[k][:, 0:2, :]),
                         reads=[psT_b[k]], writes=[yts_b[j][k]])
                c0 = grp * 512
                S.dma("pool", yTv[:, :, c0:c0 + 512], yts[j][:, :, :], reads=yts_b[j], writes=[self.B(yT.name)])
            S.emit()

    def phase_mla(self):
        nc, S = self.nc, self.S
        w_in = self.w_in
        with ExitStack() as es:
            sb = lambda n, sh, dt: es.enter_context(self.sbt("ml_" + n, sh, dt))
            wq = sb("wq", [128, 8, 256], BF16); wkv = sb("wkv", [128, 8, 128], BF16)
            wkr = sb("wkr", [128, 8, 96], BF16); wkrs = sb("wkrs", [128, 8, 96], BF16)
            w_b = Buf()
            self.wload(wq, w_in[:, 0:256], wb=w_b); self.wload(wkv, w_in[:, 256:384], wb=w_b)
            self.wload(wkr, w_in[:, 256:320], c0=0, wb=w_b); self.wload(wkr, w_in[:, 384:416], c0=64, wb=w_b)
            self.wload(wkrs, w_in[:, 256:320], c0=0, wb=w_b); self.wload(wkrs, w_in[:, 400:416], c0=64, wb=w_b)
            self.wload(wkrs, w_in[:, 384:400], c0=80, wb=w_b)
            uqf = sb("uqf", [128, 2, 384], F32); gq = sb("gq", [128, 2], F32)
            uq = sb("uq", [128, 2, 384], BF16); uqs = sb("uqs", [128, 2, 384], BF16)
            ukvf = sb("ukvf", [128, 512], F32); gkv = sb("gkv", [128, 1], F32)
            ukv = sb("ukv", [128, 512], BF16); wv4 = sb("wv4", [128, 256], BF16)
            u_b = Buf()
            S.dma("sp", uqf[:, :, :], self.mla_w_uq.ap().rearrange("(kc p) n -> p kc n", p=128), writes=[u_b])
            for kc in range(2):
                S.dma("sp", gq[:, kc:kc + 1], self.mla_q_norm[kc * 128:(kc + 1) * 128].rearrange("(p o) -> p o", o=1), writes=[u_b])
            S.dma("sp", ukvf[:, :], self.mla_w_ukv[:, :], writes=[u_b])
            S.dma("sp", gkv[:, :], self.mla_kv_norm.ap().rearrange("(p o) -> p o", o=1), writes=[u_b])
            for kc in range(2):
                S.op("dve", lambda e, kc=kc: e.tensor_scalar(out=uq[:, kc, :], in0=uqf[:, kc, :], scalar1=gq[:, kc:kc + 1],
                                                              scalar2=None, op0=ALU.mult), reads=[u_b], writes=[u_b])
            S.op("dve", lambda e: e.tensor_copy(out=uqs[:, :, :], in_=uq[:, :, :]), reads=[u_b], writes=[u_b])
            for h in range(4):
                S.op("dve", lambda e, h=h: e.tensor_copy(out=uqs[:, :, h * 96 + 64:h * 96 + 80], in_=uq[:, :, h * 96 + 80:h * 96 + 96]),
                     reads=[u_b], writes=[u_b])
                S.op("dve", lambda e, h=h: e.tensor_copy(out=uqs[:, :, h * 96 + 80:h * 96 + 96], in_=uq[:, :, h * 96 + 64:h * 96 + 80]),
                     reads=[u_b], writes=[u_b])
            S.op("dve", lambda e: e.tensor_scalar(out=ukv[:, :], in0=ukvf[:, :], scalar1=gkv[:, 0:1], scalar2=None, op0=ALU.mult),
                 reads=[u_b], writes=[u_b])
            for h in range(4):
                S.op("dve", lambda e, h=h: e.tensor_copy(out=wv4[:, h * 64:(h + 1) * 64], in_=ukv[:, h * 128 + 64:h * 128 + 128]),
                     reads=[u_b], writes=[u_b])
            identf = sb("idf", [128, 128], F32); ident = sb("idb", [128, 128], BF16); id_b = Buf()
            S.dma("sp", identf[:, :], self.ident_f[:, :], writes=[id_b])
            S.op("dve", lambda e: e.tensor_copy(out=ident[:, :], in_=identf[:, :]), reads=[id_b], writes=[id_b])
            eps6 = sb("eps6", [128, 1], F32); eps_b = Buf()
            S.op("dve", lambda e: e.memset(eps6[:, :], 1e-6), writes=[eps_b])
            kT = [sb("kT%d" % h, [96, SEQ], BF16) for h in range(4)]; kT_b = Buf()
            qT = [sb("qT%d" % h, [96, T], BF16) for h in range(4)]; qT_b = Buf()
            vx = sb("vx", [128, SEQ // 128, 260], BF16); vx_b = Buf()
            S.op("pool", lambda e: e.memset(vx[:, :, :], 1.0), writes=[vx_b])
            hc = [sb("hc%d" % i, [128, 8, 512], BF16) for i in range(2)]; hc_b = [Buf() for _ in range(2)]
            rt = [sb("rt%d" % i, [96, 2, 512], F32) for i in range(2)]; rt_b = [Buf() for _ in range(2)]
            junk = sb("junk", [128, 256], F32); junk_b = Buf()
            ss = [sb("ss%d" % i, [128, 3], F32) for i in range(2)]; ss_b = [Buf() for _ in range(2)]
            cn = [sb("cn%d" % i, [128, 256], BF16) for i in range(2)]; cn_b = [Buf() for _ in range(2)]
            cnT = [sb("cnT%d" % i, [128, 2, 512], BF16) for i in range(2)]; cnT_b = [Buf() for _ in range(2)]
            t1 = [sb("t1%d" % i, [96, 512], F32) for i in range(2)]; t1_b = [Buf() for _ in range(2)]
            t2 = [sb("t2%d" % i, [96, 512], F32) for i in range(2)]; t2_b = [Buf() for _ in range(2)]
            with ExitStack() as es2:
                NP = 6
                pp = [es2.enter_context(self.pst("ml_pp%d" % i, [128, 512], F32)) for i in range(NP)]
                pp_b = [Buf() for _ in range(NP)]
                psT = [es2.enter_context(self.pst("ml_psT%d" % i, [128, 2, 128], BF16)) for i in range(2)]
                psT_b = [Buf() for _ in range(2)]
                ctr = [0]
                def nextp():
                    i = ctr[0] % NP; ctr[0] += 1
                    return i

                def norm_tile(hci, s, wt, ncol, nfeat, si):
                    i = nextp()
                    def f(e):
                        for kc in range(8):
                            ins = e.matmul(pp[i][:, 0:ncol], lhsT=hc[hci][:, kc, s * 128:(s + 1) * 128], rhs=wt[:, kc, 0:ncol],
                                           start=(kc == 0), stop=(kc == 7))
                        return ins
                    S.op("pe", f, reads=[w_b, hc_b[hci]], writes=[pp_b[i]])
                    S.op("act", lambda e: e.activation(out=junk[:, 0:ncol], in_=pp[i][:, 0:ncol], func=AF.Square, accum_out=ss[si][:, 0:1]),
                         reads=[pp_b[i]], writes=[junk_b, ss_b[si]])
                    S.op("act", lambda e: e.activation(out=ss[si][:, 1:2], in_=ss[si][:, 0:1], func=AF.Sqrt, bias=eps6[:, 0:1], scale=1.0 / nfeat),
                         reads=[ss_b[si], eps_b], writes=[ss_b[si]])
                    S.op("dve", lambda e: e.reciprocal(out=ss[si][:, 2:3], in_=ss[si][:, 1:2]), reads=[ss_b[si]], writes=[ss_b[si]])
                    S.op("dve", lambda e: e.tensor_scalar(out=cn[si][:, 0:ncol], in0=pp[i][:, 0:ncol], scalar1=ss[si][:, 2:3], scalar2=None, op0=ALU.mult),
                         reads=[pp_b[i], ss_b[si]], writes=[cn_b[si]])

                def rope_rows(pa, pb, ri, dsts, dst_b, c0, ti):
                    S.op("dve", lambda e: e.tensor_tensor(out=t1[ti][64:96, :], in0=pp[pa][64:96, :], in1=rt[ri][64:96, 0, :], op=ALU.mult),
                         reads=[pp_b[pa], rt_b[ri]], writes=[t1_b[ti]])
                    S.op("dve", lambda e: e.tensor_tensor(out=t2[ti][64:96, :], in0=pp[pb][64:96, :], in1=rt[ri][64:96, 1, :], op=ALU.mult),
                         reads=[pp_b[pb], rt_b[ri]], writes=[t2_b[ti]])
                    for d in dsts:
                        S.op("dve", lambda e, d=d: e.tensor_tensor(out=d[64:96, c0:c0 + 512], in0=t1[ti][64:96, :], in1=t2[ti][64:96, :], op=ALU.add),
                             reads=[t1_b[ti], t2_b[ti]], writes=[dst_b])

                for c in range(SEQ // 512):
                    hci = c % 2
                    S.dma("sp", hc[hci][:, :, :], self.G_v[c // 8][:, :, (c % 8) * 512:(c % 8) * 512 + 512], reads=[self.B("G")], writes=[hc_b[hci]])
                    S.dma("sp", rt[hci][64:96, :, :], self.ropek[:, :, c * 512:(c + 1) * 512].rearrange("a p t -> p a t"),
                          reads=[self.B("ropek")], writes=[rt_b[hci]])
                    for s in range(4):
                        si = (c * 4 + s) % 2
                        norm_tile(hci, s, wkv, 128, 128.0, si)
                        def ftr(e, si=si):
                            return e.transpose(out=psT[si][:, 0, :], in_=cn[si][:, 0:128], identity=ident[:, :])
                        S.op("pe", ftr, reads=[cn_b[si], id_b], writes=[psT_b[si]])
                        S.op("dve", lambda e, si=si, s=s, hci=hci: e.tensor_copy(out=cnT[hci][:, 0, s * 128:(s + 1) * 128], in_=psT[si][:, 0, :]),
                             reads=[psT_b[si]], writes=[cnT_b[hci]])
                        i = nextp()
                        S.op("pe", lambda e, i=i, s=s, hci=hci: e.matmul(pp[i][:, 0:256], lhsT=cnT[hci][:, 0, s * 128:(s + 1) * 128], rhs=wv4[:, :],
                                                                         start=True, stop=True), reads=[cnT_b[hci], u_b], writes=[pp_b[i]])
                        S.op("dve", lambda e, i=i, blk=c * 4 + s: e.tensor_copy(
                            out=vx[:, blk, :].rearrange("p (h d) -> p h d", d=65)[:, :, 0:64],
                            in_=pp[i][:, 0:256].rearrange("p (h d) -> p h d", d=64)), reads=[pp_b[i]], writes=[vx_b])
                    for h in range(4):
                        i = nextp()
                        S.op("pe", lambda e, i=i, h=h, hci=hci: e.matmul(pp[i][0:64, :], lhsT=ukv[:, h * 128:h * 128 + 64], rhs=cnT[hci][:, 0, :],
                                                                         start=True, stop=True), reads=[cnT_b[hci], u_b], writes=[pp_b[i]])
                        S.op("act", lambda e, i=i, h=h, c=c: e.activation(out=kT[h][0:64, c * 512:(c + 1) * 512], in_=pp[i][0:64, :], func=AF.Copy),
                             reads=[pp_b[i]], writes=[kT_b])
                    pa, pb = nextp(), nextp()
                    for (pi, wt) in ((pa, wkr), (pb, wkrs)):
                        def f(e, pi=pi, wt=wt, hci=hci):
                            for kc in range(8):
                                ins = e.matmul(pp[pi][0:96, :], lhsT=wt[:, kc, :], rhs=hc[hci][:, kc, :], start=(kc == 0), stop=(kc == 7))
                            return ins
                        S.op("pe", f, reads=[w_b, hc_b[hci]], writes=[pp_b[pi]])
                    rope_rows(pa, pb, hci, kT, kT_b, c * 512, hci)
                for c in range(T // 512):
                    hci = c % 2
                    S.dma("sp", hc[hci][:, :, :], self.hTo_v[:, :, c * 512:(c + 1) * 512], reads=[self.B("hTo")], writes=[hc_b[hci]])
                    S.dma("sp", rt[hci][64:96, :, :], self.ropeq[:, :, c * 512:(c + 1) * 512].rearrange("a p t -> p a t"),
                          reads=[self.B("ropeq")], writes=[rt_b[hci]])
                    for s in range(4):
                        si = (c * 4 + s) % 2
                        norm_tile(hci, s, wq, 256, 256.0, si)
                        def ftr(e, si=si):
                            e.transpose(out=psT[si][:, 0, :], in_=cn[si][:, 0:128], identity=ident[:, :])
                            return e.transpose(out=psT[si][:, 1, :], in_=cn[si][:, 128:256], identity=ident[:, :])
                        S.op("pe", ftr, reads=[cn_b[si], id_b], writes=[psT_b[si]])
                        S.op("dve", lambda e, si=si, s=s, hci=hci: e.tensor_copy(out=cnT[hci][:, :, s * 128:(s + 1) * 128], in_=psT[si][:, :, :]),
                             reads=[psT_b[si]], writes=[cnT_b[hci]])
                    for h in range(4):
                        pa, pb = nextp(), nextp()
                        for (pi, wt) in ((pa, uq), (pb, uqs)):
                            def f(e, pi=pi, wt=wt, hci=hci, h=h):
                                for kc in range(2):
                                    ins = e.matmul(pp[pi][0:96, :], lhsT=wt[:, kc, h * 96:(h + 1) * 96], rhs=cnT[hci][:, kc, :],
                                                   start=(kc == 0), stop=(kc == 1))
                                return ins
                            S.op("pe", f, reads=[u_b, cnT_b[hci]], writes=[pp_b[pi]])
                        S.op("act", lambda e, pa=pa, h=h, c=c: e.activation(out=qT[h][0:64, c * 512:(c + 1) * 512], in_=pp[pa][0:64, :], func=AF.Copy),
                             reads=[pp_b[pa]], writes=[qT_b])
                        rope_rows(pa, pb, hci, [qT[h]], qT_b, c * 512, h % 2)
                S.emit()
            NB3 = 3
            ps = [es.enter_context(self.pst("ml_ps%d" % i, [128, 512], F32)) for i in range(NB3)]; ps_b = [Buf() for _ in range(NB3)]
            po = [es.enter_context(self.pst("ml_po%d" % i, [128, 512], F32)) for i in range(4)]; po_b = [Buf() for _ in range(4)]
            et = [sb("et%d" % i, [128, 512], BF16) for i in range(NB3)]; et_b = [Buf() for _ in range(NB3)]
            ost = [[sb("ost%d_%d" % (i, q), [128, 4, 65], F32) for q in range(4)] for i in range(2)]
            ost_b = [[[Buf() for h in range(4)] for q in range(4)] for i in range(2)]
            sc = 96.0 ** -0.5
            NKB = SEQ // 128
            its = [(qc, h, kb) for qc in range(T // 512) for h in range(4) for kb in range(NKB)]

            def emit_st(n):
                qc, h, kb = its[n]
                i = n % NB3
                S.op("pe", lambda e: e.matmul(ps[i][:, :], lhsT=kT[h][0:96, kb * 128:(kb + 1) * 128],
                                              rhs=qT[h][0:96, qc * 512:(qc + 1) * 512], start=True, stop=True),
                     reads=[kT_b, qT_b], writes=[ps_b[i]])

            emit_st(0); emit_st(1)
            for n, (qc, h, kb) in enumerate(its):
                i = n % NB3
                oi = qc % 2
                if n + 2 < len(its):
                    emit_st(n + 2)
                S.op("act", lambda e, i=i: e.activation(out=et[i][:, :], in_=ps[i][:, :], func=AF.Exp, scale=sc),
                     reads=[ps_b[i]], writes=[et_b[i]])
                def g(e, i=i, h=h, kb=kb):
                    for qs in range(4):
                        ins = e.matmul(po[qs][:, 0:65], lhsT=et[i][:, qs * 128:(qs + 1) * 128], rhs=vx[:, kb, h * 65:(h + 1) * 65],
                                       start=(kb == 0), stop=(kb == NKB - 1))
                    return ins
                S.op("pe", g, reads=[et_b[i], vx_b], writes=po_b)
                if kb == NKB - 1:
                    for qs in range(4):
                        S.op("dve", lambda e, qs=qs, h=h, oi=oi: e.tensor_copy(out=ost[oi][qs][:, h, :], in_=po[qs][:, 0:65]),
                             reads=[po_b[qs]], writes=[ost_b[oi][qs][h]])
                    if h == 3:
                        for qs in range(4):
                            r0 = qc * 512 + qs * 128
                            S.dma("pool", self.Og["A"][r0:r0 + 128, :], ost[oi][qs][:, :, :].rearrange("p h d -> p (h d)"),
                                  reads=ost_b[oi][qs], writes=[self.B("OgA")])
            S.emit()

    def phase_merge1(self):
        nc, S = self.nc, self.S
        if not hasattr(self, "mT"):
            self.mT = self.scratch("mT", [D, T], BF16)
        mTv = self.mT.ap().rearrange("(kc p) t -> p kc t", p=128)
        yTv = [y.ap().rearrange("(kc p) t -> p kc t", p=128) for y in self.yT]
        with ExitStack() as es:
            sb = lambda n, sh, dt: es.enter_context(self.sbt("m1_" + n, sh, dt))
            wg = sb("wg", [128, 8, 4096], BF16); wb = sb("wb", [128, 8, 1024], BF16); w_b = Buf()
            self.wload(wg, self.w_in[:, 4000:8096], wb=w_b)
            self.wload(wb, self.w_branch[:, :], wb=w_b)
            hc = [sb("hc%d" % i, [128, 8, 512], BF16) for i in range(2)]; hc_b = [Buf() for _ in range(2)]
            yc = [[sb("yc%d_%d" % (i, n), [128, 2, 512], BF16) for n in range(4)] for i in range(2)]
            yc_b = [[Buf() for n in range(4)] for i in range(2)]
            mt = [sb("mt%d" % i, [128, 8, 512], BF16) for i in range(2)]; mt_b = [Buf() for _ in range(2)]
            gt = [sb("gt%d" % i, [128, 512], F32) for i in range(2)]; gt_b = [Buf() for _ in range(2)]
            acc = [sb("acc%d" % i, [128, 512], F32) for i in range(2)]; acc_b = [Buf() for _ in range(2)]
            tmp = [sb("tmp%d" % i, [128, 512], F32) for i in range(2)]; tmp_b = [Buf() for _ in range(2)]
            pg = [es.enter_context(self.pst("m1_pg%d" % i, [128, 512], F32)) for i in range(2)]; pg_b = [Buf() for _ in range(2)]
            pb = [es.enter_context(self.pst("m1_pb%d" % i, [128, 512], F32)) for i in range(2)]; pb_b = [Buf() for _ in range(2)]
            it = 0
            for c in range(T // 512):
                ci = c % 2
                S.dma("sp", hc[ci][:, :, :], self.hTo_v[:, :, c * 512:(c + 1) * 512], reads=[self.B("hTo")], writes=[hc_b[ci]])
                for n in range(4):
                    S.dma("sp", yc[ci][n][:, :, :], yTv[n][:, :, c * 512:(c + 1) * 512], reads=[self.B(self.yT[n].name)], writes=[yc_b[ci][n]])
                for dc in range(8):
                    ai = (c * 8 + dc) % 2
                    for n in range(4):
                        i = it % 2; it += 1
                        def f(e, i=i, n=n, dc=dc, ci=ci):
                            for kc in range(8):
                                ins = e.matmul(pg[i][:, :], lhsT=wg[:, kc, n * 1024 + dc * 128:n * 1024 + dc * 128 + 128], rhs=hc[ci][:, kc, :],
                                               start=(kc == 0), stop=(kc == 7))
                            return ins
                        S.op("pe", f, reads=[w_b, hc_b[ci]], writes=[pg_b[i]])
                        S.op("act", lambda e, i=i: e.activation(out=gt[i][:, :], in_=pg[i][:, :], func=AF.Sigmoid), reads=[pg_b[i]], writes=[gt_b[i]])
                        def f2(e, i=i, n=n, dc=dc, ci=ci):
                            for kc in range(2):
                                ins = e.matmul(pb[i][:, :], lhsT=wb[:, n * 2 + kc, dc * 128:dc * 128 + 128], rhs=yc[ci][n][:, kc, :],
                                               start=(kc == 0), stop=(kc == 1))
                            return ins
                        S.op("pe", f2, reads=[w_b, yc_b[ci][n]], writes=[pb_b[i]])
                        if n == 0:
                            S.op("dve", lambda e, i=i, ai=ai: e.tensor_tensor(out=acc[ai][:, :], in0=pb[i][:, :], in1=gt[i][:, :], op=ALU.mult),
                                 reads=[pb_b[i], gt_b[i]], writes=[acc_b[ai]])
                        else:
                            S.op("dve", lambda e, i=i: e.tensor_tensor(out=tmp[i][:, :], in0=pb[i][:, :], in1=gt[i][:, :], op=ALU.mult),
                                 reads=[pb_b[i], gt_b[i]], writes=[tmp_b[i]])
                            if n < 3:
                                S.op("pool", lambda e, i=i, ai=ai: e.tensor_tensor(out=acc[ai][:, :], in0=acc[ai][:, :], in1=tmp[i][:, :], op=ALU.add),
                                     reads=[acc_b[ai], tmp_b[i]], writes=[acc_b[ai]])
                            else:
                                S.op("pool", lambda e, i=i, ai=ai, dc=dc, ci=ci: e.tensor_tensor(out=mt[ci][:, dc, :], in0=acc[ai][:, :], in1=tmp[i][:, :], op=ALU.add),
                                     reads=[acc_b[ai], tmp_b[i]], writes=[mt_b[ci]])
                S.dma("pool", mTv[:, :, c * 512:(c + 1) * 512], mt[ci][:, :, :], reads=[mt_b[ci]], writes=[self.B("mT")])
            S.emit()

    def phase_merge2(self):
        nc, S = self.nc, self.S
        mTv = self.mT.ap().rearrange("(kc p) t -> p kc t", p=128)
        h1Tv = self.h1T.ap().rearrange("(kc p) t -> p kc t", p=128)
        with ExitStack() as es:
            sb = lambda n, sh, dt: es.enter_context(self.sbt("m2_" + n, sh, dt))
            c = self.ln_consts(es, self.ln1_g.ap(), self.ln1_b.ap(), "m2_")
            wo = sb("wo", [128, 8, 1024], BF16); wpg = sb("wpg", [128, 8, 1024], BF16); wpl = sb("wpl", [128, 2, 1024], BF16); w_b = Buf()
            self.wload(wo, self.w_out[:, :], wb=w_b); self.wload(wpg, self.w_ple_gate[:, :], wb=w_b); self.wload(wpl, self.w_ple[:, :], wb=w_b)
            mc = [sb("mc%d" % i, [128, 8, 512], BF16) for i in range(2)]; mc_b = [Buf() for _ in range(2)]
            ha = [sb("ha%d" % i, [128, D], F32) for i in range(2)]; ha_b = [Buf() for _ in range(2)]
            r1 = [sb("r1%d" % i, [128, D], F32) for i in range(2)]; r1_b = [Buf() for _ in range(2)]
            h1 = [sb("h1%d" % i, [128, D], F32) for i in range(2)]; h1_b = [Buf() for _ in range(2)]
            hb = [sb("hb%d" % i, [128, D], BF16) for i in range(2)]; hb_b = [Buf() for _ in range(2)]
            hts = [sb("hts%d" % i, [128, 8, 512], BF16) for i in range(2)]; hts_b = [Buf() for _ in range(2)]
            pt = [sb("pt%d" % i, [128, 256], F32) for i in range(2)]; pt_b = [Buf() for _ in range(2)]
            ptb = [sb("ptb%d" % i, [128, 256], BF16) for i in range(2)]; ptb_b = [Buf() for _ in range(2)]
            pT = [sb("pT%d" % i, [128, 2, 128], BF16) for i in range(2)]; pT_b = [Buf() for _ in range(2)]
            sg = [sb("sg%d" % i, [128, D], F32) for i in range(2)]; sg_b = [Buf() for _ in range(2)]
            ps = lambda n, sh, dt: es.enter_context(self.pst("m2_" + n, sh, dt))
            po = [ps("po%d" % i, [128, 512], F32) for i in range(2)]; po_b = [Buf() for _ in range(2)]
            pq = [ps("pq%d" % i, [128, 512], F32) for i in range(2)]; pq_b = [Buf() for _ in range(2)]
            pp = [ps("pp%d" % i, [128, 512], F32) for i in range(2)]; pp_b = [Buf() for _ in range(2)]
            psT = [ps("psT", [128, 8, 128], BF16)]; psT_b = [Buf()]
            psP = [ps("psP", [128, 2, 128], BF16)]; psP_b = [Buf()]
            for tt in range(T // 128):
                i = tt % 2
                cc, s = tt // 4, tt % 4
                ci = cc % 2
                if s == 0:
                    S.dma("sp", mc[ci][:, :, :], mTv[:, :, cc * 512:(cc + 1) * 512], reads=[self.B("mT")], writes=[mc_b[ci]])
                S.dma("sp", ha[i][:, :], self.hA[tt * 128:(tt + 1) * 128, :], reads=[self.B("hA")], writes=[ha_b[i]])
                S.dma("sp", pt[i][:, :], self.p[tt * 128:(tt + 1) * 128, :], reads=[self.B("p")], writes=[pt_b[i]])
                for hf in range(2):
                    def f(e, hf=hf, ci=ci, s=s):
                        for dc in range(8):
                            ins = e.matmul(po[hf][:, :], lhsT=mc[ci][:, dc, s * 128:(s + 1) * 128], rhs=wo[:, dc, hf * 512:(hf + 1) * 512],
                                           start=(dc == 0), stop=(dc == 7))
                        return ins
                    S.op("pe", f, reads=[w_b, mc_b[ci]], writes=[po_b[hf]])
                    S.op("dve", lambda e, hf=hf, i=i: e.scalar_tensor_tensor(out=r1[i][:, hf * 512:(hf + 1) * 512], in0=ha[i][:, hf * 512:(hf + 1) * 512],
                                                                              scalar=ALPHA, in1=po[hf][:, :], op0=ALU.mult, op1=ALU.add),
                         reads=[ha_b[i], po_b[hf]], writes=[r1_b[i]])
                self.ln_ops("m2", r1[i], r1_b[i], h1[i], h1_b[i], c["g"], c["bb"], c["tmp"][i])
                S.op("act", lambda e, i=i: e.activation(out=hb[i][:, :], in_=h1[i][:, :], func=AF.Copy), reads=[h1_b[i]], writes=[hb_b[i]])
                self.transpose_ops(hb[i], hb_b[i], psT[0], psT_b[0], hts[ci], hts_b[ci], s * 128, c["ident"], c["ident_b"])
                S.op("act", lambda e, i=i: e.activation(out=ptb[i][:, :], in_=pt[i][:, :], func=AF.Copy), reads=[pt_b[i]], writes=[ptb_b[i]])
                self.transpose_ops(ptb[i], ptb_b[i], psP[0], psP_b[0], pT[i], pT_b[i], 0, c["ident"], c["ident_b"], nk=2)
                for hf in range(2):
                    def f(e, hf=hf, ci=ci, s=s):
                        for kc in range(8):
                            ins = e.matmul(pq[hf][:, :], lhsT=hts[ci][:, kc, s * 128:(s + 1) * 128], rhs=wpg[:, kc, hf * 512:(hf + 1) * 512],
                                           start=(kc == 0), stop=(kc == 7))
                        return ins
                    S.op("pe", f, reads=[w_b, hts_b[ci]], writes=[pq_b[hf]])
                    S.op("act", lambda e, hf=hf, i=i: e.activation(out=sg[i][:, hf * 512:(hf + 1) * 512], in_=pq[hf][:, :], func=AF.Sigmoid),
                         reads=[pq_b[hf]], writes=[sg_b[i]])
                    def f2(e, hf=hf, i=i):
                        for kc in range(2):
                            ins = e.matmul(pp[hf][:, :], lhsT=pT[i][:, kc, :], rhs=wpl[:, kc, hf * 512:(hf + 1) * 512], start=(kc == 0), stop=(kc == 1))
                        return ins
                    S.op("pe", f2, reads=[w_b, pT_b[i]], writes=[pp_b[hf]])
                    S.op("dve", lambda e, hf=hf, i=i: e.tensor_tensor(out=sg[i][:, hf * 512:(hf + 1) * 512], in0=pp[hf][:, :], in1=sg[i][:, hf * 512:(hf + 1) * 512], op=ALU.mult),
                         reads=[pp_b[hf], sg_b[i]], writes=[sg_b[i]])
                S.op("dve", lambda e, i=i: e.scalar_tensor_tensor(out=sg[i][:, :], in0=h1[i][:, :], scalar=ALPHA, in1=sg[i][:, :], op0=ALU.mult, op1=ALU.add),
                     reads=[h1_b[i], sg_b[i]], writes=[sg_b[i]])
                S.dma("pool", self.r2[tt * 128:(tt + 1) * 128, :], sg[i][:, :], reads=[sg_b[i]], writes=[self.B("r2")])
                if s == 3:
                    S.dma("pool", h1Tv[:, :, cc * 512:(cc + 1) * 512], hts[ci][:, :, :], reads=[hts_b[ci]], writes=[self.B("h1T")])
            S.emit()

    def phase_ffn(self):
        nc, S = self.nc, self.S
        h1Tv = self.h1T.ap().rearrange("(kc p) t -> p kc t", p=128)
        hTov = self.hT_o.ap().rearrange("(kc p) t -> p kc t", p=128)
        with ExitStack() as es:
            sb = lambda n, sh, dt: es.enter_context(self.sbt("ff_" + n, sh, dt))
            c = self.ln_consts(es, self.ln2_g.ap(), self.ln2_b.ap(), "ff_")
            w1 = sb("w1", [128, 8, 4096], BF16); w2 = sb("w2", [128, 32, 1024], BF16); w_b = Buf()
            self.wload(w1, self.w_ff1[:, :], wb=w_b); self.wload(w2, self.w_ff2[:, :], wb=w_b)
            hc = [sb("hc%d" % i, [128, 8, 256], BF16) for i in range(2)]; hc_b = [Buf() for _ in range(2)]
            uT = sb("uT", [128, 32, 256], BF16); uT_b = Buf()
            rl = [sb("rl%d" % i, [128, 256], F32) for i in range(2)]; rl_b = [Buf() for _ in range(2)]
            rt = [sb("rt%d" % i, [128, D], F32) for i in range(2)]; rt_b = [Buf() for _ in range(2)]
            ot = [sb("ot%d" % i, [128, D], F32) for i in range(2)]; ot_b = [Buf() for _ in range(2)]
            ob = [sb("ob%d" % i, [128, D], BF16) for i in range(2)]; ob_b = [Buf() for _ in range(2)]
            hts = [sb("hts%d" % i, [128, 8, 256], BF16) for i in range(2)]; hts_b = [Buf() for _ in range(2)]
            ps = lambda n, sh, dt: es.enter_context(self.pst("ff_" + n, sh, dt))
            pu = [ps("pu%d" % i, [128, 512], F32) for i in range(2)]; pu_b = [Buf() for _ in range(2)]
            po = [ps("po%d" % i, [128, 512], F32) for i in range(2)]; po_b = [Buf() for _ in range(2)]
            psT = [ps("psT%d" % i, [128, 8, 128], BF16) for i in range(2)]; psT_b = [Buf() for _ in range(2)]
            it = 0
            for c2 in range(T // 256):
                ci = c2 % 2
                S.dma("sp", hc[ci][:, :, :], h1Tv[:, :, c2 * 256:(c2 + 1) * 256], reads=[self.B("h1T")], writes=[hc_b[ci]])
                for fc in range(32):
                    i = it % 2; it += 1
                    def f(e, i=i, fc=fc, ci=ci):
                        for kc in range(8):
                            ins = e.matmul(pu[i][:, 0:256], lhsT=w1[:, kc, fc * 128:(fc + 1) * 128], rhs=hc[ci][:, kc, :], start=(kc == 0), stop=(kc == 7))
                        return ins
                    S.op("pe", f, reads=[w_b, hc_b[ci]], writes=[pu_b[i]])
                    S.op("act", lambda e, i=i: e.activation(out=rl[i][:, :], in_=pu[i][:, 0:256], func=AF.Relu), reads=[pu_b[i]], writes=[rl_b[i]])
                    S.op("dve", lambda e, i=i, fc=fc: e.tensor_tensor(out=uT[:, fc, :], in0=rl[i][:, :], in1=rl[i][:, :], op=ALU.mult),
                         reads=[rl_b[i]], writes=[uT_b])
                for s in range(2):
                    tt = c2 * 2 + s
                    i = tt % 2
                    S.dma("sp", rt[i][:, :], self.r2[tt * 128:(tt + 1) * 128, :], reads=[self.B("r2")], writes=[rt_b[i]])
                    for hf in range(2):
                        def f(e, hf=hf, s=s):
                            for fc in range(32):
                                ins = e.matmul(po[hf][:, :], lhsT=uT[:, fc, s * 128:(s + 1) * 128], rhs=w2[:, fc, hf * 512:(hf + 1) * 512],
                                               start=(fc == 0), stop=(fc == 31))
                            return ins
                        S.op("pe", f, reads=[w_b, uT_b], writes=[po_b[hf]])
                        S.op("dve", lambda e, hf=hf, i=i: e.tensor_tensor(out=rt[i][:, hf * 512:(hf + 1) * 512], in0=po[hf][:, :], in1=rt[i][:, hf * 512:(hf + 1) * 512], op=ALU.add),
                             reads=[po_b[hf], rt_b[i]], writes=[rt_b[i]])
                    self.ln_ops("ff", rt[i], rt_b[i], ot[i], ot_b[i], c["g"], c["bb"], c["tmp"][i])
                    S.dma("pool", self.hA_o[tt * 128:(tt + 1) * 128, :], ot[i][:, :], reads=[ot_b[i]], writes=[self.B("hA_out")])
                    S.op("act", lambda e, i=i: e.activation(out=ob[i][:, :], in_=ot[i][:, :], func=AF.Copy), reads=[ot_b[i]], writes=[ob_b[i]])
                    self.transpose_ops(ob[i], ob_b[i], psT[i], psT_b[i], hts[ci], hts_b[ci], s * 128, c["ident"], c["ident_b"])
                S.dma("pool", hTov[:, :, c2 * 256:(c2 + 1) * 256], hts[ci][:, :, :], reads=[hts_b[ci]], writes=[self.B("hT_out")])
            S.emit()


NEG = -30000.0


def _t5_bucket(rel):
    nb, max_exact = 16, 8
    n = np.abs(rel)
    large = max_exact + (np.log(np.maximum(n, 1).astype(np.float32) / max_exact)
                         / math.log(1024 / max_exact) * (nb - max_exact)).astype(np.int32)
    large = np.minimum(large, nb - 1)
    return np.where(rel > 0, nb, 0) + np.where(n < max_exact, n, large)


def _banded_tab(rel_bias, heads, r, half, kb0, dms, hf):
    L = T // r
    nb = L // 128
    k = np.arange(128)[:, None]; q = np.arange(128)[None, :]
    out = []
    for h in heads:
        for mode in range(3):
            b = {0: 1, 1: 0, 2: nb - 1}[mode]
            if nb == 2 and mode == 0:
                b = 0
            for dm in dms:
                m = b + dm
                f = kb0 + 128 * m + k
                rel = f - (128 * b + q)
                val = rel_bias[_t5_bucket(rel * r), h].astype(np.float32)
                ok = np.abs(rel) <= half
                if mode == 1 and hf == 0:
                    ok = ok & (f >= 0)
                if mode == 2 and hf == 1:
                    ok = ok & (f < L)
                if mode == 0:
                    pass
                out.append(np.where(ok, val, NEG).astype(np.float32))
    return np.stack(out)


def _na_tab(rpb, hf):
    k = np.arange(128)[:, None]; q = np.arange(128)[None, :]
    a_loc, kc = k // 64, k % 64
    r_loc, c = q // 64, q % 64
    col_start = np.clip(c - 8, 0, 48)
    col_ok = (kc >= col_start) & (kc < col_start + 16)
    dc = np.clip(kc - c, -15, 15) + 15

    def tile(h, i, j):
        if j < 0 or j >= 64:
            return np.full((128, 128), NEG, np.float32)
        r = 2 * i + r_loc; a = 2 * j + a_loc
        start = np.clip(r - 4, 0, 120)
        ok = (a >= start) & (a < start + 8) & col_ok
        dr = np.clip(a - r + 7, 0, 14)
        return np.where(ok, rpb[h, dr, dc], NEG).astype(np.float32)
    out = []
    for h in range(4):
        for dj in range(-2, 3):
            out.append(tile(h, 10, 10 + dj))
        for b in (0, 1, 30, 31):
            i = hf * 32 + b
            for dj in range(-3, 4):
                out.append(tile(h, i, i + dj))
    return np.stack(out)


def _rope_tab(pos):
    half = 16
    inv = (10000.0 ** (-np.arange(half, dtype=np.float32) / half)).astype(np.float32)
    ang = pos.astype(np.float32)[None, :] * inv[:, None]
    cos, sin = np.cos(ang).astype(np.float32), np.sin(ang).astype(np.float32)
    return np.stack([np.concatenate([cos, cos], 0), np.concatenate([-sin, sin], 0)]).astype(np.float32)


_PROGS = {}


def build_A():
    k = K()
    k.declare_A()
    k.phase_ln_emb()
    k.finish([k.B("hA_out"), k.B("hT_out")])
    return k


def build_B(debug=()):
    k = K(debug=debug)
    k.declare_B()
    edge = {0: 0, 1: 1, 30: 2, 31: 3}
    k.phase_mla()
    k.phase_combine("ca_", [k.Og["A"]], k.yT[0])
    k.phase_banded("sw_", 416, 672, 800, 2, 1, 128, lambda b, nb: ([b, b + 1, b + 2], 3 if b == 0 else (6 if b == nb - 1 else 0)),
                   9, k.tabB, k.Og["B"], True)
    k.phase_combine("cb_", [k.Og["B"]], k.yT[1], sink=True)
    k.phase_banded("na_", 928, 1184, 1440, 4, 1, 384,
                   lambda b, nb: (list(range(b, b + 7)), 5 + 7 * edge[b]) if b in edge else (list(range(b + 1, b + 6)), 0),
                   33, k.tabC, k.Og["C"], False)
    k.phase_combine("cc_", [k.Og["C"]], k.yT[2])
    for g, r in enumerate((1, 4, 16)):
        k.phase_banded("d%d_" % g, 1696 + g * 256, 2464 + g * 256, 3232 + g * 256, 4, r, 64,
                       lambda b, nb: ([b, b + 1], 2 if b == 0 else (4 if b == nb - 1 else 0)), 6, k.tabD[g], k.Og["D%d" % g], False)
    k.phase_combine("cd_", [k.Og["D0"], k.Og["D1"], k.Og["D2"]], k.yT[3])
    k.phase_merge1()
    k.phase_merge2()
    k.phase_ffn()
    outs = [k.B("hA_out"), k.B("hT_out")]
    if debug:
        outs += k.dump(debug)
    k.finish(outs)
    return k


def _f(a):
    return np.ascontiguousarray(np.asarray(a, dtype=np.float32))


def build_F():
    k = K()
    k.declare_F()
    k.phase_ln_emb()
    edge = {0: 0, 1: 1, 30: 2, 31: 3}
    for l in range(NL):
        k.allgather(l)
        k.set_layer(l)
        k.phase_mla()
        k.phase_combine("ca_", [k.Og["A"]], k.yT[0])
        k.phase_banded("sw_", 416, 672, 800, 2, 1, 128, lambda b, nb: ([b, b + 1, b + 2], 3 if b == 0 else (6 if b == nb - 1 else 0)),
                       9, k.tabB, k.Og["B"], True)
        k.phase_combine("cb_", [k.Og["B"]], k.yT[1], sink=True)
        k.phase_banded("na_", 928, 1184, 1440, 4, 1, 384,
                       lambda b, nb: (list(range(b, b + 7)), 5 + 7 * edge[b]) if b in edge else (list(range(b + 1, b + 6)), 0),
                       33, k.tabC, k.Og["C"], False)
        k.phase_combine("cc_", [k.Og["C"]], k.yT[2])
        for g, r in enumerate((1, 4, 16)):
            k.phase_banded("d%d_" % g, 1696 + g * 256, 2464 + g * 256, 3232 + g * 256, 4, r, 64,
                           lambda b, nb: ([b, b + 1], 2 if b == 0 else (4 if b == nb - 1 else 0)), 6, k.tabD[g], k.Og["D%d" % g], False)
        k.phase_combine("cd_", [k.Og["D0"], k.Og["D1"], k.Og["D2"]], k.yT[3])
        k.phase_merge1()
        k.phase_merge2()
        k.phase_ffn()
    k.finish([k.B("out")])
    return k


def kernel(**inputs):
    x = _f(inputs["x"]); p = _f(inputs["p"])
    rel_bias = _f(inputs["rel_bias"]); na_rpb = _f(inputs["na_rpb"])
    ident = np.eye(128, dtype=np.float32)
    if "F" not in _PROGS:
        _PROGS["F"] = build_F()
    ropek = _rope_tab(np.arange(SEQ))
    shared = {"ident": ident, "ln_emb_g": _f(inputs["ln_emb_g"]), "ln_emb_b": _f(inputs["ln_emb_b"]), "ropek": ropek}
    for l in range(NL):
        for n in ("w_in", "mla_q_norm", "mla_w_uq", "mla_kv_norm", "mla_w_ukv", "swa_sink", "w_out", "ln1_g", "ln1_b",
                  "w_ff1", "w_ff2", "w_ple", "w_ple_gate", "ln2_g", "ln2_b"):
            shared["%s_%d" % (n, l)] = _f(inputs[n][l])
        shared["w_branch_%d" % l] = _f(inputs["w_branch"][l]).reshape(4 * 256, D)
    maps = []
    for c in range(8):
        b, hf = c // 2, c % 2
        m = dict(shared)
        m["x"] = np.ascontiguousarray(x[b, hf * T:(hf + 1) * T])
        m["ropeq"] = _rope_tab(hf * T + np.arange(T))
        m["tabB"] = _banded_tab(rel_bias, range(4), 1, 128, -128, (0, 1, 2), hf)
        for g, r in enumerate((1, 4, 16)):
            m["tabD%d" % g] = _banded_tab(rel_bias, range(4 + 4 * g, 8 + 4 * g), r, 64, -64, (0, 1), hf)
        for l in range(NL):
            m["p_%d" % l] = np.ascontiguousarray(p[l, b, hf * T:(hf + 1) * T])
            m["tabC_%d" % l] = _na_tab(na_rpb[l], hf)
        maps.append(m)
    res = run_bass_kernel_spmd(_PROGS["F"].nc, maps, core_ids=list(range(8)))
    out = np.zeros((4, SEQ, D), np.float32)
    for c in range(8):
        b, hf = c // 2, c % 2
        out[b, hf * T:(hf + 1) * T] = np.asarray(res.results[c]["out"], dtype=np.float32)
    return out
```

```python
import math
from contextlib import ExitStack

import numpy as np
import concourse.bass as bass
import concourse.mybir as mybir
from concourse.bass_utils import run_bass_kernel_spmd

F32 = mybir.dt.float32
BF16 = mybir.dt.bfloat16
AF = mybir.ActivationFunctionType
ALU = mybir.AluOpType

D = 1024
T = 4096
SEQ = 8192
NL = 2
ALPHA = (2 * NL) ** 0.25
IN_COLS = 8096
NDMA_SEM = 8


class Buf:
    def __init__(self, name="", multi=False):
        self.name = name
        self.multi = multi
        self.writers = {}
        self.readers = {}


class Op:
    __slots__ = ("eng", "emit", "deps", "needed", "sem", "val", "is_dma", "key", "idx")


class Sched:
    ENGS = ("pe", "act", "dve", "pool", "sp")

    def __init__(self, nc, es):
        self.nc = nc
        self.sem = {e: nc.alloc_semaphore(name="s_" + e) for e in self.ENGS}
        self.dsem = {e: [nc.alloc_semaphore(name="d_%s%d" % (e, i)) for i in range(NDMA_SEM)]
                     for e in ("sp", "pool", "act")}
        self.all_sems = list(self.sem.values()) + [x for v in self.dsem.values() for x in v]
        self.cnt = {e: 0 for e in self.ENGS}
        self.dcnt = {e: 0 for e in self.dsem}
        self.dlast = {e: [None] * NDMA_SEM for e in self.dsem}
        self.waited = {e: {} for e in self.ENGS}
        self.ops = {e: [] for e in self.ENGS}
        self.same_engine_sync = True
        self.nops = 0

    def _mk(self, eng, emit, reads, writes, is_dma):
        op = Op()
        op.eng, op.emit, op.is_dma, op.needed = eng, emit, is_dma, False
        op.sem = op.val = None
        op.idx = self.nops
        self.nops += 1
        deps = {}

        def add(o):
            if o is None:
                return
            if (not o.is_dma) and o.eng == eng:
                if eng == "pe" or not self.same_engine_sync:
                    return
            deps[id(o)] = o

        for b in reads:
            for o in b.writers.values():
                add(o)
        for b in writes:
            for o in b.readers.values():
                add(o)
            for o in b.writers.values():
                add(o)
        if is_dma:
            k = self.dcnt[eng]
            slot = k % NDMA_SEM
            self.dcnt[eng] = k + 1
            op.sem = self.dsem[eng][slot]
            op.val = 16 * (k // NDMA_SEM + 1)
            op.key = ("d", eng, slot)
            add(self.dlast[eng][slot])
            self.dlast[eng][slot] = op
            op.needed = True
        else:
            op.key = ("e", eng)
        op.deps = list(deps.values())
        for o in op.deps:
            o.needed = True
        for b in reads:
            b.readers[op.key] = op
        for b in writes:
            if b.multi:
                b.writers[op.key] = op
            else:
                b.writers = {op.key: op}
                b.readers = {}
        self.ops[eng].append(op)
        return op

    def op(self, eng, emit, reads=(), writes=()):
        return self._mk(eng, emit, reads, writes, False)

    def dma(self, eng, out, in_, reads=(), writes=()):
        return self._mk(eng, lambda e: e.dma_start(out=out, in_=in_), reads, writes, True)

    def collective(self, emit, sem, val, reads=(), writes=()):
        op = self._mk("pool", emit, reads, writes, False)
        op.is_dma = True
        op.needed = True
        op.sem, op.val = sem, val
        op.key = ("cc", id(sem))
        for b in reads:
            b.readers.pop(("e", "pool"), None)
            b.readers[op.key] = op
        for b in writes:
            b.writers = {op.key: op}
        return op

    def emit(self):
        nc = self.nc
        for e in self.ENGS:
            for op in self.ops[e]:
                if not op.is_dma and op.needed and op.sem is None:
                    self.cnt[e] += 1
                    op.sem = self.sem[e]
                    op.val = self.cnt[e]
        with nc.Block() as block:
            hooks = {"pe": block.tensor, "act": block.scalar, "dve": block.vector,
                     "pool": block.gpsimd, "sp": block.sync}
            for e in self.ENGS:
                ops = self.ops[e]
                if not ops:
                    continue
                waited = self.waited[e]

                def body(eng, ops=ops, waited=waited):
                    for op in ops:
                        for d in op.deps:
                            if d.val is None:
                                continue
                            k = id(d.sem)
                            if waited.get(k, 0) < d.val:
                                eng.wait_ge(d.sem, d.val)
                                waited[k] = d.val
                        inst = op.emit(eng)
                        if op.key[0] == "cc":
                            inst.then_inc(op.sem)
                        elif op.is_dma:
                            inst.then_inc(op.sem, 16)
                        elif op.needed:
                            inst.then_inc(op.sem, 1)

                hooks[e](body)
        self.ops = {e: [] for e in self.ENGS}

    def final_wait(self, eng, bufs):
        self._mk(eng, lambda e: e.nop(), bufs, (), False)


def _bcast_rows(ap1d, n=128):
    return ap1d.partition_broadcast(n)


class K:
    def __init__(self, debug=()):
        self.debug = set(debug)
        self.nc = bass.Bass("TRN2", target_bir_lowering=False)
        self.es = ExitStack()
        self.cc_sems = [self.nc.alloc_semaphore(name="cc%d" % i) for i in range(NL)]
        self.cm = self.nc.cleanup_on_exit()
        self.cm.__enter__()
        self.S = Sched(self.nc, self.es)
        self.dram = {}
        self.bufs = {}

    def sbt(self, name, shape, dt):
        self.uid = getattr(self, "uid", 0) + 1
        return self.nc.sbuf_tensor("%s_%d" % (name, self.uid), shape, dt)

    def pst(self, name, shape, dt):
        self.uid = getattr(self, "uid", 0) + 1
        return self.nc.psum_tensor("%s_%d" % (name, self.uid), shape, dt)

    def ext_in(self, name, shape, dt=F32):
        t = self.nc.dram_tensor(name, list(shape), dt, kind="ExternalInput")
        self.dram[name] = t
        self.bufs[name] = Buf(name, multi=True)
        return t

    def ext_out(self, name, shape, dt=F32):
        t = self.nc.dram_tensor(name, list(shape), dt, kind="ExternalOutput")
        self.dram[name] = t
        self.bufs[name] = Buf(name, multi=True)
        return t

    def scratch(self, name, shape, dt):
        t = self.nc.dram_tensor(name, list(shape), dt)
        self.shapes = getattr(self, "shapes", {})
        self.shapes[name] = (list(shape), dt)
        self.dram[name] = t
        self.bufs[name] = Buf(name, multi=True)
        return t

    def B(self, name):
        return self.bufs[name]

    def ln_ops(self, tag, src, src_b, dst, dst_b, g_bc, b_bc, tmp, eps=1e-5):
        S = self.S
        st, mv, sd = tmp["st"], tmp["mv"], tmp["sd"]
        bst = tmp["b"]
        def f_stats(e):
            e.bn_stats(out=st[:, 0, :], in_=src[:, 0:512])
            return e.bn_stats(out=st[:, 1, :], in_=src[:, 512:1024])
        S.op("dve", f_stats, reads=[src_b], writes=[tmp["b0"]])
        S.op("dve", lambda e: e.bn_aggr(out=mv[:, :], in_=st[:, :, :]), reads=[tmp["b0"]], writes=[bst])
        S.op("act", lambda e: e.activation(out=sd[:, 0:1], in_=mv[:, 1:2], func=AF.Sqrt, bias=tmp["eps"][:, 0:1], scale=1.0),
             reads=[bst], writes=[tmp["b2"]])
        S.op("dve", lambda e: e.reciprocal(out=sd[:, 1:2], in_=sd[:, 0:1]), reads=[tmp["b2"]], writes=[tmp["b3"]])
        S.op("dve", lambda e: e.tensor_scalar(out=dst[:, :], in0=src[:, :], scalar1=mv[:, 0:1], scalar2=sd[:, 1:2],
                                              op0=ALU.subtract, op1=ALU.mult),
             reads=[src_b, bst, tmp["b3"]], writes=[dst_b])
        S.op("pool", lambda e: e.tensor_tensor(out=dst[:, :], in0=dst[:, :], in1=g_bc[:, :], op=ALU.mult),
             reads=[dst_b, tmp["gb"]], writes=[dst_b])
        S.op("dve", lambda e: e.tensor_tensor(out=dst[:, :], in0=dst[:, :], in1=b_bc[:, :], op=ALU.add),
             reads=[dst_b, tmp["gb"]], writes=[dst_b])

    def transpose_ops(self, src_bf, src_b, psT, psT_b, dstT, dstT_b, col0, ident, ident_b, nk=8):
        S = self.S
        def f_tr(e):
            for kc in range(nk):
                i = e.transpose(out=psT[:, kc, :], in_=src_bf[:, kc * 128:(kc + 1) * 128], identity=ident[:, :])
            return i
        S.op("pe", f_tr, reads=[src_b, ident_b], writes=[psT_b])
        S.op("dve", lambda e: e.tensor_copy(out=dstT[:, 0:nk, col0:col0 + 128], in_=psT[:, 0:nk, :]),
             reads=[psT_b], writes=[dstT_b])

    def ln_consts(self, es, g_ap, b_ap, tagp):
        nc, S = self.nc, self.S
        sb = lambda n, sh, dt: es.enter_context(self.sbt(tagp + n, sh, dt))
        c = {}
        c["g"] = sb("g_bc", [128, D], F32); c["bb"] = sb("b_bc", [128, D], F32)
        c["eps"] = sb("eps", [128, 1], F32)
        c["ident"] = sb("identb", [128, 128], BF16); c["identf"] = sb("identf", [128, 128], F32)
        c["gb"] = Buf("gb"); c["ident_b"] = Buf("ident")
        S.dma("sp", c["g"][:, :], g_ap.partition_broadcast(128), writes=[c["gb"]])
        S.dma("sp", c["bb"][:, :], b_ap.partition_broadcast(128), writes=[c["gb"]])
        S.dma("sp", c["identf"][:, :], self.ident_f[:, :], writes=[c["ident_b"]])
        S.op("dve", lambda e: e.tensor_copy(out=c["ident"][:, :], in_=c["identf"][:, :]), reads=[c["ident_b"]], writes=[c["ident_b"]])
        S.op("dve", lambda e: e.memset(c["eps"][:, :], 1e-5), writes=[c["gb"]])
        c["tmp"] = []
        for i in range(2):
            t = {"st": sb("st%d" % i, [128, 2, 6], F32), "mv": sb("mv%d" % i, [128, 2], F32),
                 "sd": sb("sd%d" % i, [128, 2], F32), "b": Buf(), "b0": Buf(), "b2": Buf(), "b3": Buf(),
                 "gb": c["gb"], "eps": c["eps"]}
            c["tmp"].append(t)
        return c

    def phase_ln_emb(self):
        nc, S = self.nc, self.S
        self.hTv = self.hT.ap().rearrange("(kc p) t -> p kc t", p=128)
        with ExitStack() as es:
            sb = lambda n, sh, dt: es.enter_context(self.sbt("le_" + n, sh, dt))
            c = self.ln_consts(es, self.ln_emb_g.ap(), self.ln_emb_b.ap(), "le_")
            xt = [sb("xt%d" % i, [128, D], F32) for i in range(2)]; xt_b = [Buf() for _ in range(2)]
            yt = [sb("yt%d" % i, [128, D], F32) for i in range(2)]; yt_b = [Buf() for _ in range(2)]
            yb = [sb("yb%d" % i, [128, D], BF16) for i in range(2)]; yb_b = [Buf() for _ in range(2)]
            hts = [sb("hts%d" % i, [128, 8, 512], BF16) for i in range(2)]; hts_b = [Buf() for _ in range(2)]
            psT = [es.enter_context(self.pst("le_psT%d" % i, [128, 8, 128], BF16)) for i in range(2)]
            psT_b = [Buf() for _ in range(2)]
            hTv = self.hT.ap().rearrange("(kc p) t -> p kc t", p=128)
            for tt in range(T // 128):
                i = tt % 2
                S.dma("sp", xt[i][:, :], self.x[tt * 128:(tt + 1) * 128, :], reads=[self.B("x")], writes=[xt_b[i]])
                self.ln_ops("le", xt[i], xt_b[i], yt[i], yt_b[i], c["g"], c["bb"], c["tmp"][i])
                S.dma("pool", self.hA[tt * 128:(tt + 1) * 128, :], yt[i][:, :], reads=[yt_b[i]], writes=[self.B("hA")])
                S.op("act", lambda e, i=i: e.activation(out=yb[i][:, :], in_=yt[i][:, :], func=AF.Copy),
                     reads=[yt_b[i]], writes=[yb_b[i]])
                j = (tt // 4) % 2
                self.transpose_ops(yb[i], yb_b[i], psT[i], psT_b[i], hts[j], hts_b[j], (tt % 4) * 128, c["ident"], c["ident_b"])
                if tt % 4 == 3:
                    c0 = (tt // 4) * 512
                    S.dma("pool", hTv[:, :, c0:c0 + 512], hts[j][:, :, :], reads=[hts_b[j]], writes=[self.B("hT")])
            S.emit()

    def dump(self, names):
        outs = []
        for n in names:
            shape, dt = self.shapes[n]
            o = self.ext_out("dbg_" + n, shape, dt)
            src, dst = self.dram[n].ap(), o.ap()
            R = shape[0]
            step = max(1, R // 4)
            for r0 in range(0, R, step):
                self.S.dma("sp", dst[r0:min(R, r0 + step)], src[r0:min(R, r0 + step)], reads=[self.B(n)], writes=[self.B("dbg_" + n)])
            outs.append(self.B("dbg_" + n))
        return outs

    def finish(self, out_bufs):
        self.S.final_wait("sp", out_bufs)
        self.S.final_wait("pool", out_bufs)
        self.S.emit()
        self.es.close()
        self.cm.__exit__(None, None, None)


    LAYER_W = (("w_in", [D, IN_COLS]), ("mla_q_norm", [256]), ("mla_w_uq", [256, 384]), ("mla_kv_norm", [128]),
               ("mla_w_ukv", [128, 512]), ("swa_sink", [4]), ("w_branch", [4 * 256, D]), ("w_out", [D, D]),
               ("ln1_g", [D]), ("ln1_b", [D]), ("w_ff1", [D, 4 * D]), ("w_ff2", [4 * D, D]), ("w_ple", [256, D]),
               ("w_ple_gate", [D, D]), ("ln2_g", [D]), ("ln2_b", [D]), ("p", [T, 256]), ("tabC", [4 * 33, 128, 128]))

    def declare_F(self):
        e = self.ext_in
        self.x = e("x", [T, D]); self.ident_f = e("ident", [128, 128])
        self.ln_emb_g = e("ln_emb_g", [D]); self.ln_emb_b = e("ln_emb_b", [D])
        self.ropeq = e("ropeq", [2, 32, T]); self.ropek = e("ropek", [2, 32, SEQ])
        self.tabB = e("tabB", [4 * 9, 128, 128])
        self.tabD = [e("tabD%d" % g, [4 * 6, 128, 128]) for g in range(3)]
        self.LW = [{n: e("%s_%d" % (n, l), sh) for n, sh in self.LAYER_W} for l in range(NL)]
        self.out = self.ext_out("out", [T, D])
        s = self.scratch
        self.hA_l = [s("hA%d" % l, [T, D], F32) for l in range(NL)]
        self.hT_l = [s("hT%d" % l, [D, T], BF16) for l in range(NL)]
        self.G_l = [s("G%d" % l, [8, 256, T], BF16) for l in range(NL)]
        self.hTd = s("hTd", [D, T], BF16)
        self.h1T = s("h1T", [D, T], BF16); self.r2 = s("r2", [T, D], F32)
        self.yT = [s("yT%d" % n, [256, T], BF16) for n in range(4)]
        self.Og = {n: s("Og" + n, [T, 260], F32) for n in ("A", "B", "C", "D0", "D1", "D2")}
        self.vS = s("vS", [T + 2048, 260], BF16)
        self.hA, self.hT = self.hA_l[0], self.hT_l[0]
        self.bufs["hA"] = self.bufs["hA0"]; self.bufs["hT"] = self.bufs["hT0"]
        self.init_sems()

    def set_layer(self, l):
        W = self.LW[l]
        for n, _ in self.LAYER_W:
            setattr(self, n, W[n])
            self.bufs[n] = self.bufs["%s_%d" % (n, l)]
        self.hA, self.hTo, self.G = self.hA_l[l], self.hT_l[l], self.G_l[l]
        self.bufs["hA"] = self.bufs["hA%d" % l]; self.bufs["hTo"] = self.bufs["hT%d" % l]; self.bufs["G"] = self.bufs["G%d" % l]
        if l < NL - 1:
            self.hA_o, self.hT_o = self.hA_l[l + 1], self.hT_l[l + 1]
            self.bufs["hA_out"] = self.bufs["hA%d" % (l + 1)]; self.bufs["hT_out"] = self.bufs["hT%d" % (l + 1)]
        else:
            self.hA_o, self.hT_o = self.out, self.hTd
            self.bufs["hA_out"] = self.bufs["out"]; self.bufs["hT_out"] = self.bufs["hTd"]
        self.hTo_v = self.hTo.ap().rearrange("(kc p) t -> p kc t", p=128)
        self.G_v = [self.G[:, r * 128:(r + 1) * 128, :].rearrange("kc p t -> p kc t") for r in range(2)]

    def allgather(self, l):
        S = self.S
        hT, G = self.hT_l[l], self.G_l[l]
        for j in range(8):
            S.collective(lambda e, j=j: e.collective_compute("AllGather", ALU.bypass,
                                                             replica_groups=[[0, 1], [2, 3], [4, 5], [6, 7]],
                                                             ins=[hT[j * 128:(j + 1) * 128, :].opt()], outs=[G[j].opt()]),
                         self.cc_sems[l], j + 1, reads=[self.B("hT%d" % l)], writes=[self.B("G%d" % l)])
        S.op("pool", lambda e: e.nop(), reads=[self.B("G%d" % l)])
        S.emit()

    def init_sems(self):
        with self.nc.Block() as block:
            @block.gpsimd
            def _(g):
                for sm in self.S.all_sems:
                    g.sem_clear(sm)

    def declare_A(self):
        e = self.ext_in
        self.x = e("x", [T, D]); self.ident_f = e("ident", [128, 128])
        self.ln_emb_g = e("ln_emb_g", [D]); self.ln_emb_b = e("ln_emb_b", [D])
        self.hA = self.ext_out("hA_out", [T, D]); self.hT = self.ext_out("hT_out", [D, T], BF16)
        self.bufs["hA"] = self.bufs["hA_out"]; self.bufs["hT"] = self.bufs["hT_out"]
        self.init_sems()

    def declare_B(self):
        e = self.ext_in
        self.hA = e("hA", [T, D]); self.hTo = e("hTo", [D, T], BF16); self.G = e("G", [2 * D, T], BF16)
        self.p = e("p", [T, 256]); self.ident_f = e("ident", [128, 128])
        self.w_in = e("w_in", [D, IN_COLS])
        self.mla_q_norm = e("mla_q_norm", [256]); self.mla_w_uq = e("mla_w_uq", [256, 384])
        self.mla_kv_norm = e("mla_kv_norm", [128]); self.mla_w_ukv = e("mla_w_ukv", [128, 512])
        self.swa_sink = e("swa_sink", [4])
        self.w_branch = e("w_branch", [4 * 256, D]); self.w_out = e("w_out", [D, D])
        self.ln1_g = e("ln1_g", [D]); self.ln1_b = e("ln1_b", [D])
        self.w_ff1 = e("w_ff1", [D, 4 * D]); self.w_ff2 = e("w_ff2", [4 * D, D])
        self.w_ple = e("w_ple", [256, D]); self.w_ple_gate = e("w_ple_gate", [D, D])
        self.ln2_g = e("ln2_g", [D]); self.ln2_b = e("ln2_b", [D])
        self.ropeq = e("ropeq", [2, 32, T]); self.ropek = e("ropek", [2, 32, SEQ])
        self.tabB = e("tabB", [4 * 9, 128, 128]); self.tabC = e("tabC", [4 * 33, 128, 128])
        self.tabD = [e("tabD%d" % g, [4 * 6, 128, 128]) for g in range(3)]
        self.hA_o = self.ext_out("hA_out", [T, D]); self.hT_o = self.ext_out("hT_out", [D, T], BF16)
        s = self.scratch
        self.h1T = s("h1T", [D, T], BF16); self.r2 = s("r2", [T, D], F32)
        self.yT = [s("yT%d" % n, [256, T], BF16) for n in range(4)]
        self.Og = {n: s("Og" + n, [T, 260], F32) for n in ("A", "B", "C", "D0", "D1", "D2")}
        self.vS = s("vS", [T + 2048, 260], BF16)
        self.hTo_v = self.hTo.ap().rearrange("(kc p) t -> p kc t", p=128)
        self.G_v = [self.G[r * D:(r + 1) * D, :].rearrange("(kc p) t -> p kc t", p=128) for r in range(2)]
        self.init_sems()

    def wload(self, dst, src, c0=0, eng="pool", wb=None):
        n = src.shape[1]
        kc = src.shape[0] // 128
        v = src.rearrange("(kc p) n -> p kc n", p=128)
        for a in range(0, n, 2048):
            b = min(n, a + 2048)
            self.S.dma(eng, dst[:, 0:kc, c0 + a:c0 + b], v[:, :, a:b], writes=[wb])

    def hT_src(self, e0, n, H):
        t0 = e0 - H
        if t0 < 0:
            return self.G_v[0][:, :, T + t0:T + t0 + n], self.B("G")
        if t0 >= T:
            return self.G_v[1][:, :, t0 - T:t0 - T + n], self.B("G")
        return self.hTo_v[:, :, t0:t0 + n], self.B("hTo")

    def phase_banded(self, tag, qcols, kcols, vcols, nkv, r, halo_f, blocks_fn, nslots, tab, og, dup_k):
        nc, S = self.nc, self.S
        H = r * halo_f
        E = T + 2 * H
        L = T // r
        nb = L // 128
        nkb = (L + 2 * halo_f) // 128
        with ExitStack() as es:
            sb = lambda n, sh, dt: es.enter_context(self.sbt(tag + n, sh, dt))
            wq = sb("wq", [128, 8, 256], BF16); wk = sb("wk", [128, 8, 256], BF16); wv = sb("wv", [128, 8, 256], BF16)
            w_b = Buf()
            self.wload(wq, self.w_in[:, qcols:qcols + 256], wb=w_b)
            if dup_k:
                for j in range(2):
                    for d in range(2):
                        self.wload(wk, self.w_in[:, kcols + j * 64:kcols + j * 64 + 64], c0=j * 128 + d * 64, wb=w_b)
            else:
                self.wload(wk, self.w_in[:, kcols:kcols + 256], wb=w_b)
            self.wload(wv, self.w_in[:, vcols:vcols + nkv * 64], wb=w_b)
            qT = [sb("qT%d" % m, [128, T], BF16) for m in range(2)]
            kT = [sb("kT%d" % m, [128, E], BF16) for m in range(2)]
            qk_b = Buf()
            tabs = sb("tab", [128, 4 * nslots, 128], F32); tab_b = Buf()
            for h in range(4):
                S.dma("sp", tabs[:, h * nslots:(h + 1) * nslots, :],
                      tab[h * nslots:(h + 1) * nslots, :, :].rearrange("s k q -> k s q"), writes=[tab_b])
            hc = [sb("hc%d" % i, [128, 8, 512], BF16) for i in range(2)]; hc_b = [Buf() for _ in range(2)]
            vst = [sb("vst%d" % i, [128, 4, nkv, 65], BF16) for i in range(2)]
            vst_b = [[Buf() for s_ in range(4)] for _ in range(2)]
            for i in range(2):
                S.op("pool", lambda e, i=i: e.memset(vst[i][:, :, :, :], 1.0), writes=vst_b[i])
            es2 = ExitStack()
            pp = [es2.enter_context(self.pst(tag + "pp%d" % i, [128, 512], F32)) for i in range(2)]
            pp_b = [Buf() for _ in range(2)]
            ppi = [0]

            def proj_fm(wt, c0, hci, n, dst, d0):
                i = ppi[0] % 2; ppi[0] += 1
                def f(e):
                    for kc in range(8):
                        ins = e.matmul(pp[i][:, 0:n], lhsT=wt[:, kc, c0:c0 + 128], rhs=hc[hci][:, kc, 0:n],
                                       start=(kc == 0), stop=(kc == 7))
                    return ins
                S.op("pe", f, reads=[w_b, hc_b[hci]], writes=[pp_b[i]])
                S.op("act", lambda e: e.activation(out=dst[:, d0:d0 + n], in_=pp[i][:, 0:n], func=AF.Copy),
                     reads=[pp_b[i]], writes=[qk_b])

            chunks = []
            e0 = 0
            while e0 < E:
                t0 = e0 - H
                if t0 < 0:
                    n = min(512, -t0)
                elif t0 >= T:
                    n = min(512, E - e0)
                else:
                    n = min(512, T - t0)
                chunks.append((e0, n)); e0 += n
            vcnt = 0
            for ci, (e0, n) in enumerate(chunks):
                hci = ci % 2
                src, src_b = self.hT_src(e0, n, H)
                S.dma("sp", hc[hci][:, :, 0:n], src, reads=[src_b], writes=[hc_b[hci]])
                own = 0 <= e0 - H < T
                for m in range(2):
                    proj_fm(wk, m * 128, hci, n, kT[m], e0)
                    if own:
                        proj_fm(wq, m * 128, hci, n, qT[m], e0 - H)
                vi = ci % 2
                nsub = (n + 127) // 128
                for si, s0 in enumerate(range(0, n, 128)):
                    ns = min(128, n - s0)
                    i = ppi[0] % 2; ppi[0] += 1
                    def f(e, i=i, s0=s0, ns=ns, hci=hci):
                        for kc in range(8):
                            ins = e.matmul(pp[i][0:ns, 0:nkv * 64], lhsT=hc[hci][:, kc, s0:s0 + ns],
                                           rhs=wv[:, kc, 0:nkv * 64], start=(kc == 0), stop=(kc == 7))
                        return ins
                    S.op("pe", f, reads=[w_b, hc_b[hci]], writes=[pp_b[i]])
                    S.op("dve", lambda e, i=i, vi=vi, ns=ns, si=si: e.tensor_copy(
                        out=vst[vi][0:ns, si, :, 0:64], in_=pp[i][0:ns, 0:nkv * 64].rearrange("p (h d) -> p h d", d=64)),
                        reads=[pp_b[i]], writes=[vst_b[vi][si]])
                if n % 128 == 0:
                    S.dma("sp", self.vS[e0:e0 + n, 0:nkv * 65].rearrange("(s p) c -> p s c", p=128),
                          vst[vi][:, 0:nsub, :, :].rearrange("p s h d -> p s (h d)"),
                          reads=vst_b[vi][0:nsub], writes=[self.B("vS")])
                else:
                    S.dma("sp", self.vS[e0:e0 + n, 0:nkv * 65], vst[vi][0:n, 0, :, :].rearrange("p h d -> p (h d)"),
                          reads=vst_b[vi][0:1], writes=[self.B("vS")])
            vx = sb("vx", [128, r * nkb, nkv * 65], BF16); vx_b = Buf()
            for rho in range(r):
                for m in range(nkb):
                    row0 = H + rho + r * (128 * m - halo_f)
                    srcv = self.vS[row0:row0 + 127 * r + 1:r, 0:nkv * 65] if r > 1 else self.vS[row0:row0 + 128, 0:nkv * 65]
                    S.dma("sp", vx[:, rho * nkb + m, :], srcv, reads=[self.B("vS")], writes=[vx_b])
            S.emit()
            es2.close()
            nkmax = max(len(blocks_fn(b, nb)[0]) for b in range(nb))
            NSET = 2 if nkmax <= 4 else 1
            ps = [[es.enter_context(self.pst(tag + "ps%d_%d" % (a, j), [128, 512 if nkmax <= 4 else 1024], F32)) for j in range(2)] for a in range(NSET)]
            ps_b = [[Buf() for j in range(2)] for a in range(NSET)]
            pob = [[es.enter_context(self.pst(tag + "po%d_%d" % (a, j), [128, 512], F32)) for j in range(2)] for a in range(2)]
            po_b = [[Buf() for j in range(2)] for a in range(2)]
            sbt = [[sb("sbt%d_%d" % (a, j), [128, nkmax * 128], F32) for j in range(2)] for a in range(2)]
            sbt_b = [[Buf() for j in range(2)] for a in range(2)]
            et = [[sb("et%d_%d" % (a, j), [128, nkmax * 128], BF16) for j in range(2)] for a in range(2)]
            et_b = [[Buf() for j in range(2)] for a in range(2)]
            ost = [sb("ost%d" % i, [128, 4, 65], F32) for i in range(2)]
            ost_b = [[Buf() for h in range(4)] for i in range(2)]
            units = [(rho, b, hp) for rho in range(r) for b in range(nb) for hp in range(2)]

            def stage_f(u):
                rho, b, hp = units[u]
                a = u % NSET
                ms, slot0 = blocks_fn(b, nb)
                q0 = rho + r * 128 * b
                for j in range(2):
                    h = 2 * hp + j
                    mq = h // 2
                    pr = slice((h % 2) * 64, (h % 2) * 64 + 64)
                    qap = qT[mq][pr, q0:q0 + 127 * r + 1:r] if r > 1 else qT[mq][pr, q0:q0 + 128]
                    def f(e, a=a, j=j, ms=ms, mq=mq, pr=pr, qap=qap, rho=rho):
                        for x, m in enumerate(ms):
                            k0 = H + rho + r * (128 * m - halo_f)
                            kap = kT[mq][pr, k0:k0 + 127 * r + 1:r] if r > 1 else kT[mq][pr, k0:k0 + 128]
                            ins = e.matmul(ps[a][j][:, x * 128:(x + 1) * 128], lhsT=kap, rhs=qap, start=True, stop=True)
                        return ins
                    S.op("pe", f, reads=[qk_b], writes=[ps_b[a][j]])

            def stage_m(u):
                rho, b, hp = units[u]
                a = u % NSET
                a2 = u % 2
                ms, slot0 = blocks_fn(b, nb)
                nk = len(ms)
                oi = (u // 2) % 2
                q0 = rho + r * 128 * b
                for j in range(2):
                    h = 2 * hp + j
                    bias = tabs[:, h * nslots + slot0:h * nslots + slot0 + nk, :].rearrange("k s q -> k (s q)")
                    S.op("dve", lambda e, a=a, a2=a2, j=j, nk=nk, bias=bias: e.scalar_tensor_tensor(
                        out=sbt[a2][j][:, 0:nk * 128], in0=ps[a][j][:, 0:nk * 128], scalar=0.125, in1=bias,
                        op0=ALU.mult, op1=ALU.add), reads=[ps_b[a][j], tab_b], writes=[sbt_b[a2][j]])
                for j in range(2):
                    S.op("act", lambda e, a2=a2, j=j, nk=nk: e.activation(out=et[a2][j][:, 0:nk * 128], in_=sbt[a2][j][:, 0:nk * 128], func=AF.Exp),
                         reads=[sbt_b[a2][j]], writes=[et_b[a2][j]])
                for j in range(2):
                    h = 2 * hp + j
                    hv = (h // 2) if dup_k else h
                    def g(e, a2=a2, j=j, ms=ms, hv=hv, rho=rho, nk=nk):
                        for x, m in enumerate(ms):
                            ins = e.matmul(pob[a2][j][:, 0:65], lhsT=et[a2][j][:, x * 128:(x + 1) * 128],
                                           rhs=vx[:, rho * nkb + m, hv * 65:(hv + 1) * 65], start=(x == 0), stop=(x == nk - 1))
                        return ins
                    S.op("pe", g, reads=[et_b[a2][j], vx_b], writes=[po_b[a2][j]])
                for j in range(2):
                    h = 2 * hp + j
                    S.op("act", lambda e, a2=a2, j=j, h=h, oi=oi: e.activation(out=ost[oi][:, h, :], in_=pob[a2][j][:, 0:65], func=AF.Copy),
                         reads=[po_b[a2][j]], writes=[ost_b[oi][h]])
                if hp == 1:
                    dst = og[q0:q0 + 127 * r + 1:r, :] if r > 1 else og[q0:q0 + 128, :]
                    S.dma("sp", dst, ost[oi][:, :, :].rearrange("p h d -> p (h d)"), reads=ost_b[oi], writes=[self.B(og.name)])

            if NSET == 2:
                stage_f(0)
            for u in range(len(units)):
                if NSET == 2:
                    if u + 1 < len(units):
                        stage_f(u + 1)
                else:
                    stage_f(u)
                stage_m(u)
            S.emit()

    def phase_combine(self, tag, ogs, yT, sink=False):
        nc, S = self.nc, self.S
        ng = len(ogs)
        with ExitStack() as es:
            sb = lambda n, sh, dt: es.enter_context(self.sbt(tag + n, sh, dt))
            identf = sb("idf", [128, 128], F32); ident = sb("idb", [128, 128], BF16); id_b = Buf()
            S.dma("sp", identf[:, :], self.ident_f[:, :], writes=[id_b])
            S.op("dve", lambda e: e.tensor_copy(out=ident[:, :], in_=identf[:, :]), reads=[id_b], writes=[id_b])
            esk = sb("esk", [128, 4], F32); esk_b = Buf()
            if sink:
                S.dma("sp", esk[:, :], self.swa_sink.ap().partition_broadcast(128), writes=[esk_b])
                S.op("act", lambda e: e.activation(out=esk[:, :], in_=esk[:, :], func=AF.Exp), reads=[esk_b], writes=[esk_b])
            NT4 = 4
            ot = [[sb("ot%d_%d" % (i, g), [128, 4, 65], F32) for g in range(ng)] for i in range(NT4)]
            ot_b = [[Buf() for g in range(ng)] for i in range(NT4)]
            den = [sb("den%d" % i, [128, 4], F32) for i in range(NT4)]; den_b = [Buf() for _ in range(NT4)]
            y = [sb("y%d" % i, [128, 256], BF16) for i in range(NT4)]; y_b = [[Buf() for h in range(4)] for _ in range(NT4)]
            psT = [es.enter_context(self.pst(tag + "psT%d" % i, [128, 8, 128], BF16)) for i in range(NT4)]
            psT_b = [Buf() for _ in range(NT4)]
            yts = [sb("yts%d" % i, [128, 2, 512], BF16) for i in range(2)]; yts_b = [[Buf() for k in range(4)] for _ in range(2)]
            yTv = yT.ap().rearrange("(kc p) t -> p kc t", p=128)
            for grp in range(T // 512):
                j = grp % 2
                tts = [grp * 4 + k for k in range(4)]
                for k, tt in enumerate(tts):
                    for g in range(ng):
                        S.dma("sp", ot[k][g][:, :, :].rearrange("p h d -> p (h d)"), ogs[g][tt * 128:(tt + 1) * 128, :],
                              reads=[self.B(ogs[g].name)], writes=[ot_b[k][g]])
                for g in range(1, ng):
                    for k in range(4):
                        S.op("dve", lambda e, k=k, g=g: e.tensor_tensor(out=ot[k][0][:, :, :], in0=ot[k][0][:, :, :], in1=ot[k][g][:, :, :], op=ALU.add),
                             reads=[ot_b[k][0], ot_b[k][g]], writes=[ot_b[k][0]])
                if sink:
                    for k in range(4):
                        S.op("dve", lambda e, k=k: e.tensor_tensor(out=den[k][:, :], in0=ot[k][0][:, :, 64], in1=esk[:, :], op=ALU.add),
                             reads=[ot_b[k][0], esk_b], writes=[den_b[k]])
                    for k in range(4):
                        S.op("dve", lambda e, k=k: e.reciprocal(out=den[k][:, :], in_=den[k][:, :]), reads=[den_b[k]], writes=[den_b[k]])
                else:
                    for k in range(4):
                        S.op("dve", lambda e, k=k: e.reciprocal(out=den[k][:, :], in_=ot[k][0][:, :, 64]), reads=[ot_b[k][0]], writes=[den_b[k]])
                for h in range(4):
                    for k in range(4):
                        S.op("dve", lambda e, k=k, h=h: e.tensor_scalar(out=y[k][:, h * 64:(h + 1) * 64], in0=ot[k][0][:, h, 0:64],
                                                                       scalar1=den[k][:, h:h + 1], scalar2=None, op0=ALU.mult),
                             reads=[ot_b[k][0], den_b[k]], writes=[y_b[k][h]])
                for k in range(4):
                    def f_tr(e, k=k):
                        e.transpose(out=psT[k][:, 0, :], in_=y[k][:, 0:128], identity=ident[:, :])
                        return e.transpose(out=psT[k][:, 1, :], in_=y[k][:, 128:256], identity=ident[:, :])
                    S.op("pe", f_tr, reads=y_b[k] + [id_b], writes=[psT_b[k]])
                for k in range(4):
                    S.op("dve", lambda e, k=k, j=j: e.tensor_copy(out=yts[j][:, 0:2, k * 128:(k + 1) * 128], in_=psT[k][:, 0:2, :]),
                         reads=[psT_b[k]], writes=[yts_b[j][k]])
                c0 = grp * 512
                S.dma("pool", yTv[:, :, c0:c0 + 512], yts[j][:, :, :], reads=yts_b[j], writes=[self.B(yT.name)])
            S.emit()

    def phase_mla(self):
        nc, S = self.nc, self.S
        w_in = self.w_in
        with ExitStack() as es:
            sb = lambda n, sh, dt: es.enter_context(self.sbt("ml_" + n, sh, dt))
            wq = sb("wq", [128, 8, 256], BF16); wkv = sb("wkv", [128, 8, 128], BF16)
            wkr = sb("wkr", [128, 8, 96], BF16); wkrs = sb("wkrs", [128, 8, 96], BF16)
            w_b = Buf()
            self.wload(wq, w_in[:, 0:256], wb=w_b); self.wload(wkv, w_in[:, 256:384], wb=w_b)
            self.wload(wkr, w_in[:, 256:320], c0=0, wb=w_b); self.wload(wkr, w_in[:, 384:416], c0=64, wb=w_b)
            self.wload(wkrs, w_in[:, 256:320], c0=0, wb=w_b); self.wload(wkrs, w_in[:, 400:416], c0=64, wb=w_b)
            self.wload(wkrs, w_in[:, 384:400], c0=80, wb=w_b)
            uqf = sb("uqf", [128, 2, 384], F32); gq = sb("gq", [128, 2], F32)
            uq = sb("uq", [128, 2, 384], BF16); uqs = sb("uqs", [128, 2, 384], BF16)
            ukvf = sb("ukvf", [128, 512], F32); gkv = sb("gkv", [128, 1], F32)
            ukv = sb("ukv", [128, 512], BF16); wv4 = sb("wv4", [128, 256], BF16)
            u_b = Buf()
            S.dma("sp", uqf[:, :, :], self.mla_w_uq.ap().rearrange("(kc p) n -> p kc n", p=128), writes=[u_b])
            for kc in range(2):
                S.dma("sp", gq[:, kc:kc + 1], self.mla_q_norm[kc * 128:(kc + 1) * 128].rearrange("(p o) -> p o", o=1), writes=[u_b])
            S.dma("sp", ukvf[:, :], self.mla_w_ukv[:, :], writes=[u_b])
            S.dma("sp", gkv[:, :], self.mla_kv_norm.ap().rearrange("(p o) -> p o", o=1), writes=[u_b])
            for kc in range(2):
                S.op("dve", lambda e, kc=kc: e.tensor_scalar(out=uq[:, kc, :], in0=uqf[:, kc, :], scalar1=gq[:, kc:kc + 1],
                                                              scalar2=None, op0=ALU.mult), reads=[u_b], writes=[u_b])
            S.op("dve", lambda e: e.tensor_copy(out=uqs[:, :, :], in_=uq[:, :, :]), reads=[u_b], writes=[u_b])
            for h in range(4):
                S.op("dve", lambda e, h=h: e.tensor_copy(out=uqs[:, :, h * 96 + 64:h * 96 + 80], in_=uq[:, :, h * 96 + 80:h * 96 + 96]),
                     reads=[u_b], writes=[u_b])
                S.op("dve", lambda e, h=h: e.tensor_copy(out=uqs[:, :, h * 96 + 80:h * 96 + 96], in_=uq[:, :, h * 96 + 64:h * 96 + 80]),
                     reads=[u_b], writes=[u_b])
            S.op("dve", lambda e: e.tensor_scalar(out=ukv[:, :], in0=ukvf[:, :], scalar1=gkv[:, 0:1], scalar2=None, op0=ALU.mult),
                 reads=[u_b], writes=[u_b])
            for h in range(4):
                S.op("dve", lambda e, h=h: e.tensor_copy(out=wv4[:, h * 64:(h + 1) * 64], in_=ukv[:, h * 128 + 64:h * 128 + 128]),
                     reads=[u_b], writes=[u_b])
            identf = sb("idf", [128, 128], F32); ident = sb("idb", [128, 128], BF16); id_b = Buf()
            S.dma("sp", identf[:, :], self.ident_f[:, :], writes=[id_b])
            S.op("dve", lambda e: e.tensor_copy(out=ident[:, :], in_=identf[:, :]), reads=[id_b], writes=[id_b])
            eps6 = sb("eps6", [128, 1], F32); eps_b = Buf()
            S.op("dve", lambda e: e.memset(eps6[:, :], 1e-6), writes=[eps_b])
            kT = [sb("kT%d" % h, [96, SEQ], BF16) for h in range(4)]; kT_b = Buf()
            qT = [sb("qT%d" % h, [96, T], BF16) for h in range(4)]; qT_b = Buf()
            vx = sb("vx", [128, SEQ // 128, 260], BF16); vx_b = Buf()
            S.op("pool", lambda e: e.memset(vx[:, :, :], 1.0), writes=[vx_b])
            hc = [sb("hc%d" % i, [128, 8, 512], BF16) for i in range(2)]; hc_b = [Buf() for _ in range(2)]
            rt = [sb("rt%d" % i, [96, 2, 512], F32) for i in range(2)]; rt_b = [Buf() for _ in range(2)]
            junk = sb("junk", [128, 256], F32); junk_b = Buf()
            ss = [sb("ss%d" % i, [128, 3], F32) for i in range(2)]; ss_b = [Buf() for _ in range(2)]
            cn = [sb("cn%d" % i, [128, 256], BF16) for i in range(2)]; cn_b = [Buf() for _ in range(2)]
            cnT = [sb("cnT%d" % i, [128, 2, 512], BF16) for i in range(2)]; cnT_b = [Buf() for _ in range(2)]
            t1 = [sb("t1%d" % i, [96, 512], F32) for i in range(2)]; t1_b = [Buf() for _ in range(2)]
            t2 = [sb("t2%d" % i, [96, 512], F32) for i in range(2)]; t2_b = [Buf() for _ in range(2)]
            with ExitStack() as es2:
                NP = 6
                pp = [es2.enter_context(self.pst("ml_pp%d" % i, [128, 512], F32)) for i in range(NP)]
                pp_b = [Buf() for _ in range(NP)]
                psT = [es2.enter_context(self.pst("ml_psT%d" % i, [128, 2, 128], BF16)) for i in range(2)]
                psT_b = [Buf() for _ in range(2)]
                ctr = [0]
                def nextp():
                    i = ctr[0] % NP; ctr[0] += 1
                    return i

                def norm_tile(hci, s, wt, ncol, nfeat, si):
                    i = nextp()
                    def f(e):
                        for kc in range(8):
                            ins = e.matmul(pp[i][:, 0:ncol], lhsT=hc[hci][:, kc, s * 128:(s + 1) * 128], rhs=wt[:, kc, 0:ncol],
                                           start=(kc == 0), stop=(kc == 7))
                        return ins
                    S.op("pe", f, reads=[w_b, hc_b[hci]], writes=[pp_b[i]])
                    S.op("act", lambda e: e.activation(out=junk[:, 0:ncol], in_=pp[i][:, 0:ncol], func=AF.Square, accum_out=ss[si][:, 0:1]),
                         reads=[pp_b[i]], writes=[junk_b, ss_b[si]])
                    S.op("act", lambda e: e.activation(out=ss[si][:, 1:2], in_=ss[si][:, 0:1], func=AF.Sqrt, bias=eps6[:, 0:1], scale=1.0 / nfeat),
                         reads=[ss_b[si], eps_b], writes=[ss_b[si]])
                    S.op("dve", lambda e: e.reciprocal(out=ss[si][:, 2:3], in_=ss[si][:, 1:2]), reads=[ss_b[si]], writes=[ss_b[si]])
                    S.op("dve", lambda e: e.tensor_scalar(out=cn[si][:, 0:ncol], in0=pp[i][:, 0:ncol], scalar1=ss[si][:, 2:3], scalar2=None, op0=ALU.mult),
                         reads=[pp_b[i], ss_b[si]], writes=[cn_b[si]])

                def rope_rows(pa, pb, ri, dsts, dst_b, c0, ti):
                    S.op("dve", lambda e: e.tensor_tensor(out=t1[ti][64:96, :], in0=pp[pa][64:96, :], in1=rt[ri][64:96, 0, :], op=ALU.mult),
                         reads=[pp_b[pa], rt_b[ri]], writes=[t1_b[ti]])
                    S.op("dve", lambda e: e.tensor_tensor(out=t2[ti][64:96, :], in0=pp[pb][64:96, :], in1=rt[ri][64:96, 1, :], op=ALU.mult),
                         reads=[pp_b[pb], rt_b[ri]], writes=[t2_b[ti]])
                    for d in dsts:
                        S.op("dve", lambda e, d=d: e.tensor_tensor(out=d[64:96, c0:c0 + 512], in0=t1[ti][64:96, :], in1=t2[ti][64:96, :], op=ALU.add),
                             reads=[t1_b[ti], t2_b[ti]], writes=[dst_b])

                for c in range(SEQ // 512):
                    hci = c % 2
                    S.dma("sp", hc[hci][:, :, :], self.G_v[c // 8][:, :, (c % 8) * 512:(c % 8) * 512 + 512], reads=[self.B("G")], writes=[hc_b[hci]])
                    S.dma("sp", rt[hci][64:96, :, :], self.ropek[:, :, c * 512:(c + 1) * 512].rearrange("a p t -> p a t"),
                          reads=[self.B("ropek")], writes=[rt_b[hci]])
                    for s in range(4):
                        si = (c * 4 + s) % 2
                        norm_tile(hci, s, wkv, 128, 128.0, si)
                        def ftr(e, si=si):
                            return e.transpose(out=psT[si][:, 0, :], in_=cn[si][:, 0:128], identity=ident[:, :])
                        S.op("pe", ftr, reads=[cn_b[si], id_b], writes=[psT_b[si]])
                        S.op("dve", lambda e, si=si, s=s, hci=hci: e.tensor_copy(out=cnT[hci][:, 0, s * 128:(s + 1) * 128], in_=psT[si][:, 0, :]),
                             reads=[psT_b[si]], writes=[cnT_b[hci]])
                        i = nextp()
                        S.op("pe", lambda e, i=i, s=s, hci=hci: e.matmul(pp[i][:, 0:256], lhsT=cnT[hci][:, 0, s * 128:(s + 1) * 128], rhs=wv4[:, :],
                                                                         start=True, stop=True), reads=[cnT_b[hci], u_b], writes=[pp_b[i]])
                        S.op("dve", lambda e, i=i, blk=c * 4 + s: e.tensor_copy(
                            out=vx[:, blk, :].rearrange("p (h d) -> p h d", d=65)[:, :, 0:64],
                            in_=pp[i][:, 0:256].rearrange("p (h d) -> p h d", d=64)), reads=[pp_b[i]], writes=[vx_b])
                    for h in range(4):
                        i = nextp()
                        S.op("pe", lambda e, i=i, h=h, hci=hci: e.matmul(pp[i][0:64, :], lhsT=ukv[:, h * 128:h * 128 + 64], rhs=cnT[hci][:, 0, :],
                                                                         start=True, stop=True), reads=[cnT_b[hci], u_b], writes=[pp_b[i]])
                        S.op("act", lambda e, i=i, h=h, c=c: e.activation(out=kT[h][0:64, c * 512:(c + 1) * 512], in_=pp[i][0:64, :], func=AF.Copy),
                             reads=[pp_b[i]], writes=[kT_b])
                    pa, pb = nextp(), nextp()
                    for (pi, wt) in ((pa, wkr), (pb, wkrs)):
                        def f(e, pi=pi, wt=wt, hci=hci):
                            for kc in range(8):
                                ins = e.matmul(pp[pi][0:96, :], lhsT=wt[:, kc, :], rhs=hc[hci][:, kc, :], start=(kc == 0), stop=(kc == 7))
                            return ins
                        S.op("pe", f, reads=[w_b, hc_b[hci]], writes=[pp_b[pi]])
                    rope_rows(pa, pb, hci, kT, kT_b, c * 512, hci)
                for c in range(T // 512):
                    hci = c % 2
                    S.dma("sp", hc[hci][:, :, :], self.hTo_v[:, :, c * 512:(c + 1) * 512], reads=[self.B("hTo")], writes=[hc_b[hci]])
                    S.dma("sp", rt[hci][64:96, :, :], self.ropeq[:, :, c * 512:(c + 1) * 512].rearrange("a p t -> p a t"),
                          reads=[self.B("ropeq")], writes=[rt_b[hci]])
                    for s in range(4):
                        si = (c * 4 + s) % 2
                        norm_tile(hci, s, wq, 256, 256.0, si)
                        def ftr(e, si=si):
                            e.transpose(out=psT[si][:, 0, :], in_=cn[si][:, 0:128], identity=ident[:, :])
                            return e.transpose(out=psT[si][:, 1, :], in_=cn[si][:, 128:256], identity=ident[:, :])
                        S.op("pe", ftr, reads=[cn_b[si], id_b], writes=[psT_b[si]])
                        S.op("dve", lambda e, si=si, s=s, hci=hci: e.tensor_copy(out=cnT[hci][:, :, s * 128:(s + 1) * 128], in_=psT[si][:, :, :]),
                             reads=[psT_b[si]], writes=[cnT_b[hci]])
                    for h in range(4):
                        pa, pb = nextp(), nextp()
                        for (pi, wt) in ((pa, uq), (pb, uqs)):
                            def f(e, pi=pi, wt=wt, hci=hci, h=h):
                                for kc in range(2):
                                    ins = e.matmul(pp[pi][0:96, :], lhsT=wt[:, kc, h * 96:(h + 1) * 96], rhs=cnT[hci][:, kc, :],
                                                   start=(kc == 0), stop=(kc == 1))
                                return ins
                            S.op("pe", f, reads=[u_b, cnT_b[hci]], writes=[pp_b[pi]])
                        S.op("act", lambda e, pa=pa, h=h, c=c: e.activation(out=qT[h][0:64, c * 512:(c + 1) * 512], in_=pp[pa][0:64, :], func=AF.Copy),
                             reads=[pp_b[pa]], writes=[qT_b])
                        rope_rows(pa, pb, hci, [qT[h]], qT_b, c * 512, h % 2)
                S.emit()
            NB3 = 3
            ps = [es.enter_context(self.pst("ml_ps%d" % i, [128, 512], F32)) for i in range(NB3)]; ps_b = [Buf() for _ in range(NB3)]
            po = [es.enter_context(self.pst("ml_po%d" % i, [128, 512], F32)) for i in range(4)]; po_b = [Buf() for _ in range(4)]
            et = [sb("et%d" % i, [128, 512], BF16) for i in range(NB3)]; et_b = [Buf() for _ in range(NB3)]
            ost = [[sb("ost%d_%d" % (i, q), [128, 4, 65], F32) for q in range(4)] for i in range(2)]
            ost_b = [[[Buf() for h in range(4)] for q in range(4)] for i in range(2)]
            sc = 96.0 ** -0.5
            NKB = SEQ // 128
            its = [(qc, h, kb) for qc in range(T // 512) for h in range(4) for kb in range(NKB)]

            def emit_st(n):
                qc, h, kb = its[n]
                i = n % NB3
                S.op("pe", lambda e: e.matmul(ps[i][:, :], lhsT=kT[h][0:96, kb * 128:(kb + 1) * 128],
                                              rhs=qT[h][0:96, qc * 512:(qc + 1) * 512], start=True, stop=True),
                     reads=[kT_b, qT_b], writes=[ps_b[i]])

            emit_st(0); emit_st(1)
            for n, (qc, h, kb) in enumerate(its):
                i = n % NB3
                oi = qc % 2
                if n + 2 < len(its):
                    emit_st(n + 2)
                S.op("act", lambda e, i=i: e.activation(out=et[i][:, :], in_=ps[i][:, :], func=AF.Exp, scale=sc),
                     reads=[ps_b[i]], writes=[et_b[i]])
                def g(e, i=i, h=h, kb=kb):
                    for qs in range(4):
                        ins = e.matmul(po[qs][:, 0:65], lhsT=et[i][:, qs * 128:(qs + 1) * 128], rhs=vx[:, kb, h * 65:(h + 1) * 65],
                                       start=(kb == 0), stop=(kb == NKB - 1))
                    return ins
                S.op("pe", g, reads=[et_b[i], vx_b], writes=po_b)
                if kb == NKB - 1:
                    for qs in range(4):
                        S.op("dve", lambda e, qs=qs, h=h, oi=oi: e.tensor_copy(out=ost[oi][qs][:, h, :], in_=po[qs][:, 0:65]),
                             reads=[po_b[qs]], writes=[ost_b[oi][qs][h]])
                    if h == 3:
                        for qs in range(4):
                            r0 = qc * 512 + qs * 128
                            S.dma("sp", self.Og["A"][r0:r0 + 128, :], ost[oi][qs][:, :, :].rearrange("p h d -> p (h d)"),
                                  reads=ost_b[oi][qs], writes=[self.B("OgA")])
            S.emit()

    def phase_merge1(self):
        nc, S = self.nc, self.S
        if not hasattr(self, "mT"):
            self.mT = self.scratch("mT", [D, T], BF16)
        mTv = self.mT.ap().rearrange("(kc p) t -> p kc t", p=128)
        yTv = [y.ap().rearrange("(kc p) t -> p kc t", p=128) for y in self.yT]
        with ExitStack() as es:
            sb = lambda n, sh, dt: es.enter_context(self.sbt("m1_" + n, sh, dt))
            wg = sb("wg", [128, 8, 4096], BF16); wb = sb("wb", [128, 8, 1024], BF16); w_b = Buf()
            self.wload(wg, self.w_in[:, 4000:8096], wb=w_b)
            self.wload(wb, self.w_branch[:, :], wb=w_b)
            hc = [sb("hc%d" % i, [128, 8, 512], BF16) for i in range(2)]; hc_b = [Buf() for _ in range(2)]
            yc = [[sb("yc%d_%d" % (i, n), [128, 2, 512], BF16) for n in range(4)] for i in range(2)]
            yc_b = [[Buf() for n in range(4)] for i in range(2)]
            mt = [sb("mt%d" % i, [128, 8, 512], BF16) for i in range(2)]; mt_b = [Buf() for _ in range(2)]
            gt = [sb("gt%d" % i, [128, 512], F32) for i in range(2)]; gt_b = [Buf() for _ in range(2)]
            acc = [sb("acc%d" % i, [128, 512], F32) for i in range(2)]; acc_b = [Buf() for _ in range(2)]
            tmp = [sb("tmp%d" % i, [128, 512], F32) for i in range(2)]; tmp_b = [Buf() for _ in range(2)]
            pg = [es.enter_context(self.pst("m1_pg%d" % i, [128, 512], F32)) for i in range(2)]; pg_b = [Buf() for _ in range(2)]
            pb = [es.enter_context(self.pst("m1_pb%d" % i, [128, 512], F32)) for i in range(2)]; pb_b = [Buf() for _ in range(2)]
            it = 0
            for c in range(T // 512):
                ci = c % 2
                S.dma("sp", hc[ci][:, :, :], self.hTo_v[:, :, c * 512:(c + 1) * 512], reads=[self.B("hTo")], writes=[hc_b[ci]])
                for n in range(4):
                    S.dma("sp", yc[ci][n][:, :, :], yTv[n][:, :, c * 512:(c + 1) * 512], reads=[self.B(self.yT[n].name)], writes=[yc_b[ci][n]])
                for dc in range(8):
                    ai = (c * 8 + dc) % 2
                    for n in range(4):
                        i = it % 2; it += 1
                        def f(e, i=i, n=n, dc=dc, ci=ci):
                            for kc in range(8):
                                ins = e.matmul(pg[i][:, :], lhsT=wg[:, kc, n * 1024 + dc * 128:n * 1024 + dc * 128 + 128], rhs=hc[ci][:, kc, :],
                                               start=(kc == 0), stop=(kc == 7))
                            return ins
                        S.op("pe", f, reads=[w_b, hc_b[ci]], writes=[pg_b[i]])
                        S.op("act", lambda e, i=i: e.activation(out=gt[i][:, :], in_=pg[i][:, :], func=AF.Sigmoid), reads=[pg_b[i]], writes=[gt_b[i]])
                        def f2(e, i=i, n=n, dc=dc, ci=ci):
                            for kc in range(2):
                                ins = e.matmul(pb[i][:, :], lhsT=wb[:, n * 2 + kc, dc * 128:dc * 128 + 128], rhs=yc[ci][n][:, kc, :],
                                               start=(kc == 0), stop=(kc == 1))
                            return ins
                        S.op("pe", f2, reads=[w_b, yc_b[ci][n]], writes=[pb_b[i]])
                        if n == 0:
                            S.op("dve", lambda e, i=i, ai=ai: e.tensor_tensor(out=acc[ai][:, :], in0=pb[i][:, :], in1=gt[i][:, :], op=ALU.mult),
                                 reads=[pb_b[i], gt_b[i]], writes=[acc_b[ai]])
                        else:
                            S.op("dve", lambda e, i=i: e.tensor_tensor(out=tmp[i][:, :], in0=pb[i][:, :], in1=gt[i][:, :], op=ALU.mult),
                                 reads=[pb_b[i], gt_b[i]], writes=[tmp_b[i]])
                            if n < 3:
                                S.op("pool", lambda e, i=i, ai=ai: e.tensor_tensor(out=acc[ai][:, :], in0=acc[ai][:, :], in1=tmp[i][:, :], op=ALU.add),
                                     reads=[acc_b[ai], tmp_b[i]], writes=[acc_b[ai]])
                            else:
                                S.op("pool", lambda e, i=i, ai=ai, dc=dc, ci=ci: e.tensor_tensor(out=mt[ci][:, dc, :], in0=acc[ai][:, :], in1=tmp[i][:, :], op=ALU.add),
                                     reads=[acc_b[ai], tmp_b[i]], writes=[mt_b[ci]])
                S.dma("pool", mTv[:, :, c * 512:(c + 1) * 512], mt[ci][:, :, :], reads=[mt_b[ci]], writes=[self.B("mT")])
            S.emit()

    def phase_merge2(self):
        nc, S = self.nc, self.S
        mTv = self.mT.ap().rearrange("(kc p) t -> p kc t", p=128)
        h1Tv = self.h1T.ap().rearrange("(kc p) t -> p kc t", p=128)
        with ExitStack() as es:
            sb = lambda n, sh, dt: es.enter_context(self.sbt("m2_" + n, sh, dt))
            c = self.ln_consts(es, self.ln1_g.ap(), self.ln1_b.ap(), "m2_")
            wo = sb("wo", [128, 8, 1024], BF16); wpg = sb("wpg", [128, 8, 1024], BF16); wpl = sb("wpl", [128, 2, 1024], BF16); w_b = Buf()
            self.wload(wo, self.w_out[:, :], wb=w_b); self.wload(wpg, self.w_ple_gate[:, :], wb=w_b); self.wload(wpl, self.w_ple[:, :], wb=w_b)
            mc = [sb("mc%d" % i, [128, 8, 512], BF16) for i in range(2)]; mc_b = [Buf() for _ in range(2)]
            ha = [sb("ha%d" % i, [128, D], F32) for i in range(2)]; ha_b = [Buf() for _ in range(2)]
            r1 = [sb("r1%d" % i, [128, D], F32) for i in range(2)]; r1_b = [Buf() for _ in range(2)]
            h1 = [sb("h1%d" % i, [128, D], F32) for i in range(2)]; h1_b = [Buf() for _ in range(2)]
            hb = [sb("hb%d" % i, [128, D], BF16) for i in range(2)]; hb_b = [Buf() for _ in range(2)]
            hts = [sb("hts%d" % i, [128, 8, 512], BF16) for i in range(2)]; hts_b = [Buf() for _ in range(2)]
            pt = [sb("pt%d" % i, [128, 256], F32) for i in range(2)]; pt_b = [Buf() for _ in range(2)]
            ptb = [sb("ptb%d" % i, [128, 256], BF16) for i in range(2)]; ptb_b = [Buf() for _ in range(2)]
            pT = [sb("pT%d" % i, [128, 2, 128], BF16) for i in range(2)]; pT_b = [Buf() for _ in range(2)]
            sg = [sb("sg%d" % i, [128, D], F32) for i in range(2)]; sg_b = [Buf() for _ in range(2)]
            ps = lambda n, sh, dt: es.enter_context(self.pst("m2_" + n, sh, dt))
            po = [ps("po%d" % i, [128, 512], F32) for i in range(2)]; po_b = [Buf() for _ in range(2)]
            pq = [ps("pq%d" % i, [128, 512], F32) for i in range(2)]; pq_b = [Buf() for _ in range(2)]
            pp = [ps("pp%d" % i, [128, 512], F32) for i in range(2)]; pp_b = [Buf() for _ in range(2)]
            psT = [ps("psT", [128, 8, 128], BF16)]; psT_b = [Buf()]
            psP = [ps("psP", [128, 2, 128], BF16)]; psP_b = [Buf()]
            for tt in range(T // 128):
                i = tt % 2
                cc, s = tt // 4, tt % 4
                ci = cc % 2
                if s == 0:
                    S.dma("sp", mc[ci][:, :, :], mTv[:, :, cc * 512:(cc + 1) * 512], reads=[self.B("mT")], writes=[mc_b[ci]])
                S.dma("sp", ha[i][:, :], self.hA[tt * 128:(tt + 1) * 128, :], reads=[self.B("hA")], writes=[ha_b[i]])
                S.dma("sp", pt[i][:, :], self.p[tt * 128:(tt + 1) * 128, :], reads=[self.B("p")], writes=[pt_b[i]])
                for hf in range(2):
                    def f(e, hf=hf, ci=ci, s=s):
                        for dc in range(8):
                            ins = e.matmul(po[hf][:, :], lhsT=mc[ci][:, dc, s * 128:(s + 1) * 128], rhs=wo[:, dc, hf * 512:(hf + 1) * 512],
                                           start=(dc == 0), stop=(dc == 7))
                        return ins
                    S.op("pe", f, reads=[w_b, mc_b[ci]], writes=[po_b[hf]])
                    S.op("dve", lambda e, hf=hf, i=i: e.scalar_tensor_tensor(out=r1[i][:, hf * 512:(hf + 1) * 512], in0=ha[i][:, hf * 512:(hf + 1) * 512],
                                                                              scalar=ALPHA, in1=po[hf][:, :], op0=ALU.mult, op1=ALU.add),
                         reads=[ha_b[i], po_b[hf]], writes=[r1_b[i]])
                self.ln_ops("m2", r1[i], r1_b[i], h1[i], h1_b[i], c["g"], c["bb"], c["tmp"][i])
                S.op("act", lambda e, i=i: e.activation(out=hb[i][:, :], in_=h1[i][:, :], func=AF.Copy), reads=[h1_b[i]], writes=[hb_b[i]])
                self.transpose_ops(hb[i], hb_b[i], psT[0], psT_b[0], hts[ci], hts_b[ci], s * 128, c["ident"], c["ident_b"])
                S.op("act", lambda e, i=i: e.activation(out=ptb[i][:, :], in_=pt[i][:, :], func=AF.Copy), reads=[pt_b[i]], writes=[ptb_b[i]])
                self.transpose_ops(ptb[i], ptb_b[i], psP[0], psP_b[0], pT[i], pT_b[i], 0, c["ident"], c["ident_b"], nk=2)
                for hf in range(2):
                    def f(e, hf=hf, ci=ci, s=s):
                        for kc in range(8):
                            ins = e.matmul(pq[hf][:, :], lhsT=hts[ci][:, kc, s * 128:(s + 1) * 128], rhs=wpg[:, kc, hf * 512:(hf + 1) * 512],
                                           start=(kc == 0), stop=(kc == 7))
                        return ins
                    S.op("pe", f, reads=[w_b, hts_b[ci]], writes=[pq_b[hf]])
                    S.op("act", lambda e, hf=hf, i=i: e.activation(out=sg[i][:, hf * 512:(hf + 1) * 512], in_=pq[hf][:, :], func=AF.Sigmoid),
                         reads=[pq_b[hf]], writes=[sg_b[i]])
                    def f2(e, hf=hf, i=i):
                        for kc in range(2):
                            ins = e.matmul(pp[hf][:, :], lhsT=pT[i][:, kc, :], rhs=wpl[:, kc, hf * 512:(hf + 1) * 512], start=(kc == 0), stop=(kc == 1))
                        return ins
                    S.op("pe", f2, reads=[w_b, pT_b[i]], writes=[pp_b[hf]])
                    S.op("dve", lambda e, hf=hf, i=i: e.tensor_tensor(out=sg[i][:, hf * 512:(hf + 1) * 512], in0=pp[hf][:, :], in1=sg[i][:, hf * 512:(hf + 1) * 512], op=ALU.mult),
                         reads=[pp_b[hf], sg_b[i]], writes=[sg_b[i]])
                S.op("dve", lambda e, i=i: e.scalar_tensor_tensor(out=sg[i][:, :], in0=h1[i][:, :], scalar=ALPHA, in1=sg[i][:, :], op0=ALU.mult, op1=ALU.add),
                     reads=[h1_b[i], sg_b[i]], writes=[sg_b[i]])
                S.dma("pool", self.r2[tt * 128:(tt + 1) * 128, :], sg[i][:, :], reads=[sg_b[i]], writes=[self.B("r2")])
                if s == 3:
                    S.dma("pool", h1Tv[:, :, cc * 512:(cc + 1) * 512], hts[ci][:, :, :], reads=[hts_b[ci]], writes=[self.B("h1T")])
            S.emit()

    def phase_ffn(self):
        nc, S = self.nc, self.S
        h1Tv = self.h1T.ap().rearrange("(kc p) t -> p kc t", p=128)
        hTov = self.hT_o.ap().rearrange("(kc p) t -> p kc t", p=128)
        with ExitStack() as es:
            sb = lambda n, sh, dt: es.enter_context(self.sbt("ff_" + n, sh, dt))
            c = self.ln_consts(es, self.ln2_g.ap(), self.ln2_b.ap(), "ff_")
            w1 = sb("w1", [128, 8, 4096], BF16); w2 = sb("w2", [128, 32, 1024], BF16); w_b = Buf()
            self.wload(w1, self.w_ff1[:, :], wb=w_b); self.wload(w2, self.w_ff2[:, :], wb=w_b)
            hc = [sb("hc%d" % i, [128, 8, 256], BF16) for i in range(2)]; hc_b = [Buf() for _ in range(2)]
            uT = sb("uT", [128, 32, 256], BF16); uT_b = Buf()
            rl = [sb("rl%d" % i, [128, 256], F32) for i in range(2)]; rl_b = [Buf() for _ in range(2)]
            rt = [sb("rt%d" % i, [128, D], F32) for i in range(2)]; rt_b = [Buf() for _ in range(2)]
            ot = [sb("ot%d" % i, [128, D], F32) for i in range(2)]; ot_b = [Buf() for _ in range(2)]
            ob = [sb("ob%d" % i, [128, D], BF16) for i in range(2)]; ob_b = [Buf() for _ in range(2)]
            hts = [sb("hts%d" % i, [128, 8, 256], BF16) for i in range(2)]; hts_b = [Buf() for _ in range(2)]
            ps = lambda n, sh, dt: es.enter_context(self.pst("ff_" + n, sh, dt))
            pu = [ps("pu%d" % i, [128, 512], F32) for i in range(2)]; pu_b = [Buf() for _ in range(2)]
            po = [ps("po%d" % i, [128, 512], F32) for i in range(2)]; po_b = [Buf() for _ in range(2)]
            psT = [ps("psT%d" % i, [128, 8, 128], BF16) for i in range(2)]; psT_b = [Buf() for _ in range(2)]
            it = 0
            for c2 in range(T // 256):
                ci = c2 % 2
                S.dma("sp", hc[ci][:, :, :], h1Tv[:, :, c2 * 256:(c2 + 1) * 256], reads=[self.B("h1T")], writes=[hc_b[ci]])
                for fc in range(32):
                    i = it % 2; it += 1
                    def f(e, i=i, fc=fc, ci=ci):
                        for kc in range(8):
                            ins = e.matmul(pu[i][:, 0:256], lhsT=w1[:, kc, fc * 128:(fc + 1) * 128], rhs=hc[ci][:, kc, :], start=(kc == 0), stop=(kc == 7))
                        return ins
                    S.op("pe", f, reads=[w_b, hc_b[ci]], writes=[pu_b[i]])
                    S.op("act", lambda e, i=i: e.activation(out=rl[i][:, :], in_=pu[i][:, 0:256], func=AF.Relu), reads=[pu_b[i]], writes=[rl_b[i]])
                    S.op("dve", lambda e, i=i, fc=fc: e.tensor_tensor(out=uT[:, fc, :], in0=rl[i][:, :], in1=rl[i][:, :], op=ALU.mult),
                         reads=[rl_b[i]], writes=[uT_b])
                for s in range(2):
                    tt = c2 * 2 + s
                    i = tt % 2
                    S.dma("sp", rt[i][:, :], self.r2[tt * 128:(tt + 1) * 128, :], reads=[self.B("r2")], writes=[rt_b[i]])
                    for hf in range(2):
                        def f(e, hf=hf, s=s):
                            for fc in range(32):
                                ins = e.matmul(po[hf][:, :], lhsT=uT[:, fc, s * 128:(s + 1) * 128], rhs=w2[:, fc, hf * 512:(hf + 1) * 512],
                                               start=(fc == 0), stop=(fc == 31))
                            return ins
                        S.op("pe", f, reads=[w_b, uT_b], writes=[po_b[hf]])
                        S.op("dve", lambda e, hf=hf, i=i: e.tensor_tensor(out=rt[i][:, hf * 512:(hf + 1) * 512], in0=po[hf][:, :], in1=rt[i][:, hf * 512:(hf + 1) * 512], op=ALU.add),
                             reads=[po_b[hf], rt_b[i]], writes=[rt_b[i]])
                    self.ln_ops("ff", rt[i], rt_b[i], ot[i], ot_b[i], c["g"], c["bb"], c["tmp"][i])
                    S.dma("pool", self.hA_o[tt * 128:(tt + 1) * 128, :], ot[i][:, :], reads=[ot_b[i]], writes=[self.B("hA_out")])
                    S.op("act", lambda e, i=i: e.activation(out=ob[i][:, :], in_=ot[i][:, :], func=AF.Copy), reads=[ot_b[i]], writes=[ob_b[i]])
                    self.transpose_ops(ob[i], ob_b[i], psT[i], psT_b[i], hts[ci], hts_b[ci], s * 128, c["ident"], c["ident_b"])
                S.dma("pool", hTov[:, :, c2 * 256:(c2 + 1) * 256], hts[ci][:, :, :], reads=[hts_b[ci]], writes=[self.B("hT_out")])
            S.emit()


NEG = -30000.0


def _t5_bucket(rel):
    nb, max_exact = 16, 8
    n = np.abs(rel)
    large = max_exact + (np.log(np.maximum(n, 1).astype(np.float32) / max_exact)
                         / math.log(1024 / max_exact) * (nb - max_exact)).astype(np.int32)
    large = np.minimum(large, nb - 1)
    return np.where(rel > 0, nb, 0) + np.where(n < max_exact, n, large)


def _banded_tab(rel_bias, heads, r, half, kb0, dms, hf):
    L = T // r
    nb = L // 128
    k = np.arange(128)[:, None]; q = np.arange(128)[None, :]
    out = []
    for h in heads:
        for mode in range(3):
            b = {0: 1, 1: 0, 2: nb - 1}[mode]
            if nb == 2 and mode == 0:
                b = 0
            for dm in dms:
                m = b + dm
                f = kb0 + 128 * m + k
                rel = f - (128 * b + q)
                val = rel_bias[_t5_bucket(rel * r), h].astype(np.float32)
                ok = np.abs(rel) <= half
                if mode == 1 and hf == 0:
                    ok = ok & (f >= 0)
                if mode == 2 and hf == 1:
                    ok = ok & (f < L)
                if mode == 0:
                    pass
                out.append(np.where(ok, val, NEG).astype(np.float32))
    return np.stack(out)


def _na_tab(rpb, hf):
    k = np.arange(128)[:, None]; q = np.arange(128)[None, :]
    a_loc, kc = k // 64, k % 64
    r_loc, c = q // 64, q % 64
    col_start = np.clip(c - 8, 0, 48)
    col_ok = (kc >= col_start) & (kc < col_start + 16)
    dc = np.clip(kc - c, -15, 15) + 15

    def tile(h, i, j):
        if j < 0 or j >= 64:
            return np.full((128, 128), NEG, np.float32)
        r = 2 * i + r_loc; a = 2 * j + a_loc
        start = np.clip(r - 4, 0, 120)
        ok = (a >= start) & (a < start + 8) & col_ok
        dr = np.clip(a - r + 7, 0, 14)
        return np.where(ok, rpb[h, dr, dc], NEG).astype(np.float32)
    out = []
    for h in range(4):
        for dj in range(-2, 3):
            out.append(tile(h, 10, 10 + dj))
        for b in (0, 1, 30, 31):
            i = hf * 32 + b
            for dj in range(-3, 4):
                out.append(tile(h, i, i + dj))
    return np.stack(out)


def _rope_tab(pos):
    half = 16
    inv = (10000.0 ** (-np.arange(half, dtype=np.float32) / half)).astype(np.float32)
    ang = pos.astype(np.float32)[None, :] * inv[:, None]
    cos, sin = np.cos(ang).astype(np.float32), np.sin(ang).astype(np.float32)
    return np.stack([np.concatenate([cos, cos], 0), np.concatenate([-sin, sin], 0)]).astype(np.float32)


_PROGS = {}


def build_A():
    k = K()
    k.declare_A()
    k.phase_ln_emb()
    k.finish([k.B("hA_out"), k.B("hT_out")])
    return k


def build_B(debug=()):
    k = K(debug=debug)
    k.declare_B()
    edge = {0: 0, 1: 1, 30: 2, 31: 3}
    k.phase_mla()
    k.phase_combine("ca_", [k.Og["A"]], k.yT[0])
    k.phase_banded("sw_", 416, 672, 800, 2, 1, 128, lambda b, nb: ([b, b + 1, b + 2], 3 if b == 0 else (6 if b == nb - 1 else 0)),
                   9, k.tabB, k.Og["B"], True)
    k.phase_combine("cb_", [k.Og["B"]], k.yT[1], sink=True)
    k.phase_banded("na_", 928, 1184, 1440, 4, 1, 384,
                   lambda b, nb: (list(range(b, b + 7)), 5 + 7 * edge[b]) if b in edge else (list(range(b + 1, b + 6)), 0),
                   33, k.tabC, k.Og["C"], False)
    k.phase_combine("cc_", [k.Og["C"]], k.yT[2])
    for g, r in enumerate((1, 4, 16)):
        k.phase_banded("d%d_" % g, 1696 + g * 256, 2464 + g * 256, 3232 + g * 256, 4, r, 64,
                       lambda b, nb: ([b, b + 1], 2 if b == 0 else (4 if b == nb - 1 else 0)), 6, k.tabD[g], k.Og["D%d" % g], False)
    k.phase_combine("cd_", [k.Og["D0"], k.Og["D1"], k.Og["D2"]], k.yT[3])
    k.phase_merge1()
    k.phase_merge2()
    k.phase_ffn()
    outs = [k.B("hA_out"), k.B("hT_out")]
    if debug:
        outs += k.dump(debug)
    k.finish(outs)
    return k


def _f(a):
    return np.ascontiguousarray(np.asarray(a, dtype=np.float32))


def build_F():
    k = K()
    k.declare_F()
    k.phase_ln_emb()
    edge = {0: 0, 1: 1, 30: 2, 31: 3}
    for l in range(NL):
        k.allgather(l)
        k.set_layer(l)
        k.phase_mla()
        k.phase_combine("ca_", [k.Og["A"]], k.yT[0])
        k.phase_banded("sw_", 416, 672, 800, 2, 1, 128, lambda b, nb: ([b, b + 1, b + 2], 3 if b == 0 else (6 if b == nb - 1 else 0)),
                       9, k.tabB, k.Og["B"], True)
        k.phase_combine("cb_", [k.Og["B"]], k.yT[1], sink=True)
        k.phase_banded("na_", 928, 1184, 1440, 4, 1, 384,
                       lambda b, nb: (list(range(b, b + 7)), 5 + 7 * edge[b]) if b in edge else (list(range(b + 1, b + 6)), 0),
                       33, k.tabC, k.Og["C"], False)
        k.phase_combine("cc_", [k.Og["C"]], k.yT[2])
        for g, r in enumerate((1, 4, 16)):
            k.phase_banded("d%d_" % g, 1696 + g * 256, 2464 + g * 256, 3232 + g * 256, 4, r, 64,
                           lambda b, nb: ([b, b + 1], 2 if b == 0 else (4 if b == nb - 1 else 0)), 6, k.tabD[g], k.Og["D%d" % g], False)
        k.phase_combine("cd_", [k.Og["D0"], k.Og["D1"], k.Og["D2"]], k.yT[3])
        k.phase_merge1()
        k.phase_merge2()
        k.phase_ffn()
    k.finish([k.B("out")])
    return k


def kernel(**inputs):
    x = _f(inputs["x"]); p = _f(inputs["p"])
    rel_bias = _f(inputs["rel_bias"]); na_rpb = _f(inputs["na_rpb"])
    ident = np.eye(128, dtype=np.float32)
    if "F" not in _PROGS:
        _PROGS["F"] = build_F()
    ropek = _rope_tab(np.arange(SEQ))
    shared = {"ident": ident, "ln_emb_g": _f(inputs["ln_emb_g"]), "ln_emb_b": _f(inputs["ln_emb_b"]), "ropek": ropek}
    for l in range(NL):
        for n in ("w_in", "mla_q_norm", "mla_w_uq", "mla_kv_norm", "mla_w_ukv", "swa_sink", "w_out", "ln1_g", "ln1_b",
                  "w_ff1", "w_ff2", "w_ple", "w_ple_gate", "ln2_g", "ln2_b"):
            shared["%s_%d" % (n, l)] = _f(inputs[n][l])
        shared["w_branch_%d" % l] = _f(inputs["w_branch"][l]).reshape(4 * 256, D)
    maps = []
    for c in range(8):
        b, hf = c // 2, c % 2
        m = dict(shared)
        m["x"] = np.ascontiguousarray(x[b, hf * T:(hf + 1) * T])
        m["ropeq"] = _rope_tab(hf * T + np.arange(T))
        m["tabB"] = _banded_tab(rel_bias, range(4), 1, 128, -128, (0, 1, 2), hf)
        for g, r in enumerate((1, 4, 16)):
            m["tabD%d" % g] = _banded_tab(rel_bias, range(4 + 4 * g, 8 + 4 * g), r, 64, -64, (0, 1), hf)
        for l in range(NL):
            m["p_%d" % l] = np.ascontiguousarray(p[l, b, hf * T:(hf + 1) * T])
            m["tabC_%d" % l] = _na_tab(na_rpb[l], hf)
        maps.append(m)
    res = run_bass_kernel_spmd(_PROGS["F"].nc, maps, core_ids=list(range(8)))
    out = np.zeros((4, SEQ, D), np.float32)
    for c in range(8):
        b, hf = c // 2, c % 2
        out[b, hf * T:(hf + 1) * T] = np.asarray(res.results[c]["out"], dtype=np.float32)
    return out
```

```python
import math
from contextlib import ExitStack

import numpy as np
import concourse.bass as bass
import concourse.mybir as mybir
from concourse.bass_utils import run_bass_kernel_spmd

F32 = mybir.dt.float32
BF16 = mybir.dt.bfloat16
AF = mybir.ActivationFunctionType
ALU = mybir.AluOpType

D = 1024
T = 4096
SEQ = 8192
NL = 2
ALPHA = (2 * NL) ** 0.25
IN_COLS = 8096
NDMA_SEM = 8


class Buf:
    def __init__(self, name="", multi=False):
        self.name = name
        self.multi = multi
        self.writers = {}
        self.readers = {}


class Op:
    __slots__ = ("eng", "emit", "deps", "needed", "sem", "val", "is_dma", "key", "idx")


class Sched:
    ENGS = ("pe", "act", "dve", "pool", "sp")

    def __init__(self, nc, es):
        self.nc = nc
        self.sem = {e: nc.alloc_semaphore(name="s_" + e) for e in self.ENGS}
        self.dsem = {e: [nc.alloc_semaphore(name="d_%s%d" % (e, i)) for i in range(NDMA_SEM)]
                     for e in ("sp", "pool", "act")}
        self.all_sems = list(self.sem.values()) + [x for v in self.dsem.values() for x in v]
        self.cnt = {e: 0 for e in self.ENGS}
        self.dcnt = {e: 0 for e in self.dsem}
        self.dlast = {e: [None] * NDMA_SEM for e in self.dsem}
        self.waited = {e: {} for e in self.ENGS}
        self.ops = {e: [] for e in self.ENGS}
        self.same_engine_sync = True
        self.nops = 0

    def _mk(self, eng, emit, reads, writes, is_dma):
        op = Op()
        op.eng, op.emit, op.is_dma, op.needed = eng, emit, is_dma, False
        op.sem = op.val = None
        op.idx = self.nops
        self.nops += 1
        deps = {}

        def add(o):
            if o is None:
                return
            if (not o.is_dma) and o.eng == eng:
                if eng == "pe" or not self.same_engine_sync:
                    return
            deps[id(o)] = o

        for b in reads:
            for o in b.writers.values():
                add(o)
        for b in writes:
            for o in b.readers.values():
                add(o)
            for o in b.writers.values():
                add(o)
        if is_dma:
            k = self.dcnt[eng]
            slot = k % NDMA_SEM
            self.dcnt[eng] = k + 1
            op.sem = self.dsem[eng][slot]
            op.val = 16 * (k // NDMA_SEM + 1)
            op.key = ("d", eng, slot)
            add(self.dlast[eng][slot])
            self.dlast[eng][slot] = op
            op.needed = True
        else:
            op.key = ("e", eng)
        op.deps = list(deps.values())
        for o in op.deps:
            o.needed = True
        for b in reads:
            b.readers[op.key] = op
        for b in writes:
            if b.multi:
                b.writers[op.key] = op
            else:
                b.writers = {op.key: op}
                b.readers = {}
        self.ops[eng].append(op)
        return op

    def op(self, eng, emit, reads=(), writes=()):
        return self._mk(eng, emit, reads, writes, False)

    def dma(self, eng, out, in_, reads=(), writes=()):
        return self._mk(eng, lambda e: e.dma_start(out=out, in_=in_), reads, writes, True)

    def collective(self, emit, sem, val, reads=(), writes=()):
        op = self._mk("pool", emit, reads, writes, False)
        op.is_dma = True
        op.needed = True
        op.sem, op.val = sem, val
        op.key = ("cc", id(sem))
        for b in reads:
            b.readers.pop(("e", "pool"), None)
            b.readers[op.key] = op
        for b in writes:
            b.writers = {op.key: op}
        return op

    def emit(self):
        nc = self.nc
        for e in self.ENGS:
            for op in self.ops[e]:
                if not op.is_dma and op.needed and op.sem is None:
                    self.cnt[e] += 1
                    op.sem = self.sem[e]
                    op.val = self.cnt[e]
        with nc.Block() as block:
            hooks = {"pe": block.tensor, "act": block.scalar, "dve": block.vector,
                     "pool": block.gpsimd, "sp": block.sync}
            for e in self.ENGS:
                ops = self.ops[e]
                if not ops:
                    continue
                waited = self.waited[e]

                def body(eng, ops=ops, waited=waited):
                    for op in ops:
                        for d in op.deps:
                            if d.val is None:
                                continue
                            k = id(d.sem)
                            if waited.get(k, 0) < d.val:
                                eng.wait_ge(d.sem, d.val)
                                waited[k] = d.val
                        inst = op.emit(eng)
                        if op.key[0] == "cc":
                            inst.then_inc(op.sem)
                        elif op.is_dma:
                            inst.then_inc(op.sem, 16)
                        elif op.needed:
                            inst.then_inc(op.sem, 1)

                hooks[e](body)
        self.ops = {e: [] for e in self.ENGS}

    def final_wait(self, eng, bufs):
        self._mk(eng, lambda e: e.nop(), bufs, (), False)


def _bcast_rows(ap1d, n=128):
    return ap1d.partition_broadcast(n)


class K:
    def __init__(self, debug=()):
        self.debug = set(debug)
        self.nc = bass.Bass("TRN2", target_bir_lowering=False)
        self.es = ExitStack()
        self.cc_sems = [self.nc.alloc_semaphore(name="cc%d" % i) for i in range(NL)]
        self.cm = self.nc.cleanup_on_exit()
        self.cm.__enter__()
        self.S = Sched(self.nc, self.es)
        self.dram = {}
        self.bufs = {}

    def sbt(self, name, shape, dt):
        self.uid = getattr(self, "uid", 0) + 1
        return self.nc.sbuf_tensor("%s_%d" % (name, self.uid), shape, dt)

    def pst(self, name, shape, dt):
        self.uid = getattr(self, "uid", 0) + 1
        return self.nc.psum_tensor("%s_%d" % (name, self.uid), shape, dt)

    def ext_in(self, name, shape, dt=F32):
        t = self.nc.dram_tensor(name, list(shape), dt, kind="ExternalInput")
        self.dram[name] = t
        self.bufs[name] = Buf(name, multi=True)
        return t

    def ext_out(self, name, shape, dt=F32):
        t = self.nc.dram_tensor(name, list(shape), dt, kind="ExternalOutput")
        self.dram[name] = t
        self.bufs[name] = Buf(name, multi=True)
        return t

    def scratch(self, name, shape, dt):
        t = self.nc.dram_tensor(name, list(shape), dt)
        self.shapes = getattr(self, "shapes", {})
        self.shapes[name] = (list(shape), dt)
        self.dram[name] = t
        self.bufs[name] = Buf(name, multi=True)
        return t

    def B(self, name):
        return self.bufs[name]

    def ln_ops(self, tag, src, src_b, dst, dst_b, g_bc, b_bc, tmp, eps=1e-5):
        S = self.S
        st, mv, sd = tmp["st"], tmp["mv"], tmp["sd"]
        bst = tmp["b"]
        def f_stats(e):
            e.bn_stats(out=st[:, 0, :], in_=src[:, 0:512])
            return e.bn_stats(out=st[:, 1, :], in_=src[:, 512:1024])
        S.op("dve", f_stats, reads=[src_b], writes=[tmp["b0"]])
        S.op("dve", lambda e: e.bn_aggr(out=mv[:, :], in_=st[:, :, :]), reads=[tmp["b0"]], writes=[bst])
        S.op("act", lambda e: e.activation(out=sd[:, 0:1], in_=mv[:, 1:2], func=AF.Sqrt, bias=tmp["eps"][:, 0:1], scale=1.0),
             reads=[bst], writes=[tmp["b2"]])
        S.op("dve", lambda e: e.reciprocal(out=sd[:, 1:2], in_=sd[:, 0:1]), reads=[tmp["b2"]], writes=[tmp["b3"]])
        S.op("dve", lambda e: e.tensor_scalar(out=dst[:, :], in0=src[:, :], scalar1=mv[:, 0:1], scalar2=sd[:, 1:2],
                                              op0=ALU.subtract, op1=ALU.mult),
             reads=[src_b, bst, tmp["b3"]], writes=[dst_b])
        S.op("pool", lambda e: e.tensor_tensor(out=dst[:, :], in0=dst[:, :], in1=g_bc[:, :], op=ALU.mult),
             reads=[dst_b, tmp["gb"]], writes=[dst_b])
        S.op("dve", lambda e: e.tensor_tensor(out=dst[:, :], in0=dst[:, :], in1=b_bc[:, :], op=ALU.add),
             reads=[dst_b, tmp["gb"]], writes=[dst_b])

    def transpose_ops(self, src_bf, src_b, psT, psT_b, dstT, dstT_b, col0, ident, ident_b, nk=8):
        S = self.S
        def f_tr(e):
            for kc in range(nk):
                i = e.transpose(out=psT[:, kc, :], in_=src_bf[:, kc * 128:(kc + 1) * 128], identity=ident[:, :])
            return i
        S.op("pe", f_tr, reads=[src_b, ident_b], writes=[psT_b])
        S.op("dve", lambda e: e.tensor_copy(out=dstT[:, 0:nk, col0:col0 + 128], in_=psT[:, 0:nk, :]),
             reads=[psT_b], writes=[dstT_b])

    def ln_consts(self, es, g_ap, b_ap, tagp):
        nc, S = self.nc, self.S
        sb = lambda n, sh, dt: es.enter_context(self.sbt(tagp + n, sh, dt))
        c = {}
        c["g"] = sb("g_bc", [128, D], F32); c["bb"] = sb("b_bc", [128, D], F32)
        c["eps"] = sb("eps", [128, 1], F32)
        c["ident"] = sb("identb", [128, 128], BF16); c["identf"] = sb("identf", [128, 128], F32)
        c["gb"] = Buf("gb"); c["ident_b"] = Buf("ident")
        S.dma("sp", c["g"][:, :], g_ap.partition_broadcast(128), writes=[c["gb"]])
        S.dma("sp", c["bb"][:, :], b_ap.partition_broadcast(128), writes=[c["gb"]])
        S.dma("sp", c["identf"][:, :], self.ident_f[:, :], writes=[c["ident_b"]])
        S.op("dve", lambda e: e.tensor_copy(out=c["ident"][:, :], in_=c["identf"][:, :]), reads=[c["ident_b"]], writes=[c["ident_b"]])
        S.op("dve", lambda e: e.memset(c["eps"][:, :], 1e-5), writes=[c["gb"]])
        c["tmp"] = []
        for i in range(2):
            t = {"st": sb("st%d" % i, [128, 2, 6], F32), "mv": sb("mv%d" % i, [128, 2], F32),
                 "sd": sb("sd%d" % i, [128, 2], F32), "b": Buf(), "b0": Buf(), "b2": Buf(), "b3": Buf(),
                 "gb": c["gb"], "eps": c["eps"]}
            c["tmp"].append(t)
        return c

    def phase_ln_emb(self):
        nc, S = self.nc, self.S
        self.hTv = self.hT.ap().rearrange("(kc p) t -> p kc t", p=128)
        with ExitStack() as es:
            sb = lambda n, sh, dt: es.enter_context(self.sbt("le_" + n, sh, dt))
            c = self.ln_consts(es, self.ln_emb_g.ap(), self.ln_emb_b.ap(), "le_")
            xt = [sb("xt%d" % i, [128, D], F32) for i in range(2)]; xt_b = [Buf() for _ in range(2)]
            yt = [sb("yt%d" % i, [128, D], F32) for i in range(2)]; yt_b = [Buf() for _ in range(2)]
            yb = [sb("yb%d" % i, [128, D], BF16) for i in range(2)]; yb_b = [Buf() for _ in range(2)]
            hts = [sb("hts%d" % i, [128, 8, 512], BF16) for i in range(2)]; hts_b = [Buf() for _ in range(2)]
            psT = [es.enter_context(self.pst("le_psT%d" % i, [128, 8, 128], BF16)) for i in range(2)]
            psT_b = [Buf() for _ in range(2)]
            hTv = self.hT.ap().rearrange("(kc p) t -> p kc t", p=128)
            for tt in range(T // 128):
                i = tt % 2
                S.dma("sp", xt[i][:, :], self.x[tt * 128:(tt + 1) * 128, :], reads=[self.B("x")], writes=[xt_b[i]])
                self.ln_ops("le", xt[i], xt_b[i], yt[i], yt_b[i], c["g"], c["bb"], c["tmp"][i])
                S.dma("pool", self.hA[tt * 128:(tt + 1) * 128, :], yt[i][:, :], reads=[yt_b[i]], writes=[self.B("hA")])
                S.op("act", lambda e, i=i: e.activation(out=yb[i][:, :], in_=yt[i][:, :], func=AF.Copy),
                     reads=[yt_b[i]], writes=[yb_b[i]])
                j = (tt // 4) % 2
                self.transpose_ops(yb[i], yb_b[i], psT[i], psT_b[i], hts[j], hts_b[j], (tt % 4) * 128, c["ident"], c["ident_b"])
                if tt % 4 == 3:
                    c0 = (tt // 4) * 512
                    S.dma("pool", hTv[:, :, c0:c0 + 512], hts[j][:, :, :], reads=[hts_b[j]], writes=[self.B("hT")])
            S.emit()

    def dump(self, names):
        outs = []
        for n in names:
            shape, dt = self.shapes[n]
            o = self.ext_out("dbg_" + n, shape, dt)
            src, dst = self.dram[n].ap(), o.ap()
            R = shape[0]
            step = max(1, R // 4)
            for r0 in range(0, R, step):
                self.S.dma("sp", dst[r0:min(R, r0 + step)], src[r0:min(R, r0 + step)], reads=[self.B(n)], writes=[self.B("dbg_" + n)])
            outs.append(self.B("dbg_" + n))
        return outs

    def finish(self, out_bufs):
        self.S.final_wait("sp", out_bufs)
        self.S.final_wait("pool", out_bufs)
        self.S.emit()
        self.es.close()
        self.cm.__exit__(None, None, None)


    LAYER_W = (("w_in", [D, IN_COLS]), ("mla_q_norm", [256]), ("mla_w_uq", [256, 384]), ("mla_kv_norm", [128]),
               ("mla_w_ukv", [128, 512]), ("swa_sink", [4]), ("w_branch", [4 * 256, D]), ("w_out", [D, D]),
               ("ln1_g", [D]), ("ln1_b", [D]), ("w_ff1", [D, 4 * D]), ("w_ff2", [4 * D, D]), ("w_ple", [256, D]),
               ("w_ple_gate", [D, D]), ("ln2_g", [D]), ("ln2_b", [D]), ("p", [T, 256]), ("tabC", [4 * 33, 128, 128]))

    def declare_F(self):
        e = self.ext_in
        self.x = e("x", [T, D]); self.ident_f = e("ident", [128, 128])
        self.ln_emb_g = e("ln_emb_g", [D]); self.ln_emb_b = e("ln_emb_b", [D])
        self.ropeq = e("ropeq", [2, 32, T]); self.ropek = e("ropek", [2, 32, SEQ])
        self.tabB = e("tabB", [4 * 9, 128, 128])
        self.tabD = [e("tabD%d" % g, [4 * 6, 128, 128]) for g in range(3)]
        self.LW = [{n: e("%s_%d" % (n, l), sh) for n, sh in self.LAYER_W} for l in range(NL)]
        self.out = self.ext_out("out", [T, D])
        s = self.scratch
        self.hA_l = [s("hA%d" % l, [T, D], F32) for l in range(NL)]
        self.hT_l = [s("hT%d" % l, [D, T], BF16) for l in range(NL)]
        self.G_l = [s("G%d" % l, [8, 256, T], BF16) for l in range(NL)]
        self.hTd = s("hTd", [D, T], BF16)
        self.h1T = s("h1T", [D, T], BF16); self.r2 = s("r2", [T, D], F32)
        self.yT = [s("yT%d" % n, [256, T], BF16) for n in range(4)]
        self.Og = {n: s("Og" + n, [T, 260], F32) for n in ("A", "B", "C", "D0", "D1", "D2")}
        self.vS = s("vS", [T + 2048, 260], BF16)
        self.hA, self.hT = self.hA_l[0], self.hT_l[0]
        self.bufs["hA"] = self.bufs["hA0"]; self.bufs["hT"] = self.bufs["hT0"]
        self.init_sems()

    def set_layer(self, l):
        W = self.LW[l]
        for n, _ in self.LAYER_W:
            setattr(self, n, W[n])
            self.bufs[n] = self.bufs["%s_%d" % (n, l)]
        self.hA, self.hTo, self.G = self.hA_l[l], self.hT_l[l], self.G_l[l]
        self.bufs["hA"] = self.bufs["hA%d" % l]; self.bufs["hTo"] = self.bufs["hT%d" % l]; self.bufs["G"] = self.bufs["G%d" % l]
        if l < NL - 1:
            self.hA_o, self.hT_o = self.hA_l[l + 1], self.hT_l[l + 1]
            self.bufs["hA_out"] = self.bufs["hA%d" % (l + 1)]; self.bufs["hT_out"] = self.bufs["hT%d" % (l + 1)]
        else:
            self.hA_o, self.hT_o = self.out, self.hTd
            self.bufs["hA_out"] = self.bufs["out"]; self.bufs["hT_out"] = self.bufs["hTd"]
        self.hTo_v = self.hTo.ap().rearrange("(kc p) t -> p kc t", p=128)
        self.G_v = [self.G[:, r * 128:(r + 1) * 128, :].rearrange("kc p t -> p kc t") for r in range(2)]

    def allgather(self, l):
        S = self.S
        hT, G = self.hT_l[l], self.G_l[l]
        for j in range(8):
            S.collective(lambda e, j=j: e.collective_compute("AllGather", ALU.bypass,
                                                             replica_groups=[[0, 1], [2, 3], [4, 5], [6, 7]],
                                                             ins=[hT[j * 128:(j + 1) * 128, :].opt()], outs=[G[j].opt()]),
                         self.cc_sems[l], j + 1, reads=[self.B("hT%d" % l)], writes=[self.B("G%d" % l)])
        S.op("pool", lambda e: e.nop(), reads=[self.B("G%d" % l)])
        S.emit()

    def init_sems(self):
        with self.nc.Block() as block:
            @block.gpsimd
            def _(g):
                for sm in self.S.all_sems:
                    g.sem_clear(sm)

    def declare_A(self):
        e = self.ext_in
        self.x = e("x", [T, D]); self.ident_f = e("ident", [128, 128])
        self.ln_emb_g = e("ln_emb_g", [D]); self.ln_emb_b = e("ln_emb_b", [D])
        self.hA = self.ext_out("hA_out", [T, D]); self.hT = self.ext_out("hT_out", [D, T], BF16)
        self.bufs["hA"] = self.bufs["hA_out"]; self.bufs["hT"] = self.bufs["hT_out"]
        self.init_sems()

    def declare_B(self):
        e = self.ext_in
        self.hA = e("hA", [T, D]); self.hTo = e("hTo", [D, T], BF16); self.G = e("G", [2 * D, T], BF16)
        self.p = e("p", [T, 256]); self.ident_f = e("ident", [128, 128])
        self.w_in = e("w_in", [D, IN_COLS])
        self.mla_q_norm = e("mla_q_norm", [256]); self.mla_w_uq = e("mla_w_uq", [256, 384])
        self.mla_kv_norm = e("mla_kv_norm", [128]); self.mla_w_ukv = e("mla_w_ukv", [128, 512])
        self.swa_sink = e("swa_sink", [4])
        self.w_branch = e("w_branch", [4 * 256, D]); self.w_out = e("w_out", [D, D])
        self.ln1_g = e("ln1_g", [D]); self.ln1_b = e("ln1_b", [D])
        self.w_ff1 = e("w_ff1", [D, 4 * D]); self.w_ff2 = e("w_ff2", [4 * D, D])
        self.w_ple = e("w_ple", [256, D]); self.w_ple_gate = e("w_ple_gate", [D, D])
        self.ln2_g = e("ln2_g", [D]); self.ln2_b = e("ln2_b", [D])
        self.ropeq = e("ropeq", [2, 32, T]); self.ropek = e("ropek", [2, 32, SEQ])
        self.tabB = e("tabB", [4 * 9, 128, 128]); self.tabC = e("tabC", [4 * 33, 128, 128])
        self.tabD = [e("tabD%d" % g, [4 * 6, 128, 128]) for g in range(3)]
        self.hA_o = self.ext_out("hA_out", [T, D]); self.hT_o = self.ext_out("hT_out", [D, T], BF16)
        s = self.scratch
        self.h1T = s("h1T", [D, T], BF16); self.r2 = s("r2", [T, D], F32)
        self.yT = [s("yT%d" % n, [256, T], BF16) for n in range(4)]
        self.Og = {n: s("Og" + n, [T, 260], F32) for n in ("A", "B", "C", "D0", "D1", "D2")}
        self.vS = s("vS", [T + 2048, 260], BF16)
        self.hTo_v = self.hTo.ap().rearrange("(kc p) t -> p kc t", p=128)
        self.G_v = [self.G[r * D:(r + 1) * D, :].rearrange("(kc p) t -> p kc t", p=128) for r in range(2)]
        self.init_sems()

    def wload(self, dst, src, c0=0, eng="pool", wb=None):
        n = src.shape[1]
        kc = src.shape[0] // 128
        v = src.rearrange("(kc p) n -> p kc n", p=128)
        for a in range(0, n, 2048):
            b = min(n, a + 2048)
            self.S.dma(eng, dst[:, 0:kc, c0 + a:c0 + b], v[:, :, a:b], writes=[wb])

    def hT_src(self, e0, n, H):
        t0 = e0 - H
        if t0 < 0:
            return self.G_v[0][:, :, T + t0:T + t0 + n], self.B("G")
        if t0 >= T:
            return self.G_v[1][:, :, t0 - T:t0 - T + n], self.B("G")
        return self.hTo_v[:, :, t0:t0 + n], self.B("hTo")

    def phase_banded(self, tag, qcols, kcols, vcols, nkv, r, halo_f, blocks_fn, nslots, tab, og, dup_k):
        nc, S = self.nc, self.S
        H = r * halo_f
        E = T + 2 * H
        L = T // r
        nb = L // 128
        nkb = (L + 2 * halo_f) // 128
        with ExitStack() as es:
            sb = lambda n, sh, dt: es.enter_context(self.sbt(tag + n, sh, dt))
            wq = sb("wq", [128, 8, 256], BF16); wk = sb("wk", [128, 8, 256], BF16); wv = sb("wv", [128, 8, 256], BF16)
            w_b = Buf()
            self.wload(wq, self.w_in[:, qcols:qcols + 256], wb=w_b)
            if dup_k:
                for j in range(2):
                    for d in range(2):
                        self.wload(wk, self.w_in[:, kcols + j * 64:kcols + j * 64 + 64], c0=j * 128 + d * 64, wb=w_b)
            else:
                self.wload(wk, self.w_in[:, kcols:kcols + 256], wb=w_b)
            self.wload(wv, self.w_in[:, vcols:vcols + nkv * 64], wb=w_b)
            qT = [sb("qT%d" % m, [128, T], BF16) for m in range(2)]
            kT = [sb("kT%d" % m, [128, E], BF16) for m in range(2)]
            qk_b = Buf()
            tabs = sb("tab", [128, 4 * nslots, 128], F32); tab_b = Buf()
            for h in range(4):
                S.dma("sp", tabs[:, h * nslots:(h + 1) * nslots, :],
                      tab[h * nslots:(h + 1) * nslots, :, :].rearrange("s k q -> k s q"), writes=[tab_b])
            hc = [sb("hc%d" % i, [128, 8, 512], BF16) for i in range(2)]; hc_b = [Buf() for _ in range(2)]
            vst = [sb("vst%d" % i, [128, 4, nkv, 65], BF16) for i in range(2)]
            vst_b = [[Buf() for s_ in range(4)] for _ in range(2)]
            for i in range(2):
                S.op("pool", lambda e, i=i: e.memset(vst[i][:, :, :, :], 1.0), writes=vst_b[i])
            es2 = ExitStack()
            pp = [es2.enter_context(self.pst(tag + "pp%d" % i, [128, 512], F32)) for i in range(2)]
            pp_b = [Buf() for _ in range(2)]
            ppi = [0]

            def proj_fm(wt, c0, hci, n, dst, d0):
                i = ppi[0] % 2; ppi[0] += 1
                def f(e):
                    for kc in range(8):
                        ins = e.matmul(pp[i][:, 0:n], lhsT=wt[:, kc, c0:c0 + 128], rhs=hc[hci][:, kc, 0:n],
                                       start=(kc == 0), stop=(kc == 7))
                    return ins
                S.op("pe", f, reads=[w_b, hc_b[hci]], writes=[pp_b[i]])
                S.op("act", lambda e: e.activation(out=dst[:, d0:d0 + n], in_=pp[i][:, 0:n], func=AF.Copy),
                     reads=[pp_b[i]], writes=[qk_b])

            chunks = []
            e0 = 0
            while e0 < E:
                t0 = e0 - H
                if t0 < 0:
                    n = min(512, -t0)
                elif t0 >= T:
                    n = min(512, E - e0)
                else:
                    n = min(512, T - t0)
                chunks.append((e0, n)); e0 += n
            vcnt = 0
            for ci, (e0, n) in enumerate(chunks):
                hci = ci % 2
                src, src_b = self.hT_src(e0, n, H)
                S.dma("sp", hc[hci][:, :, 0:n], src, reads=[src_b], writes=[hc_b[hci]])
                own = 0 <= e0 - H < T
                for m in range(2):
                    proj_fm(wk, m * 128, hci, n, kT[m], e0)
                    if own:
                        proj_fm(wq, m * 128, hci, n, qT[m], e0 - H)
                vi = ci % 2
                nsub = (n + 127) // 128
                for si, s0 in enumerate(range(0, n, 128)):
                    ns = min(128, n - s0)
                    i = ppi[0] % 2; ppi[0] += 1
                    def f(e, i=i, s0=s0, ns=ns, hci=hci):
                        for kc in range(8):
                            ins = e.matmul(pp[i][0:ns, 0:nkv * 64], lhsT=hc[hci][:, kc, s0:s0 + ns],
                                           rhs=wv[:, kc, 0:nkv * 64], start=(kc == 0), stop=(kc == 7))
                        return ins
                    S.op("pe", f, reads=[w_b, hc_b[hci]], writes=[pp_b[i]])
                    S.op("dve", lambda e, i=i, vi=vi, ns=ns, si=si: e.tensor_copy(
                        out=vst[vi][0:ns, si, :, 0:64], in_=pp[i][0:ns, 0:nkv * 64].rearrange("p (h d) -> p h d", d=64)),
                        reads=[pp_b[i]], writes=[vst_b[vi][si]])
                if n % 128 == 0:
                    S.dma("pool", self.vS[e0:e0 + n, 0:nkv * 65].rearrange("(s p) c -> p s c", p=128),
                          vst[vi][:, 0:nsub, :, :].rearrange("p s h d -> p s (h d)"),
                          reads=vst_b[vi][0:nsub], writes=[self.B("vS")])
                else:
                    S.dma("pool", self.vS[e0:e0 + n, 0:nkv * 65], vst[vi][0:n, 0, :, :].rearrange("p h d -> p (h d)"),
                          reads=vst_b[vi][0:1], writes=[self.B("vS")])
            vx = sb("vx", [128, r * nkb, nkv * 65], BF16); vx_b = Buf()
            for rho in range(r):
                for m in range(nkb):
                    row0 = H + rho + r * (128 * m - halo_f)
                    srcv = self.vS[row0:row0 + 127 * r + 1:r, 0:nkv * 65] if r > 1 else self.vS[row0:row0 + 128, 0:nkv * 65]
                    S.dma("sp", vx[:, rho * nkb + m, :], srcv, reads=[self.B("vS")], writes=[vx_b])
            S.emit()
            es2.close()
            nkmax = max(len(blocks_fn(b, nb)[0]) for b in range(nb))
            NSET = 2 if nkmax <= 4 else 1
            ps = [[es.enter_context(self.pst(tag + "ps%d_%d" % (a, j), [128, 512 if nkmax <= 4 else 1024], F32)) for j in range(2)] for a in range(NSET)]
            ps_b = [[Buf() for j in range(2)] for a in range(NSET)]
            pob = [[es.enter_context(self.pst(tag + "po%d_%d" % (a, j), [128, 512], F32)) for j in range(2)] for a in range(2)]
            po_b = [[Buf() for j in range(2)] for a in range(2)]
            sbt = [[sb("sbt%d_%d" % (a, j), [128, nkmax * 128], F32) for j in range(2)] for a in range(2)]
            sbt_b = [[Buf() for j in range(2)] for a in range(2)]
            et = [[sb("et%d_%d" % (a, j), [128, nkmax * 128], BF16) for j in range(2)] for a in range(2)]
            et_b = [[Buf() for j in range(2)] for a in range(2)]
            ost = [sb("ost%d" % i, [128, 4, 65], F32) for i in range(2)]
            ost_b = [[Buf() for h in range(4)] for i in range(2)]
            units = [(rho, b, hp) for rho in range(r) for b in range(nb) for hp in range(2)]

            def stage_f(u):
                rho, b, hp = units[u]
                a = u % NSET
                ms, slot0 = blocks_fn(b, nb)
                q0 = rho + r * 128 * b
                for j in range(2):
                    h = 2 * hp + j
                    mq = h // 2
                    pr = slice((h % 2) * 64, (h % 2) * 64 + 64)
                    qap = qT[mq][pr, q0:q0 + 127 * r + 1:r] if r > 1 else qT[mq][pr, q0:q0 + 128]
                    def f(e, a=a, j=j, ms=ms, mq=mq, pr=pr, qap=qap, rho=rho):
                        for x, m in enumerate(ms):
                            k0 = H + rho + r * (128 * m - halo_f)
                            kap = kT[mq][pr, k0:k0 + 127 * r + 1:r] if r > 1 else kT[mq][pr, k0:k0 + 128]
                            ins = e.matmul(ps[a][j][:, x * 128:(x + 1) * 128], lhsT=kap, rhs=qap, start=True, stop=True)
                        return ins
                    S.op("pe", f, reads=[qk_b], writes=[ps_b[a][j]])

            def stage_m(u):
                rho, b, hp = units[u]
                a = u % NSET
                a2 = u % 2
                ms, slot0 = blocks_fn(b, nb)
                nk = len(ms)
                oi = (u // 2) % 2
                q0 = rho + r * 128 * b
                for j in range(2):
                    h = 2 * hp + j
                    bias = tabs[:, h * nslots + slot0:h * nslots + slot0 + nk, :].rearrange("k s q -> k (s q)")
                    S.op("dve", lambda e, a=a, a2=a2, j=j, nk=nk, bias=bias: e.scalar_tensor_tensor(
                        out=sbt[a2][j][:, 0:nk * 128], in0=ps[a][j][:, 0:nk * 128], scalar=0.125, in1=bias,
                        op0=ALU.mult, op1=ALU.add), reads=[ps_b[a][j], tab_b], writes=[sbt_b[a2][j]])
                for j in range(2):
                    S.op("act", lambda e, a2=a2, j=j, nk=nk: e.activation(out=et[a2][j][:, 0:nk * 128], in_=sbt[a2][j][:, 0:nk * 128], func=AF.Exp),
                         reads=[sbt_b[a2][j]], writes=[et_b[a2][j]])
                for j in range(2):
                    h = 2 * hp + j
                    hv = (h // 2) if dup_k else h
                    def g(e, a2=a2, j=j, ms=ms, hv=hv, rho=rho, nk=nk):
                        for x, m in enumerate(ms):
                            ins = e.matmul(pob[a2][j][:, 0:65], lhsT=et[a2][j][:, x * 128:(x + 1) * 128],
                                           rhs=vx[:, rho * nkb + m, hv * 65:(hv + 1) * 65], start=(x == 0), stop=(x == nk - 1))
                        return ins
                    S.op("pe", g, reads=[et_b[a2][j], vx_b], writes=[po_b[a2][j]])
                for j in range(2):
                    h = 2 * hp + j
                    S.op("act", lambda e, a2=a2, j=j, h=h, oi=oi: e.activation(out=ost[oi][:, h, :], in_=pob[a2][j][:, 0:65], func=AF.Copy),
                         reads=[po_b[a2][j]], writes=[ost_b[oi][h]])
                if hp == 1:
                    dst = og[q0:q0 + 127 * r + 1:r, :] if r > 1 else og[q0:q0 + 128, :]
                    S.dma("sp", dst, ost[oi][:, :, :].rearrange("p h d -> p (h d)"), reads=ost_b[oi], writes=[self.B(og.name)])

            if NSET == 2:
                stage_f(0)
            for u in range(len(units)):
                if NSET == 2:
                    if u + 1 < len(units):
                        stage_f(u + 1)
                else:
                    stage_f(u)
                stage_m(u)
            S.emit()

    def phase_combine(self, tag, ogs, yT, sink=False):
        nc, S = self.nc, self.S
        ng = len(ogs)
        with ExitStack() as es:
            sb = lambda n, sh, dt: es.enter_context(self.sbt(tag + n, sh, dt))
            identf = sb("idf", [128, 128], F32); ident = sb("idb", [128, 128], BF16); id_b = Buf()
            S.dma("sp", identf[:, :], self.ident_f[:, :], writes=[id_b])
            S.op("dve", lambda e: e.tensor_copy(out=ident[:, :], in_=identf[:, :]), reads=[id_b], writes=[id_b])
            esk = sb("esk", [128, 4], F32); esk_b = Buf()
            if sink:
                S.dma("sp", esk[:, :], self.swa_sink.ap().partition_broadcast(128), writes=[esk_b])
                S.op("act", lambda e: e.activation(out=esk[:, :], in_=esk[:, :], func=AF.Exp), reads=[esk_b], writes=[esk_b])
            NT4 = 4
            ot = [[sb("ot%d_%d" % (i, g), [128, 4, 65], F32) for g in range(ng)] for i in range(NT4)]
            ot_b = [[Buf() for g in range(ng)] for i in range(NT4)]
            den = [sb("den%d" % i, [128, 4], F32) for i in range(NT4)]; den_b = [Buf() for _ in range(NT4)]
            y = [sb("y%d" % i, [128, 256], BF16) for i in range(NT4)]; y_b = [[Buf() for h in range(4)] for _ in range(NT4)]
            psT = [es.enter_context(self.pst(tag + "psT%d" % i, [128, 8, 128], BF16)) for i in range(NT4)]
            psT_b = [Buf() for _ in range(NT4)]
            yts = [sb("yts%d" % i, [128, 2, 512], BF16) for i in range(2)]; yts_b = [[Buf() for k in range(4)] for _ in range(2)]
            yTv = yT.ap().rearrange("(kc p) t -> p kc t", p=128)
            for grp in range(T // 512):
                j = grp % 2
                tts = [grp * 4 + k for k in range(4)]
                for k, tt in enumerate(tts):
                    for g in range(ng):
                        S.dma("sp", ot[k][g][:, :, :].rearrange("p h d -> p (h d)"), ogs[g][tt * 128:(tt + 1) * 128, :],
                              reads=[self.B(ogs[g].name)], writes=[ot_b[k][g]])
                for g in range(1, ng):
                    for k in range(4):
                        S.op("dve", lambda e, k=k, g=g: e.tensor_tensor(out=ot[k][0][:, :, :], in0=ot[k][0][:, :, :], in1=ot[k][g][:, :, :], op=ALU.add),
                             reads=[ot_b[k][0], ot_b[k][g]], writes=[ot_b[k][0]])
                if sink:
                    for k in range(4):
                        S.op("dve", lambda e, k=k: e.tensor_tensor(out=den[k][:, :], in0=ot[k][0][:, :, 64], in1=esk[:, :], op=ALU.add),
                             reads=[ot_b[k][0], esk_b], writes=[den_b[k]])
                    for k in range(4):
                        S.op("dve", lambda e, k=k: e.reciprocal(out=den[k][:, :], in_=den[k][:, :]), reads=[den_b[k]], writes=[den_b[k]])
                else:
                    for k in range(4):
                        S.op("dve", lambda e, k=k: e.reciprocal(out=den[k][:, :], in_=ot[k][0][:, :, 64]), reads=[ot_b[k][0]], writes=[den_b[k]])
                for h in range(4):
                    for k in range(4):
                        S.op("dve", lambda e, k=k, h=h: e.tensor_scalar(out=y[k][:, h * 64:(h + 1) * 64], in0=ot[k][0][:, h, 0:64],
                                                                       scalar1=den[k][:, h:h + 1], scalar2=None, op0=ALU.mult),
                             reads=[ot_b[k][0], den_b[k]], writes=[y_b[k][h]])
                for k in range(4):
                    def f_tr(e, k=k):
                        e.transpose(out=psT[k][:, 0, :], in_=y[k][:, 0:128], identity=ident[:, :])
                        return e.transpose(out=psT[k][:, 1, :], in_=y[k][:, 128:256], identity=ident[:, :])
                    S.op("pe", f_tr, reads=y_b[k] + [id_b], writes=[psT_b[k]])
                for k in range(4):
                    S.op("dve", lambda e, k=k, j=j: e.tensor_copy(out=yts[j][:, 0:2, k * 128:(k + 1) * 128], in_=psT[k][:, 0:2, :]),
                         reads=[psT_b[k]], writes=[yts_b[j][k]])
                c0 = grp * 512
                S.dma("pool", yTv[:, :, c0:c0 + 512], yts[j][:, :, :], reads=yts_b[j], writes=[self.B(yT.name)])
            S.emit()

    def phase_mla(self):
        nc, S = self.nc, self.S
        w_in = self.w_in
        with ExitStack() as es:
            sb = lambda n, sh, dt: es.enter_context(self.sbt("ml_" + n, sh, dt))
            wq = sb("wq", [128, 8, 256], BF16); wkv = sb("wkv", [128, 8, 128], BF16)
            wkr = sb("wkr", [128, 8, 96], BF16); wkrs = sb("wkrs", [128, 8, 96], BF16)
            w_b = Buf()
            self.wload(wq, w_in[:, 0:256], wb=w_b); self.wload(wkv, w_in[:, 256:384], wb=w_b)
            self.wload(wkr, w_in[:, 256:320], c0=0, wb=w_b); self.wload(wkr, w_in[:, 384:416], c0=64, wb=w_b)
            self.wload(wkrs, w_in[:, 256:320], c0=0, wb=w_b); self.wload(wkrs, w_in[:, 400:416], c0=64, wb=w_b)
            self.wload(wkrs, w_in[:, 384:400], c0=80, wb=w_b)
            uqf = sb("uqf", [128, 2, 384], F32); gq = sb("gq", [128, 2], F32)
            uq = sb("uq", [128, 2, 384], BF16); uqs = sb("uqs", [128, 2, 384], BF16)
            ukvf = sb("ukvf", [128, 512], F32); gkv = sb("gkv", [128, 1], F32)
            ukv = sb("ukv", [128, 512], BF16); wv4 = sb("wv4", [128, 256], BF16)
            u_b = Buf()
            S.dma("sp", uqf[:, :, :], self.mla_w_uq.ap().rearrange("(kc p) n -> p kc n", p=128), writes=[u_b])
            for kc in range(2):
                S.dma("sp", gq[:, kc:kc + 1], self.mla_q_norm[kc * 128:(kc + 1) * 128].rearrange("(p o) -> p o", o=1), writes=[u_b])
            S.dma("sp", ukvf[:, :], self.mla_w_ukv[:, :], writes=[u_b])
            S.dma("sp", gkv[:, :], self.mla_kv_norm.ap().rearrange("(p o) -> p o", o=1), writes=[u_b])
            for kc in range(2):
                S.op("dve", lambda e, kc=kc: e.tensor_scalar(out=uq[:, kc, :], in0=uqf[:, kc, :], scalar1=gq[:, kc:kc + 1],
                                                              scalar2=None, op0=ALU.mult), reads=[u_b], writes=[u_b])
            S.op("dve", lambda e: e.tensor_copy(out=uqs[:, :, :], in_=uq[:, :, :]), reads=[u_b], writes=[u_b])
            for h in range(4):
                S.op("dve", lambda e, h=h: e.tensor_copy(out=uqs[:, :, h * 96 + 64:h * 96 + 80], in_=uq[:, :, h * 96 + 80:h * 96 + 96]),
                     reads=[u_b], writes=[u_b])
                S.op("dve", lambda e, h=h: e.tensor_copy(out=uqs[:, :, h * 96 + 80:h * 96 + 96], in_=uq[:, :, h * 96 + 64:h * 96 + 80]),
                     reads=[u_b], writes=[u_b])
            S.op("dve", lambda e: e.tensor_scalar(out=ukv[:, :], in0=ukvf[:, :], scalar1=gkv[:, 0:1], scalar2=None, op0=ALU.mult),
                 reads=[u_b], writes=[u_b])
            for h in range(4):
                S.op("dve", lambda e, h=h: e.tensor_copy(out=wv4[:, h * 64:(h + 1) * 64], in_=ukv[:, h * 128 + 64:h * 128 + 128]),
                     reads=[u_b], writes=[u_b])
            identf = sb("idf", [128, 128], F32); ident = sb("idb", [128, 128], BF16); id_b = Buf()
            S.dma("sp", identf[:, :], self.ident_f[:, :], writes=[id_b])
            S.op("dve", lambda e: e.tensor_copy(out=ident[:, :], in_=identf[:, :]), reads=[id_b], writes=[id_b])
            eps6 = sb("eps6", [128, 1], F32); eps_b = Buf()
            S.op("dve", lambda e: e.memset(eps6[:, :], 1e-6), writes=[eps_b])
            kT = [sb("kT%d" % h, [96, SEQ], BF16) for h in range(4)]; kT_b = Buf()
            qT = [sb("qT%d" % h, [96, T], BF16) for h in range(4)]; qT_b = Buf()
            vx = sb("vx", [128, SEQ // 128, 260], BF16); vx_b = Buf()
            S.op("pool", lambda e: e.memset(vx[:, :, :], 1.0), writes=[vx_b])
            hc = [sb("hc%d" % i, [128, 8, 512], BF16) for i in range(2)]; hc_b = [Buf() for _ in range(2)]
            rt = [sb("rt%d" % i, [96, 2, 512], F32) for i in range(2)]; rt_b = [Buf() for _ in range(2)]
            junk = sb("junk", [128, 256], F32); junk_b = Buf()
            ss = [sb("ss%d" % i, [128, 3], F32) for i in range(2)]; ss_b = [Buf() for _ in range(2)]
            cn = [sb("cn%d" % i, [128, 256], BF16) for i in range(2)]; cn_b = [Buf() for _ in range(2)]
            cnT = [sb("cnT%d" % i, [128, 2, 512], BF16) for i in range(2)]; cnT_b = [Buf() for _ in range(2)]
            t1 = [sb("t1%d" % i, [96, 512], F32) for i in range(2)]; t1_b = [Buf() for _ in range(2)]
            t2 = [sb("t2%d" % i, [96, 512], F32) for i in range(2)]; t2_b = [Buf() for _ in range(2)]
            with ExitStack() as es2:
                NP = 6
                pp = [es2.enter_context(self.pst("ml_pp%d" % i, [128, 512], F32)) for i in range(NP)]
                pp_b = [Buf() for _ in range(NP)]
                psT = [es2.enter_context(self.pst("ml_psT%d" % i, [128, 2, 128], BF16)) for i in range(2)]
                psT_b = [Buf() for _ in range(2)]
                ctr = [0]
                def nextp():
                    i = ctr[0] % NP; ctr[0] += 1
                    return i

                def norm_tile(hci, s, wt, ncol, nfeat, si):
                    i = nextp()
                    def f(e):
                        for kc in range(8):
                            ins = e.matmul(pp[i][:, 0:ncol], lhsT=hc[hci][:, kc, s * 128:(s + 1) * 128], rhs=wt[:, kc, 0:ncol],
                                           start=(kc == 0), stop=(kc == 7))
                        return ins
                    S.op("pe", f, reads=[w_b, hc_b[hci]], writes=[pp_b[i]])
                    S.op("act", lambda e: e.activation(out=junk[:, 0:ncol], in_=pp[i][:, 0:ncol], func=AF.Square, accum_out=ss[si][:, 0:1]),
                         reads=[pp_b[i]], writes=[junk_b, ss_b[si]])
                    S.op("act", lambda e: e.activation(out=ss[si][:, 1:2], in_=ss[si][:, 0:1], func=AF.Sqrt, bias=eps6[:, 0:1], scale=1.0 / nfeat),
                         reads=[ss_b[si], eps_b], writes=[ss_b[si]])
                    S.op("dve", lambda e: e.reciprocal(out=ss[si][:, 2:3], in_=ss[si][:, 1:2]), reads=[ss_b[si]], writes=[ss_b[si]])
                    S.op("dve", lambda e: e.tensor_scalar(out=cn[si][:, 0:ncol], in0=pp[i][:, 0:ncol], scalar1=ss[si][:, 2:3], scalar2=None, op0=ALU.mult),
                         reads=[pp_b[i], ss_b[si]], writes=[cn_b[si]])

                def rope_rows(pa, pb, ri, dsts, dst_b, c0, ti):
                    S.op("dve", lambda e: e.tensor_tensor(out=t1[ti][64:96, :], in0=pp[pa][64:96, :], in1=rt[ri][64:96, 0, :], op=ALU.mult),
                         reads=[pp_b[pa], rt_b[ri]], writes=[t1_b[ti]])
                    S.op("dve", lambda e: e.tensor_tensor(out=t2[ti][64:96, :], in0=pp[pb][64:96, :], in1=rt[ri][64:96, 1, :], op=ALU.mult),
                         reads=[pp_b[pb], rt_b[ri]], writes=[t2_b[ti]])
                    for d in dsts:
                        S.op("dve", lambda e, d=d: e.tensor_tensor(out=d[64:96, c0:c0 + 512], in0=t1[ti][64:96, :], in1=t2[ti][64:96, :], op=ALU.add),
                             reads=[t1_b[ti], t2_b[ti]], writes=[dst_b])

                for c in range(SEQ // 512):
                    hci = c % 2
                    S.dma("sp", hc[hci][:, :, :], self.G_v[c // 8][:, :, (c % 8) * 512:(c % 8) * 512 + 512], reads=[self.B("G")], writes=[hc_b[hci]])
                    S.dma("sp", rt[hci][64:96, :, :], self.ropek[:, :, c * 512:(c + 1) * 512].rearrange("a p t -> p a t"),
                          reads=[self.B("ropek")], writes=[rt_b[hci]])
                    for s in range(4):
                        si = (c * 4 + s) % 2
                        norm_tile(hci, s, wkv, 128, 128.0, si)
                        def ftr(e, si=si):
                            return e.transpose(out=psT[si][:, 0, :], in_=cn[si][:, 0:128], identity=ident[:, :])
                        S.op("pe", ftr, reads=[cn_b[si], id_b], writes=[psT_b[si]])
                        S.op("dve", lambda e, si=si, s=s, hci=hci: e.tensor_copy(out=cnT[hci][:, 0, s * 128:(s + 1) * 128], in_=psT[si][:, 0, :]),
                             reads=[psT_b[si]], writes=[cnT_b[hci]])
                        i = nextp()
                        S.op("pe", lambda e, i=i, s=s, hci=hci: e.matmul(pp[i][:, 0:256], lhsT=cnT[hci][:, 0, s * 128:(s + 1) * 128], rhs=wv4[:, :],
                                                                         start=True, stop=True), reads=[cnT_b[hci], u_b], writes=[pp_b[i]])
                        S.op("dve", lambda e, i=i, blk=c * 4 + s: e.tensor_copy(
                            out=vx[:, blk, :].rearrange("p (h d) -> p h d", d=65)[:, :, 0:64],
                            in_=pp[i][:, 0:256].rearrange("p (h d) -> p h d", d=64)), reads=[pp_b[i]], writes=[vx_b])
                    for h in range(4):
                        i = nextp()
                        S.op("pe", lambda e, i=i, h=h, hci=hci: e.matmul(pp[i][0:64, :], lhsT=ukv[:, h * 128:h * 128 + 64], rhs=cnT[hci][:, 0, :],
                                                                         start=True, stop=True), reads=[cnT_b[hci], u_b], writes=[pp_b[i]])
                        S.op("act", lambda e, i=i, h=h, c=c: e.activation(out=kT[h][0:64, c * 512:(c + 1) * 512], in_=pp[i][0:64, :], func=AF.Copy),
                             reads=[pp_b[i]], writes=[kT_b])
                    pa, pb = nextp(), nextp()
                    for (pi, wt) in ((pa, wkr), (pb, wkrs)):
                        def f(e, pi=pi, wt=wt, hci=hci):
                            for kc in range(8):
                                ins = e.matmul(pp[pi][0:96, :], lhsT=wt[:, kc, :], rhs=hc[hci][:, kc, :], start=(kc == 0), stop=(kc == 7))
                            return ins
                        S.op("pe", f, reads=[w_b, hc_b[hci]], writes=[pp_b[pi]])
                    rope_rows(pa, pb, hci, kT, kT_b, c * 512, hci)
                for c in range(T // 512):
                    hci = c % 2
                    S.dma("sp", hc[hci][:, :, :], self.hTo_v[:, :, c * 512:(c + 1) * 512], reads=[self.B("hTo")], writes=[hc_b[hci]])
                    S.dma("sp", rt[hci][64:96, :, :], self.ropeq[:, :, c * 512:(c + 1) * 512].rearrange("a p t -> p a t"),
                          reads=[self.B("ropeq")], writes=[rt_b[hci]])
                    for s in range(4):
                        si = (c * 4 + s) % 2
                        norm_tile(hci, s, wq, 256, 256.0, si)
                        def ftr(e, si=si):
                            e.transpose(out=psT[si][:, 0, :], in_=cn[si][:, 0:128], identity=ident[:, :])
                            return e.transpose(out=psT[si][:, 1, :], in_=cn[si][:, 128:256], identity=ident[:, :])
                        S.op("pe", ftr, reads=[cn_b[si], id_b], writes=[psT_b[si]])
                        S.op("dve", lambda e, si=si, s=s, hci=hci: e.tensor_copy(out=cnT[hci][:, :, s * 128:(s + 1) * 128], in_=psT[si][:, :, :]),
                             reads=[psT_b[si]], writes=[cnT_b[hci]])
                    for h in range(4):
                        pa, pb = nextp(), nextp()
                        for (pi, wt) in ((pa, uq), (pb, uqs)):
                            def f(e, pi=pi, wt=wt, hci=hci, h=h):
                                for kc in range(2):
                                    ins = e.matmul(pp[pi][0:96, :], lhsT=wt[:, kc, h * 96:(h + 1) * 96], rhs=cnT[hci][:, kc, :],
                                                   start=(kc == 0), stop=(kc == 1))
                                return ins
                            S.op("pe", f, reads=[u_b, cnT_b[hci]], writes=[pp_b[pi]])
                        S.op("act", lambda e, pa=pa, h=h, c=c: e.activation(out=qT[h][0:64, c * 512:(c + 1) * 512], in_=pp[pa][0:64, :], func=AF.Copy),
                             reads=[pp_b[pa]], writes=[qT_b])
                        rope_rows(pa, pb, hci, [qT[h]], qT_b, c * 512, h % 2)
                S.emit()
            NB3 = 3
            ps = [es.enter_context(self.pst("ml_ps%d" % i, [128, 512], F32)) for i in range(NB3)]; ps_b = [Buf() for _ in range(NB3)]
            po = [es.enter_context(self.pst("ml_po%d" % i, [128, 512], F32)) for i in range(4)]; po_b = [Buf() for _ in range(4)]
            et = [sb("et%d" % i, [128, 512], BF16) for i in range(NB3)]; et_b = [Buf() for _ in range(NB3)]
            ost = [[sb("ost%d_%d" % (i, q), [128, 4, 65], F32) for q in range(4)] for i in range(2)]
            ost_b = [[[Buf() for h in range(4)] for q in range(4)] for i in range(2)]
            sc = 96.0 ** -0.5
            NKB = SEQ // 128
            its = [(qc, h, kb) for qc in range(T // 512) for h in range(4) for kb in range(NKB)]

            def emit_st(n):
                qc, h, kb = its[n]
                i = n % NB3
                S.op("pe", lambda e: e.matmul(ps[i][:, :], lhsT=kT[h][0:96, kb * 128:(kb + 1) * 128],
                                              rhs=qT[h][0:96, qc * 512:(qc + 1) * 512], start=True, stop=True),
                     reads=[kT_b, qT_b], writes=[ps_b[i]])

            emit_st(0); emit_st(1)
            for n, (qc, h, kb) in enumerate(its):
                i = n % NB3
                oi = qc % 2
                if n + 2 < len(its):
                    emit_st(n + 2)
                S.op("act", lambda e, i=i: e.activation(out=et[i][:, :], in_=ps[i][:, :], func=AF.Exp, scale=sc),
                     reads=[ps_b[i]], writes=[et_b[i]])
                def g(e, i=i, h=h, kb=kb):
                    for qs in range(4):
                        ins = e.matmul(po[qs][:, 0:65], lhsT=et[i][:, qs * 128:(qs + 1) * 128], rhs=vx[:, kb, h * 65:(h + 1) * 65],
                                       start=(kb == 0), stop=(kb == NKB - 1))
                    return ins
                S.op("pe", g, reads=[et_b[i], vx_b], writes=po_b)
                if kb == NKB - 1:
                    for qs in range(4):
                        S.op("dve", lambda e, qs=qs, h=h, oi=oi: e.tensor_copy(out=ost[oi][qs][:, h, :], in_=po[qs][:, 0:65]),
                             reads=[po_b[qs]], writes=[ost_b[oi][qs][h]])
                    if h == 3:
                        for qs in range(4):
                            r0 = qc * 512 + qs * 128
                            S.dma("sp", self.Og["A"][r0:r0 + 128, :], ost[oi][qs][:, :, :].rearrange("p h d -> p (h d)"),
                                  reads=ost_b[oi][qs], writes=[self.B("OgA")])
            S.emit()

    def phase_merge1(self):
        nc, S = self.nc, self.S
        if not hasattr(self, "mT"):
            self.mT = self.scratch("mT", [D, T], BF16)
        mTv = self.mT.ap().rearrange("(kc p) t -> p kc t", p=128)
        yTv = [y.ap().rearrange("(kc p) t -> p kc t", p=128) for y in self.yT]
        with ExitStack() as es:
            sb = lambda n, sh, dt: es.enter_context(self.sbt("m1_" + n, sh, dt))
            wg = sb("wg", [128, 8, 4096], BF16); wb = sb("wb", [128, 8, 1024], BF16); w_b = Buf()
            self.wload(wg, self.w_in[:, 4000:8096], wb=w_b)
            self.wload(wb, self.w_branch[:, :], wb=w_b)
            hc = [sb("hc%d" % i, [128, 8, 512], BF16) for i in range(2)]; hc_b = [Buf() for _ in range(2)]
            yc = [[sb("yc%d_%d" % (i, n), [128, 2, 512], BF16) for n in range(4)] for i in range(2)]
            yc_b = [[Buf() for n in range(4)] for i in range(2)]
            mt = [sb("mt%d" % i, [128, 8, 512], BF16) for i in range(2)]; mt_b = [Buf() for _ in range(2)]
            gt = [sb("gt%d" % i, [128, 512], F32) for i in range(2)]; gt_b = [Buf() for _ in range(2)]
            acc = [sb("acc%d" % i, [128, 512], F32) for i in range(2)]; acc_b = [Buf() for _ in range(2)]
            tmp = [sb("tmp%d" % i, [128, 512], F32) for i in range(2)]; tmp_b = [Buf() for _ in range(2)]
            pg = [es.enter_context(self.pst("m1_pg%d" % i, [128, 512], F32)) for i in range(2)]; pg_b = [Buf() for _ in range(2)]
            pb = [es.enter_context(self.pst("m1_pb%d" % i, [128, 512], F32)) for i in range(2)]; pb_b = [Buf() for _ in range(2)]
            it = 0
            for c in range(T // 512):
                ci = c % 2
                S.dma("sp", hc[ci][:, :, :], self.hTo_v[:, :, c * 512:(c + 1) * 512], reads=[self.B("hTo")], writes=[hc_b[ci]])
                for n in range(4):
                    S.dma("sp", yc[ci][n][:, :, :], yTv[n][:, :, c * 512:(c + 1) * 512], reads=[self.B(self.yT[n].name)], writes=[yc_b[ci][n]])
                for dc in range(8):
                    ai = (c * 8 + dc) % 2
                    for n in range(4):
                        i = it % 2; it += 1
                        def f(e, i=i, n=n, dc=dc, ci=ci):
                            for kc in range(8):
                                ins = e.matmul(pg[i][:, :], lhsT=wg[:, kc, n * 1024 + dc * 128:n * 1024 + dc * 128 + 128], rhs=hc[ci][:, kc, :],
                                               start=(kc == 0), stop=(kc == 7))
                            return ins
                        S.op("pe", f, reads=[w_b, hc_b[ci]], writes=[pg_b[i]])
                        S.op("act", lambda e, i=i: e.activation(out=gt[i][:, :], in_=pg[i][:, :], func=AF.Sigmoid), reads=[pg_b[i]], writes=[gt_b[i]])
                        def f2(e, i=i, n=n, dc=dc, ci=ci):
                            for kc in range(2):
                                ins = e.matmul(pb[i][:, :], lhsT=wb[:, n * 2 + kc, dc * 128:dc * 128 + 128], rhs=yc[ci][n][:, kc, :],
                                               start=(kc == 0), stop=(kc == 1))
                            return ins
                        S.op("pe", f2, reads=[w_b, yc_b[ci][n]], writes=[pb_b[i]])
                        if n == 0:
                            S.op("dve", lambda e, i=i, ai=ai: e.tensor_tensor(out=acc[ai][:, :], in0=pb[i][:, :], in1=gt[i][:, :], op=ALU.mult),
                                 reads=[pb_b[i], gt_b[i]], writes=[acc_b[ai]])
                        else:
                            S.op("dve", lambda e, i=i: e.tensor_tensor(out=tmp[i][:, :], in0=pb[i][:, :], in1=gt[i][:, :], op=ALU.mult),
                                 reads=[pb_b[i], gt_b[i]], writes=[tmp_b[i]])
                            if n < 3:
                                S.op("pool", lambda e, i=i, ai=ai: e.tensor_tensor(out=acc[ai][:, :], in0=acc[ai][:, :], in1=tmp[i][:, :], op=ALU.add),
                                     reads=[acc_b[ai], tmp_b[i]], writes=[acc_b[ai]])
                            else:
                                S.op("pool", lambda e, i=i, ai=ai, dc=dc, ci=ci: e.tensor_tensor(out=mt[ci][:, dc, :], in0=acc[ai][:, :], in1=tmp[i][:, :], op=ALU.add),
                                     reads=[acc_b[ai], tmp_b[i]], writes=[mt_b[ci]])
                S.dma("pool", mTv[:, :, c * 512:(c + 1) * 512], mt[ci][:, :, :], reads=[mt_b[ci]], writes=[self.B("mT")])
            S.emit()

    def phase_merge2(self):
        nc, S = self.nc, self.S
        mTv = self.mT.ap().rearrange("(kc p) t -> p kc t", p=128)
        h1Tv = self.h1T.ap().rearrange("(kc p) t -> p kc t", p=128)
        with ExitStack() as es:
            sb = lambda n, sh, dt: es.enter_context(self.sbt("m2_" + n, sh, dt))
            c = self.ln_consts(es, self.ln1_g.ap(), self.ln1_b.ap(), "m2_")
            wo = sb("wo", [128, 8, 1024], BF16); wpg = sb("wpg", [128, 8, 1024], BF16); wpl = sb("wpl", [128, 2, 1024], BF16); w_b = Buf()
            self.wload(wo, self.w_out[:, :], wb=w_b); self.wload(wpg, self.w_ple_gate[:, :], wb=w_b); self.wload(wpl, self.w_ple[:, :], wb=w_b)
            mc = [sb("mc%d" % i, [128, 8, 512], BF16) for i in range(2)]; mc_b = [Buf() for _ in range(2)]
            ha = [sb("ha%d" % i, [128, D], F32) for i in range(2)]; ha_b = [Buf() for _ in range(2)]
            r1 = [sb("r1%d" % i, [128, D], F32) for i in range(2)]; r1_b = [Buf() for _ in range(2)]
            h1 = [sb("h1%d" % i, [128, D], F32) for i in range(2)]; h1_b = [Buf() for _ in range(2)]
            hb = [sb("hb%d" % i, [128, D], BF16) for i in range(2)]; hb_b = [Buf() for _ in range(2)]
            hts = [sb("hts%d" % i, [128, 8, 512], BF16) for i in range(2)]; hts_b = [Buf() for _ in range(2)]
            pt = [sb("pt%d" % i, [128, 256], F32) for i in range(2)]; pt_b = [Buf() for _ in range(2)]
            ptb = [sb("ptb%d" % i, [128, 256], BF16) for i in range(2)]; ptb_b = [Buf() for _ in range(2)]
            pT = [sb("pT%d" % i, [128, 2, 128], BF16) for i in range(2)]; pT_b = [Buf() for _ in range(2)]
            sg = [sb("sg%d" % i, [128, D], F32) for i in range(2)]; sg_b = [Buf() for _ in range(2)]
            ps = lambda n, sh, dt: es.enter_context(self.pst("m2_" + n, sh, dt))
            po = [ps("po%d" % i, [128, 512], F32) for i in range(2)]; po_b = [Buf() for _ in range(2)]
            pq = [ps("pq%d" % i, [128, 512], F32) for i in range(2)]; pq_b = [Buf() for _ in range(2)]
            pp = [ps("pp%d" % i, [128, 512], F32) for i in range(2)]; pp_b = [Buf() for _ in range(2)]
            psT = [ps("psT", [128, 8, 128], BF16)]; psT_b = [Buf()]
            psP = [ps("psP", [128, 2, 128], BF16)]; psP_b = [Buf()]
            for tt in range(T // 128):
                i = tt % 2
                cc, s = tt // 4, tt % 4
                ci = cc % 2
                if s == 0:
                    S.dma("sp", mc[ci][:, :, :], mTv[:, :, cc * 512:(cc + 1) * 512], reads=[self.B("mT")], writes=[mc_b[ci]])
                S.dma("sp", ha[i][:, :], self.hA[tt * 128:(tt + 1) * 128, :], reads=[self.B("hA")], writes=[ha_b[i]])
                S.dma("sp", pt[i][:, :], self.p[tt * 128:(tt + 1) * 128, :], reads=[self.B("p")], writes=[pt_b[i]])
                for hf in range(2):
                    def f(e, hf=hf, ci=ci, s=s):
                        for dc in range(8):
                            ins = e.matmul(po[hf][:, :], lhsT=mc[ci][:, dc, s * 128:(s + 1) * 128], rhs=wo[:, dc, hf * 512:(hf + 1) * 512],
                                           start=(dc == 0), stop=(dc == 7))
                        return ins
                    S.op("pe", f, reads=[w_b, mc_b[ci]], writes=[po_b[hf]])
                    S.op("dve", lambda e, hf=hf, i=i: e.scalar_tensor_tensor(out=r1[i][:, hf * 512:(hf + 1) * 512], in0=ha[i][:, hf * 512:(hf + 1) * 512],
                                                                              scalar=ALPHA, in1=po[hf][:, :], op0=ALU.mult, op1=ALU.add),
                         reads=[ha_b[i], po_b[hf]], writes=[r1_b[i]])
                self.ln_ops("m2", r1[i], r1_b[i], h1[i], h1_b[i], c["g"], c["bb"], c["tmp"][i])
                S.op("act", lambda e, i=i: e.activation(out=hb[i][:, :], in_=h1[i][:, :], func=AF.Copy), reads=[h1_b[i]], writes=[hb_b[i]])
                self.transpose_ops(hb[i], hb_b[i], psT[0], psT_b[0], hts[ci], hts_b[ci], s * 128, c["ident"], c["ident_b"])
                S.op("act", lambda e, i=i: e.activation(out=ptb[i][:, :], in_=pt[i][:, :], func=AF.Copy), reads=[pt_b[i]], writes=[ptb_b[i]])
                self.transpose_ops(ptb[i], ptb_b[i], psP[0], psP_b[0], pT[i], pT_b[i], 0, c["ident"], c["ident_b"], nk=2)
                for hf in range(2):
                    def f(e, hf=hf, ci=ci, s=s):
                        for kc in range(8):
                            ins = e.matmul(pq[hf][:, :], lhsT=hts[ci][:, kc, s * 128:(s + 1) * 128], rhs=wpg[:, kc, hf * 512:(hf + 1) * 512],
                                           start=(kc == 0), stop=(kc == 7))
                        return ins
                    S.op("pe", f, reads=[w_b, hts_b[ci]], writes=[pq_b[hf]])
                    S.op("act", lambda e, hf=hf, i=i: e.activation(out=sg[i][:, hf * 512:(hf + 1) * 512], in_=pq[hf][:, :], func=AF.Sigmoid),
                         reads=[pq_b[hf]], writes=[sg_b[i]])
                    def f2(e, hf=hf, i=i):
                        for kc in range(2):
                            ins = e.matmul(pp[hf][:, :], lhsT=pT[i][:, kc, :], rhs=wpl[:, kc, hf * 512:(hf + 1) * 512], start=(kc == 0), stop=(kc == 1))
                        return ins
                    S.op("pe", f2, reads=[w_b, pT_b[i]], writes=[pp_b[hf]])
                    S.op("dve", lambda e, hf=hf, i=i: e.tensor_tensor(out=sg[i][:, hf * 512:(hf + 1) * 512], in0=pp[hf][:, :], in1=sg[i][:, hf * 512:(hf + 1) * 512], op=ALU.mult),
                         reads=[pp_b[hf], sg_b[i]], writes=[sg_b[i]])
                S.op("dve", lambda e, i=i: e.scalar_tensor_tensor(out=sg[i][:, :], in0=h1[i][:, :], scalar=ALPHA, in1=sg[i][:, :], op0=ALU.mult, op1=ALU.add),
                     reads=[h1_b[i], sg_b[i]], writes=[sg_b[i]])
                S.dma("pool", self.r2[tt * 128:(tt + 1) * 128, :], sg[i][:, :], reads=[sg_b[i]], writes=[self.B("r2")])
                if s == 3:
                    S.dma("pool", h1Tv[:, :, cc * 512:(cc + 1) * 512], hts[ci][:, :, :], reads=[hts_b[ci]], writes=[self.B("h1T")])
            S.emit()

    def phase_ffn(self):
        nc, S = self.nc, self.S
        h1Tv = self.h1T.ap().rearrange("(kc p) t -> p kc t", p=128)
        hTov = self.hT_o.ap().rearrange("(kc p) t -> p kc t", p=128)
        with ExitStack() as es:
            sb = lambda n, sh, dt: es.enter_context(self.sbt("ff_" + n, sh, dt))
            c = self.ln_consts(es, self.ln2_g.ap(), self.ln2_b.ap(), "ff_")
            w1 = sb("w1", [128, 8, 4096], BF16); w2 = sb("w2", [128, 32, 1024], BF16); w_b = Buf()
            self.wload(w1, self.w_ff1[:, :], wb=w_b); self.wload(w2, self.w_ff2[:, :], wb=w_b)
            hc = [sb("hc%d" % i, [128, 8, 256], BF16) for i in range(2)]; hc_b = [Buf() for _ in range(2)]
            uT = sb("uT", [128, 32, 256], BF16); uT_b = Buf()
            rl = [sb("rl%d" % i, [128, 256], F32) for i in range(2)]; rl_b = [Buf() for _ in range(2)]
            rt = [sb("rt%d" % i, [128, D], F32) for i in range(2)]; rt_b = [Buf() for _ in range(2)]
            ot = [sb("ot%d" % i, [128, D], F32) for i in range(2)]; ot_b = [Buf() for _ in range(2)]
            ob = [sb("ob%d" % i, [128, D], BF16) for i in range(2)]; ob_b = [Buf() for _ in range(2)]
            hts = [sb("hts%d" % i, [128, 8, 256], BF16) for i in range(2)]; hts_b = [Buf() for _ in range(2)]
            ps = lambda n, sh, dt: es.enter_context(self.pst("ff_" + n, sh, dt))
            pu = [ps("pu%d" % i, [128, 512], F32) for i in range(2)]; pu_b = [Buf() for _ in range(2)]
            po = [ps("po%d" % i, [128, 512], F32) for i in range(2)]; po_b = [Buf() for _ in range(2)]
            psT = [ps("psT%d" % i, [128, 8, 128], BF16) for i in range(2)]; psT_b = [Buf() for _ in range(2)]
            it = 0
            for c2 in range(T // 256):
                ci = c2 % 2
                S.dma("sp", hc[ci][:, :, :], h1Tv[:, :, c2 * 256:(c2 + 1) * 256], reads=[self.B("h1T")], writes=[hc_b[ci]])
                for fc in range(32):
                    i = it % 2; it += 1
                    def f(e, i=i, fc=fc, ci=ci):
                        for kc in range(8):
                            ins = e.matmul(pu[i][:, 0:256], lhsT=w1[:, kc, fc * 128:(fc + 1) * 128], rhs=hc[ci][:, kc, :], start=(kc == 0), stop=(kc == 7))
                        return ins
                    S.op("pe", f, reads=[w_b, hc_b[ci]], writes=[pu_b[i]])
                    S.op("act", lambda e, i=i: e.activation(out=rl[i][:, :], in_=pu[i][:, 0:256], func=AF.Relu), reads=[pu_b[i]], writes=[rl_b[i]])
                    S.op("dve", lambda e, i=i, fc=fc: e.tensor_tensor(out=uT[:, fc, :], in0=rl[i][:, :], in1=rl[i][:, :], op=ALU.mult),
                         reads=[rl_b[i]], writes=[uT_b])
                for s in range(2):
                    tt = c2 * 2 + s
                    i = tt % 2
                    S.dma("sp", rt[i][:, :], self.r2[tt * 128:(tt + 1) * 128, :], reads=[self.B("r2")], writes=[rt_b[i]])
                    for hf in range(2):
                        def f(e, hf=hf, s=s):
                            for fc in range(32):
                                ins = e.matmul(po[hf][:, :], lhsT=uT[:, fc, s * 128:(s + 1) * 128], rhs=w2[:, fc, hf * 512:(hf + 1) * 512],
                                               start=(fc == 0), stop=(fc == 31))
                            return ins
                        S.op("pe", f, reads=[w_b, uT_b], writes=[po_b[hf]])
                        S.op("dve", lambda e, hf=hf, i=i: e.tensor_tensor(out=rt[i][:, hf * 512:(hf + 1) * 512], in0=po[hf][:, :], in1=rt[i][:, hf * 512:(hf + 1) * 512], op=ALU.add),
                             reads=[po_b[hf], rt_b[i]], writes=[rt_b[i]])
                    self.ln_ops("ff", rt[i], rt_b[i], ot[i], ot_b[i], c["g"], c["bb"], c["tmp"][i])
                    S.dma("pool", self.hA_o[tt * 128:(tt + 1) * 128, :], ot[i][:, :], reads=[ot_b[i]], writes=[self.B("hA_out")])
                    S.op("act", lambda e, i=i: e.activation(out=ob[i][:, :], in_=ot[i][:, :], func=AF.Copy), reads=[ot_b[i]], writes=[ob_b[i]])
                    self.transpose_ops(ob[i], ob_b[i], psT[i], psT_b[i], hts[ci], hts_b[ci], s * 128, c["ident"], c["ident_b"])
                S.dma("pool", hTov[:, :, c2 * 256:(c2 + 1) * 256], hts[ci][:, :, :], reads=[hts_b[ci]], writes=[self.B("hT_out")])
            S.emit()


NEG = -30000.0


def _t5_bucket(rel):
    nb, max_exact = 16, 8
    n = np.abs(rel)
    large = max_exact + (np.log(np.maximum(n, 1).astype(np.float32) / max_exact)
                         / math.log(1024 / max_exact) * (nb - max_exact)).astype(np.int32)
    large = np.minimum(large, nb - 1)
    return np.where(rel > 0, nb, 0) + np.where(n < max_exact, n, large)


def _banded_tab(rel_bias, heads, r, half, kb0, dms, hf):
    L = T // r
    nb = L // 128
    k = np.arange(128)[:, None]; q = np.arange(128)[None, :]
    out = []
    for h in heads:
        for mode in range(3):
            b = {0: 1, 1: 0, 2: nb - 1}[mode]
            if nb == 2 and mode == 0:
                b = 0
            for dm in dms:
                m = b + dm
                f = kb0 + 128 * m + k
                rel = f - (128 * b + q)
                val = rel_bias[_t5_bucket(rel * r), h].astype(np.float32)
                ok = np.abs(rel) <= half
                if mode == 1 and hf == 0:
                    ok = ok & (f >= 0)
                if mode == 2 and hf == 1:
                    ok = ok & (f < L)
                if mode == 0:
                    pass
                out.append(np.where(ok, val, NEG).astype(np.float32))
    return np.stack(out)


def _na_tab(rpb, hf):
    k = np.arange(128)[:, None]; q = np.arange(128)[None, :]
    a_loc, kc = k // 64, k % 64
    r_loc, c = q // 64, q % 64
    col_start = np.clip(c - 8, 0, 48)
    col_ok = (kc >= col_start) & (kc < col_start + 16)
    dc = np.clip(kc - c, -15, 15) + 15

    def tile(h, i, j):
        if j < 0 or j >= 64:
            return np.full((128, 128), NEG, np.float32)
        r = 2 * i + r_loc; a = 2 * j + a_loc
        start = np.clip(r - 4, 0, 120)
        ok = (a >= start) & (a < start + 8) & col_ok
        dr = np.clip(a - r + 7, 0, 14)
        return np.where(ok, rpb[h, dr, dc], NEG).astype(np.float32)
    out = []
    for h in range(4):
        for dj in range(-2, 3):
            out.append(tile(h, 10, 10 + dj))
        for b in (0, 1, 30, 31):
            i = hf * 32 + b
            for dj in range(-3, 4):
                out.append(tile(h, i, i + dj))
    return np.stack(out)


def _rope_tab(pos):
    half = 16
    inv = (10000.0 ** (-np.arange(half, dtype=np.float32) / half)).astype(np.float32)
    ang = pos.astype(np.float32)[None, :] * inv[:, None]
    cos, sin = np.cos(ang).astype(np.float32), np.sin(ang).astype(np.float32)
    return np.stack([np.concatenate([cos, cos], 0), np.concatenate([-sin, sin], 0)]).astype(np.float32)


_PROGS = {}


def build_A():
    k = K()
    k.declare_A()
    k.phase_ln_emb()
    k.finish([k.B("hA_out"), k.B("hT_out")])
    return k


def build_B(debug=()):
    k = K(debug=debug)
    k.declare_B()
    edge = {0: 0, 1: 1, 30: 2, 31: 3}
    k.phase_mla()
    k.phase_combine("ca_", [k.Og["A"]], k.yT[0])
    k.phase_banded("sw_", 416, 672, 800, 2, 1, 128, lambda b, nb: ([b, b + 1, b + 2], 3 if b == 0 else (6 if b == nb - 1 else 0)),
                   9, k.tabB, k.Og["B"], True)
    k.phase_combine("cb_", [k.Og["B"]], k.yT[1], sink=True)
    k.phase_banded("na_", 928, 1184, 1440, 4, 1, 384,
                   lambda b, nb: (list(range(b, b + 7)), 5 + 7 * edge[b]) if b in edge else (list(range(b + 1, b + 6)), 0),
                   33, k.tabC, k.Og["C"], False)
    k.phase_combine("cc_", [k.Og["C"]], k.yT[2])
    for g, r in enumerate((1, 4, 16)):
        k.phase_banded("d%d_" % g, 1696 + g * 256, 2464 + g * 256, 3232 + g * 256, 4, r, 64,
                       lambda b, nb: ([b, b + 1], 2 if b == 0 else (4 if b == nb - 1 else 0)), 6, k.tabD[g], k.Og["D%d" % g], False)
    k.phase_combine("cd_", [k.Og["D0"], k.Og["D1"], k.Og["D2"]], k.yT[3])
    k.phase_merge1()
    k.phase_merge2()
    k.phase_ffn()
    outs = [k.B("hA_out"), k.B("hT_out")]
    if debug:
        outs += k.dump(debug)
    k.finish(outs)
    return k


def _f(a):
    return np.ascontiguousarray(np.asarray(a, dtype=np.float32))


def build_F():
    k = K()
    k.declare_F()
    k.phase_ln_emb()
    edge = {0: 0, 1: 1, 30: 2, 31: 3}
    for l in range(NL):
        k.allgather(l)
        k.set_layer(l)
        k.phase_mla()
        k.phase_combine("ca_", [k.Og["A"]], k.yT[0])
        k.phase_banded("sw_", 416, 672, 800, 2, 1, 128, lambda b, nb: ([b, b + 1, b + 2], 3 if b == 0 else (6 if b == nb - 1 else 0)),
                       9, k.tabB, k.Og["B"], True)
        k.phase_combine("cb_", [k.Og["B"]], k.yT[1], sink=True)
        k.phase_banded("na_", 928, 1184, 1440, 4, 1, 384,
                       lambda b, nb: (list(range(b, b + 7)), 5 + 7 * edge[b]) if b in edge else (list(range(b + 1, b + 6)), 0),
                       33, k.tabC, k.Og["C"], False)
        k.phase_combine("cc_", [k.Og["C"]], k.yT[2])
        for g, r in enumerate((1, 4, 16)):
            k.phase_banded("d%d_" % g, 1696 + g * 256, 2464 + g * 256, 3232 + g * 256, 4, r, 64,
                           lambda b, nb: ([b, b + 1], 2 if b == 0 else (4 if b == nb - 1 else 0)), 6, k.tabD[g], k.Og["D%d" % g], False)
        k.phase_combine("cd_", [k.Og["D0"], k.Og["D1"], k.Og["D2"]], k.yT[3])
        k.phase_merge1()
        k.phase_merge2()
        k.phase_ffn()
    k.finish([k.B("out")])
    return k


def kernel(**inputs):
    x = _f(inputs["x"]); p = _f(inputs["p"])
    rel_bias = _f(inputs["rel_bias"]); na_rpb = _f(inputs["na_rpb"])
    ident = np.eye(128, dtype=np.float32)
    if "F" not in _PROGS:
        _PROGS["F"] = build_F()
    ropek = _rope_tab(np.arange(SEQ))
    shared = {"ident": ident, "ln_emb_g": _f(inputs["ln_emb_g"]), "ln_emb_b": _f(inputs["ln_emb_b"]), "ropek": ropek}
    for l in range(NL):
        for n in ("w_in", "mla_q_norm", "mla_w_uq", "mla_kv_norm", "mla_w_ukv", "swa_sink", "w_out", "ln1_g", "ln1_b",
                  "w_ff1", "w_ff2", "w_ple", "w_ple_gate", "ln2_g", "ln2_b"):
            shared["%s_%d" % (n, l)] = _f(inputs[n][l])
        shared["w_branch_%d" % l] = _f(inputs["w_branch"][l]).reshape(4 * 256, D)
    maps = []
    for c in range(8):
        b, hf = c // 2, c % 2
        m = dict(shared)
        m["x"] = np.ascontiguousarray(x[b, hf * T:(hf + 1) * T])
        m["ropeq"] = _rope_tab(hf * T + np.arange(T))
        m["tabB"] = _banded_tab(rel_bias, range(4), 1, 128, -128, (0, 1, 2), hf)
        for g, r in enumerate((1, 4, 16)):
            m["tabD%d" % g] = _banded_tab(rel_bias, range(4 + 4 * g, 8 + 4 * g), r, 64, -64, (0, 1), hf)
        for l in range(NL):
            m["p_%d" % l] = np.ascontiguousarray(p[l, b, hf * T:(hf + 1) * T])
            m["tabC_%d" % l] = _na_tab(na_rpb[l], hf)
        maps.append(m)
    res = run_bass_kernel_spmd(_PROGS["F"].nc, maps, core_ids=list(range(8)))
    out = np.zeros((4, SEQ, D), np.float32)
    for c in range(8):
        b, hf = c // 2, c % 2
        out[b, hf * T:(hf + 1) * T] = np.asarray(res.results[c]["out"], dtype=np.float32)
    return out
```

```python
import math
from contextlib import ExitStack

import numpy as np
import concourse.bass as bass
import concourse.mybir as mybir
from concourse.bass_utils import run_bass_kernel_spmd

F32 = mybir.dt.float32
BF16 = mybir.dt.bfloat16
AF = mybir.ActivationFunctionType
ALU = mybir.AluOpType

D = 1024
T = 4096
SEQ = 8192
NL = 2
ALPHA = (2 * NL) ** 0.25
IN_COLS = 8096
NDMA_SEM = 8


class Buf:
    def __init__(self, name="", multi=False):
        self.name = name
        self.multi = multi
        self.writers = {}
        self.readers = {}


class Op:
    __slots__ = ("eng", "emit", "deps", "needed", "sem", "val", "is_dma", "key", "idx")


class Sched:
    ENGS = ("pe", "act", "dve", "pool", "sp")

    def __init__(self, nc, es):
        self.nc = nc
        self.sem = {e: nc.alloc_semaphore(name="s_" + e) for e in self.ENGS}
        self.dsem = {e: [nc.alloc_semaphore(name="d_%s%d" % (e, i)) for i in range(NDMA_SEM)]
                     for e in ("sp", "pool", "act")}
        self.all_sems = list(self.sem.values()) + [x for v in self.dsem.values() for x in v]
        self.cnt = {e: 0 for e in self.ENGS}
        self.dcnt = {e: 0 for e in self.dsem}
        self.dlast = {e: [None] * NDMA_SEM for e in self.dsem}
        self.waited = {e: {} for e in self.ENGS}
        self.ops = {e: [] for e in self.ENGS}
        self.same_engine_sync = True
        self.nops = 0

    def _mk(self, eng, emit, reads, writes, is_dma):
        op = Op()
        op.eng, op.emit, op.is_dma, op.needed = eng, emit, is_dma, False
        op.sem = op.val = None
        op.idx = self.nops
        self.nops += 1
        deps = {}

        def add(o):
            if o is None:
                return
            if (not o.is_dma) and o.eng == eng:
                if eng == "pe" or not self.same_engine_sync:
                    return
            deps[id(o)] = o

        for b in reads:
            for o in b.writers.values():
                add(o)
        for b in writes:
            for o in b.readers.values():
                add(o)
            for o in b.writers.values():
                add(o)
        if is_dma:
            k = self.dcnt[eng]
            slot = k % NDMA_SEM
            self.dcnt[eng] = k + 1
            op.sem = self.dsem[eng][slot]
            op.val = 16 * (k // NDMA_SEM + 1)
            op.key = ("d", eng, slot)
            add(self.dlast[eng][slot])
            self.dlast[eng][slot] = op
            op.needed = True
        else:
            op.key = ("e", eng)
        op.deps = list(deps.values())
        for o in op.deps:
            o.needed = True
        for b in reads:
            b.readers[op.key] = op
        for b in writes:
            if b.multi:
                b.writers[op.key] = op
            else:
                b.writers = {op.key: op}
                b.readers = {}
        self.ops[eng].append(op)
        return op

    def op(self, eng, emit, reads=(), writes=()):
        return self._mk(eng, emit, reads, writes, False)

    def dma(self, eng, out, in_, reads=(), writes=()):
        return self._mk(eng, lambda e: e.dma_start(out=out, in_=in_), reads, writes, True)

    def collective(self, emit, sem, val, reads=(), writes=()):
        op = self._mk("pool", emit, reads, writes, False)
        op.is_dma = True
        op.needed = True
        op.sem, op.val = sem, val
        op.key = ("cc", id(sem))
        for b in reads:
            b.readers.pop(("e", "pool"), None)
            b.readers[op.key] = op
        for b in writes:
            b.writers = {op.key: op}
        return op

    def emit(self):
        nc = self.nc
        for e in self.ENGS:
            for op in self.ops[e]:
                if not op.is_dma and op.needed and op.sem is None:
                    self.cnt[e] += 1
                    op.sem = self.sem[e]
                    op.val = self.cnt[e]
        with nc.Block() as block:
            hooks = {"pe": block.tensor, "act": block.scalar, "dve": block.vector,
                     "pool": block.gpsimd, "sp": block.sync}
            for e in self.ENGS:
                ops = self.ops[e]
                if not ops:
                    continue
                waited = self.waited[e]

                def body(eng, ops=ops, waited=waited):
                    for op in ops:
                        for d in op.deps:
                            if d.val is None:
                                continue
                            k = id(d.sem)
                            if waited.get(k, 0) < d.val:
                                eng.wait_ge(d.sem, d.val)
                                waited[k] = d.val
                        inst = op.emit(eng)
                        if op.key[0] == "cc":
                            inst.then_inc(op.sem)
                        elif op.is_dma:
                            inst.then_inc(op.sem, 16)
                        elif op.needed:
                            inst.then_inc(op.sem, 1)

                hooks[e](body)
        self.ops = {e: [] for e in self.ENGS}

    def final_wait(self, eng, bufs):
        self._mk(eng, lambda e: e.nop(), bufs, (), False)


def _bcast_rows(ap1d, n=128):
    return ap1d.partition_broadcast(n)


class K:
    def __init__(self, debug=()):
        self.debug = set(debug)
        self.nc = bass.Bass("TRN2", target_bir_lowering=False)
        self.es = ExitStack()
        self.cc_sems = [self.nc.alloc_semaphore(name="cc%d" % i) for i in range(NL)]
        self.cm = self.nc.cleanup_on_exit()
        self.cm.__enter__()
        self.S = Sched(self.nc, self.es)
        self.dram = {}
        self.bufs = {}

    def sbt(self, name, shape, dt):
        self.uid = getattr(self, "uid", 0) + 1
        return self.nc.sbuf_tensor("%s_%d" % (name, self.uid), shape, dt)

    def pst(self, name, shape, dt):
        self.uid = getattr(self, "uid", 0) + 1
        return self.nc.psum_tensor("%s_%d" % (name, self.uid), shape, dt)

    def ext_in(self, name, shape, dt=F32):
        t = self.nc.dram_tensor(name, list(shape), dt, kind="ExternalInput")
        self.dram[name] = t
        self.bufs[name] = Buf(name, multi=True)
        return t

    def ext_out(self, name, shape, dt=F32):
        t = self.nc.dram_tensor(name, list(shape), dt, kind="ExternalOutput")
        self.dram[name] = t
        self.bufs[name] = Buf(name, multi=True)
        return t

    def scratch(self, name, shape, dt):
        t = self.nc.dram_tensor(name, list(shape), dt)
        self.shapes = getattr(self, "shapes", {})
        self.shapes[name] = (list(shape), dt)
        self.dram[name] = t
        self.bufs[name] = Buf(name, multi=True)
        return t

    def B(self, name):
        return self.bufs[name]

    def ln_ops(self, tag, src, src_b, dst, dst_b, g_bc, b_bc, tmp, eps=1e-5):
        S = self.S
        st, mv, sd = tmp["st"], tmp["mv"], tmp["sd"]
        bst = tmp["b"]
        def f_stats(e):
            e.bn_stats(out=st[:, 0, :], in_=src[:, 0:512])
            return e.bn_stats(out=st[:, 1, :], in_=src[:, 512:1024])
        S.op("dve", f_stats, reads=[src_b], writes=[tmp["b0"]])
        S.op("dve", lambda e: e.bn_aggr(out=mv[:, :], in_=st[:, :, :]), reads=[tmp["b0"]], writes=[bst])
        S.op("act", lambda e: e.activation(out=sd[:, 0:1], in_=mv[:, 1:2], func=AF.Sqrt, bias=tmp["eps"][:, 0:1], scale=1.0),
             reads=[bst], writes=[tmp["b2"]])
        S.op("dve", lambda e: e.reciprocal(out=sd[:, 1:2], in_=sd[:, 0:1]), reads=[tmp["b2"]], writes=[tmp["b3"]])
        S.op("dve", lambda e: e.tensor_scalar(out=dst[:, :], in0=src[:, :], scalar1=mv[:, 0:1], scalar2=sd[:, 1:2],
                                              op0=ALU.subtract, op1=ALU.mult),
             reads=[src_b, bst, tmp["b3"]], writes=[dst_b])
        S.op("pool", lambda e: e.tensor_tensor(out=dst[:, :], in0=dst[:, :], in1=g_bc[:, :], op=ALU.mult),
             reads=[dst_b, tmp["gb"]], writes=[dst_b])
        S.op("dve", lambda e: e.tensor_tensor(out=dst[:, :], in0=dst[:, :], in1=b_bc[:, :], op=ALU.add),
             reads=[dst_b, tmp["gb"]], writes=[dst_b])

    def transpose_ops(self, src_bf, src_b, psT, psT_b, dstT, dstT_b, col0, ident, ident_b, nk=8):
        S = self.S
        def f_tr(e):
            for kc in range(nk):
                i = e.transpose(out=psT[:, kc, :], in_=src_bf[:, kc * 128:(kc + 1) * 128], identity=ident[:, :])
            return i
        S.op("pe", f_tr, reads=[src_b, ident_b], writes=[psT_b])
        S.op("dve", lambda e: e.tensor_copy(out=dstT[:, 0:nk, col0:col0 + 128], in_=psT[:, 0:nk, :]),
             reads=[psT_b], writes=[dstT_b])

    def ln_consts(self, es, g_ap, b_ap, tagp):
        nc, S = self.nc, self.S
        sb = lambda n, sh, dt: es.enter_context(self.sbt(tagp + n, sh, dt))
        c = {}
        c["g"] = sb("g_bc", [128, D], F32); c["bb"] = sb("b_bc", [128, D], F32)
        c["eps"] = sb("eps", [128, 1], F32)
        c["ident"] = sb("identb", [128, 128], BF16); c["identf"] = sb("identf", [128, 128], F32)
        c["gb"] = Buf("gb"); c["ident_b"] = Buf("ident")
        S.dma("sp", c["g"][:, :], g_ap.partition_broadcast(128), writes=[c["gb"]])
        S.dma("sp", c["bb"][:, :], b_ap.partition_broadcast(128), writes=[c["gb"]])
        S.dma("sp", c["identf"][:, :], self.ident_f[:, :], writes=[c["ident_b"]])
        S.op("dve", lambda e: e.tensor_copy(out=c["ident"][:, :], in_=c["identf"][:, :]), reads=[c["ident_b"]], writes=[c["ident_b"]])
        S.op("dve", lambda e: e.memset(c["eps"][:, :], 1e-5), writes=[c["gb"]])
        c["tmp"] = []
        for i in range(2):
            t = {"st": sb("st%d" % i, [128, 2, 6], F32), "mv": sb("mv%d" % i, [128, 2], F32),
                 "sd": sb("sd%d" % i, [128, 2], F32), "b": Buf(), "b0": Buf(), "b2": Buf(), "b3": Buf(),
                 "gb": c["gb"], "eps": c["eps"]}
            c["tmp"].append(t)
        return c

    def phase_ln_emb(self):
        nc, S = self.nc, self.S
        self.hTv = self.hT.ap().rearrange("(kc p) t -> p kc t", p=128)
        with ExitStack() as es:
            sb = lambda n, sh, dt: es.enter_context(self.sbt("le_" + n, sh, dt))
            c = self.ln_consts(es, self.ln_emb_g.ap(), self.ln_emb_b.ap(), "le_")
            xt = [sb("xt%d" % i, [128, D], F32) for i in range(2)]; xt_b = [Buf() for _ in range(2)]
            yt = [sb("yt%d" % i, [128, D], F32) for i in range(2)]; yt_b = [Buf() for _ in range(2)]
            yb = [sb("yb%d" % i, [128, D], BF16) for i in range(2)]; yb_b = [Buf() for _ in range(2)]
            hts = [sb("hts%d" % i, [128, 8, 512], BF16) for i in range(2)]; hts_b = [Buf() for _ in range(2)]
            psT = [es.enter_context(self.pst("le_psT%d" % i, [128, 8, 128], BF16)) for i in range(2)]
            psT_b = [Buf() for _ in range(2)]
            hTv = self.hT.ap().rearrange("(kc p) t -> p kc t", p=128)
            for tt in range(T // 128):
                i = tt % 2
                S.dma("sp", xt[i][:, :], self.x[tt * 128:(tt + 1) * 128, :], reads=[self.B("x")], writes=[xt_b[i]])
                self.ln_ops("le", xt[i], xt_b[i], yt[i], yt_b[i], c["g"], c["bb"], c["tmp"][i])
                S.dma("pool", self.hA[tt * 128:(tt + 1) * 128, :], yt[i][:, :], reads=[yt_b[i]], writes=[self.B("hA")])
                S.op("act", lambda e, i=i: e.activation(out=yb[i][:, :], in_=yt[i][:, :], func=AF.Copy),
                     reads=[yt_b[i]], writes=[yb_b[i]])
                j = (tt // 4) % 2
                self.transpose_ops(yb[i], yb_b[i], psT[i], psT_b[i], hts[j], hts_b[j], (tt % 4) * 128, c["ident"], c["ident_b"])
                if tt % 4 == 3:
                    c0 = (tt // 4) * 512
                    S.dma("pool", hTv[:, :, c0:c0 + 512], hts[j][:, :, :], reads=[hts_b[j]], writes=[self.B("hT")])
            S.emit()

    def dump(self, names):
        outs = []
        for n in names:
            shape, dt = self.shapes[n]
            o = self.ext_out("dbg_" + n, shape, dt)
            src, dst = self.dram[n].ap(), o.ap()
            R = shape[0]
            step = max(1, R // 4)
            for r0 in range(0, R, step):
                self.S.dma("sp", dst[r0:min(R, r0 + step)], src[r0:min(R, r0 + step)], reads=[self.B(n)], writes=[self.B("dbg_" + n)])
            outs.append(self.B("dbg_" + n))
        return outs

    def finish(self, out_bufs):
        self.S.final_wait("sp", out_bufs)
        self.S.final_wait("pool", out_bufs)
        self.S.emit()
        self.es.close()
        self.cm.__exit__(None, None, None)


    LAYER_W = (("w_in", [D, IN_COLS]), ("mla_q_norm", [256]), ("mla_w_uq", [256, 384]), ("mla_kv_norm", [128]),
               ("mla_w_ukv", [128, 512]), ("swa_sink", [4]), ("w_branch", [4 * 256, D]), ("w_out", [D, D]),
               ("ln1_g", [D]), ("ln1_b", [D]), ("w_ff1", [D, 4 * D]), ("w_ff2", [4 * D, D]), ("w_ple", [256, D]),
               ("w_ple_gate", [D, D]), ("ln2_g", [D]), ("ln2_b", [D]), ("p", [T, 256]), ("tabC", [4 * 33, 128, 128]))

    def declare_F(self):
        e = self.ext_in
        self.x = e("x", [T, D]); self.ident_f = e("ident", [128, 128])
        self.ln_emb_g = e("ln_emb_g", [D]); self.ln_emb_b = e("ln_emb_b", [D])
        self.ropeq = e("ropeq", [2, 32, T]); self.ropek = e("ropek", [2, 32, SEQ])
        self.tabB = e("tabB", [4 * 9, 128, 128])
        self.tabD = [e("tabD%d" % g, [4 * 6, 128, 128]) for g in range(3)]
        self.LW = [{n: e("%s_%d" % (n, l), sh) for n, sh in self.LAYER_W} for l in range(NL)]
        self.out = self.ext_out("out", [T, D])
        s = self.scratch
        self.hA_l = [s("hA%d" % l, [T, D], F32) for l in range(NL)]
        self.hT_l = [s("hT%d" % l, [D, T], BF16) for l in range(NL)]
        self.G_l = [s("G%d" % l, [8, 256, T], BF16) for l in range(NL)]
        self.hTd = s("hTd", [D, T], BF16)
        self.h1T = s("h1T", [D, T], BF16); self.r2 = s("r2", [T, D], F32)
        self.yT = [s("yT%d" % n, [256, T], BF16) for n in range(4)]
        self.Og = {n: s("Og" + n, [T, 260], F32) for n in ("A", "B", "C", "D0", "D1", "D2")}
        self.vS = s("vS", [T + 2048, 260], BF16)
        self.hA, self.hT = self.hA_l[0], self.hT_l[0]
        self.bufs["hA"] = self.bufs["hA0"]; self.bufs["hT"] = self.bufs["hT0"]
        self.init_sems()

    def set_layer(self, l):
        W = self.LW[l]
        for n, _ in self.LAYER_W:
            setattr(self, n, W[n])
            self.bufs[n] = self.bufs["%s_%d" % (n, l)]
        self.hA, self.hTo, self.G = self.hA_l[l], self.hT_l[l], self.G_l[l]
        self.bufs["hA"] = self.bufs["hA%d" % l]; self.bufs["hTo"] = self.bufs["hT%d" % l]; self.bufs["G"] = self.bufs["G%d" % l]
        if l < NL - 1:
            self.hA_o, self.hT_o = self.hA_l[l + 1], self.hT_l[l + 1]
            self.bufs["hA_out"] = self.bufs["hA%d" % (l + 1)]; self.bufs["hT_out"] = self.bufs["hT%d" % (l + 1)]
        else:
            self.hA_o, self.hT_o = self.out, self.hTd
            self.bufs["hA_out"] = self.bufs["out"]; self.bufs["hT_out"] = self.bufs["hTd"]
        self.hTo_v = self.hTo.ap().rearrange("(kc p) t -> p kc t", p=128)
        self.G_v = [self.G[:, r * 128:(r + 1) * 128, :].rearrange("kc p t -> p kc t") for r in range(2)]

    def allgather(self, l):
        S = self.S
        hT, G = self.hT_l[l], self.G_l[l]
        for j in range(8):
            S.collective(lambda e, j=j: e.collective_compute("AllGather", ALU.bypass,
                                                             replica_groups=[[0, 1], [2, 3], [4, 5], [6, 7]],
                                                             ins=[hT[j * 128:(j + 1) * 128, :].opt()], outs=[G[j].opt()]),
                         self.cc_sems[l], j + 1, reads=[self.B("hT%d" % l)], writes=[self.B("G%d" % l)])
        S.op("pool", lambda e: e.nop(), reads=[self.B("G%d" % l)])
        S.emit()

    def init_sems(self):
        with self.nc.Block() as block:
            @block.gpsimd
            def _(g):
                for sm in self.S.all_sems:
                    g.sem_clear(sm)

    def declare_A(self):
        e = self.ext_in
        self.x = e("x", [T, D]); self.ident_f = e("ident", [128, 128])
        self.ln_emb_g = e("ln_emb_g", [D]); self.ln_emb_b = e("ln_emb_b", [D])
        self.hA = self.ext_out("hA_out", [T, D]); self.hT = self.ext_out("hT_out", [D, T], BF16)
        self.bufs["hA"] = self.bufs["hA_out"]; self.bufs["hT"] = self.bufs["hT_out"]
        self.init_sems()

    def declare_B(self):
        e = self.ext_in
        self.hA = e("hA", [T, D]); self.hTo = e("hTo", [D, T], BF16); self.G = e("G", [2 * D, T], BF16)
        self.p = e("p", [T, 256]); self.ident_f = e("ident", [128, 128])
        self.w_in = e("w_in", [D, IN_COLS])
        self.mla_q_norm = e("mla_q_norm", [256]); self.mla_w_uq = e("mla_w_uq", [256, 384])
        self.mla_kv_norm = e("mla_kv_norm", [128]); self.mla_w_ukv = e("mla_w_ukv", [128, 512])
        self.swa_sink = e("swa_sink", [4])
        self.w_branch = e("w_branch", [4 * 256, D]); self.w_out = e("w_out", [D, D])
        self.ln1_g = e("ln1_g", [D]); self.ln1_b = e("ln1_b", [D])
        self.w_ff1 = e("w_ff1", [D, 4 * D]); self.w_ff2 = e("w_ff2", [4 * D, D])
        self.w_ple = e("w_ple", [256, D]); self.w_ple_gate = e("w_ple_gate", [D, D])
        self.ln2_g = e("ln2_g", [D]); self.ln2_b = e("ln2_b", [D])
        self.ropeq = e("ropeq", [2, 32, T]); self.ropek = e("ropek", [2, 32, SEQ])
        self.tabB = e("tabB", [4 * 9, 128, 128]); self.tabC = e("tabC", [4 * 33, 128, 128])
        self.tabD = [e("tabD%d" % g, [4 * 6, 128, 128]) for g in range(3)]
        self.hA_o = self.ext_out("hA_out", [T, D]); self.hT_o = self.ext_out("hT_out", [D, T], BF16)
        s = self.scratch
        self.h1T = s("h1T", [D, T], BF16); self.r2 = s("r2", [T, D], F32)
        self.yT = [s("yT%d" % n, [256, T], BF16) for n in range(4)]
        self.Og = {n: s("Og" + n, [T, 260], F32) for n in ("A", "B", "C", "D0", "D1", "D2")}
        self.vS = s("vS", [T + 2048, 260], BF16)
        self.hTo_v = self.hTo.ap().rearrange("(kc p) t -> p kc t", p=128)
        self.G_v = [self.G[r * D:(r + 1) * D, :].rearrange("(kc p) t -> p kc t", p=128) for r in range(2)]
        self.init_sems()

    def wload(self, dst, src, c0=0, eng="pool", wb=None):
        n = src.shape[1]
        kc = src.shape[0] // 128
        v = src.rearrange("(kc p) n -> p kc n", p=128)
        for a in range(0, n, 2048):
            b = min(n, a + 2048)
            self.S.dma(eng, dst[:, 0:kc, c0 + a:c0 + b], v[:, :, a:b], writes=[wb])

    def hT_src(self, e0, n, H):
        t0 = e0 - H
        if t0 < 0:
            return self.G_v[0][:, :, T + t0:T + t0 + n], self.B("G")
        if t0 >= T:
            return self.G_v[1][:, :, t0 - T:t0 - T + n], self.B("G")
        return self.hTo_v[:, :, t0:t0 + n], self.B("hTo")

    def phase_banded(self, tag, qcols, kcols, vcols, nkv, r, halo_f, blocks_fn, nslots, tab, og, dup_k):
        nc, S = self.nc, self.S
        H = r * halo_f
        E = T + 2 * H
        L = T // r
        nb = L // 128
        nkb = (L + 2 * halo_f) // 128
        with ExitStack() as es:
            sb = lambda n, sh, dt: es.enter_context(self.sbt(tag + n, sh, dt))
            wq = sb("wq", [128, 8, 256], BF16); wk = sb("wk", [128, 8, 256], BF16); wv = sb("wv", [128, 8, 256], BF16)
            w_b = Buf(); wq_b = Buf(); wv_b = Buf(); w_b_k = w_b
            if dup_k:
                for j in range(2):
                    for d in range(2):
                        self.wload(wk, self.w_in[:, kcols + j * 64:kcols + j * 64 + 64], c0=j * 128 + d * 64, wb=w_b)
            else:
                self.wload(wk, self.w_in[:, kcols:kcols + 256], wb=w_b)
            self.wload(wq, self.w_in[:, qcols:qcols + 256], wb=wq_b)
            self.wload(wv, self.w_in[:, vcols:vcols + nkv * 64], wb=wv_b)
            qT = [sb("qT%d" % m, [128, T], BF16) for m in range(2)]
            kT = [sb("kT%d" % m, [128, E], BF16) for m in range(2)]
            qk_b = Buf()
            tabs = sb("tab", [128, 4 * nslots, 128], F32); tab_b = Buf()
            for h in range(4):
                S.dma("sp", tabs[:, h * nslots:(h + 1) * nslots, :],
                      tab[h * nslots:(h + 1) * nslots, :, :].rearrange("s k q -> k s q"), writes=[tab_b])
            hc = [sb("hc%d" % i, [128, 8, 512], BF16) for i in range(2)]; hc_b = [Buf() for _ in range(2)]
            vst = [sb("vst%d" % i, [128, 4, nkv, 65], BF16) for i in range(2)]
            vst_b = [[Buf() for s_ in range(4)] for _ in range(2)]
            for i in range(2):
                S.op("pool", lambda e, i=i: e.memset(vst[i][:, :, :, :], 1.0), writes=vst_b[i])
            es2 = ExitStack()
            pp = [es2.enter_context(self.pst(tag + "pp%d" % i, [128, 512], F32)) for i in range(2)]
            pp_b = [Buf() for _ in range(2)]
            ppi = [0]

            def proj_fm(wt, c0, hci, n, dst, d0):
                w_b = wq_b if wt is wq else w_b_k
                i = ppi[0] % 2; ppi[0] += 1
                def f(e):
                    for kc in range(8):
                        ins = e.matmul(pp[i][:, 0:n], lhsT=wt[:, kc, c0:c0 + 128], rhs=hc[hci][:, kc, 0:n],
                                       start=(kc == 0), stop=(kc == 7))
                    return ins
                S.op("pe", f, reads=[w_b, hc_b[hci]], writes=[pp_b[i]])
                S.op("act", lambda e: e.activation(out=dst[:, d0:d0 + n], in_=pp[i][:, 0:n], func=AF.Copy),
                     reads=[pp_b[i]], writes=[qk_b])

            chunks = []
            e0 = 0
            while e0 < E:
                t0 = e0 - H
                if t0 < 0:
                    n = min(512, -t0)
                elif t0 >= T:
                    n = min(512, E - e0)
                else:
                    n = min(512, T - t0)
                chunks.append((e0, n)); e0 += n
            vcnt = 0
            for ci, (e0, n) in enumerate(chunks):
                hci = ci % 2
                src, src_b = self.hT_src(e0, n, H)
                S.dma("sp", hc[hci][:, :, 0:n], src, reads=[src_b], writes=[hc_b[hci]])
                own = 0 <= e0 - H < T
                for m in range(2):
                    proj_fm(wk, m * 128, hci, n, kT[m], e0)
                    if own:
                        proj_fm(wq, m * 128, hci, n, qT[m], e0 - H)
                vi = ci % 2
                nsub = (n + 127) // 128
                for si, s0 in enumerate(range(0, n, 128)):
                    ns = min(128, n - s0)
                    i = ppi[0] % 2; ppi[0] += 1
                    def f(e, i=i, s0=s0, ns=ns, hci=hci):
                        for kc in range(8):
                            ins = e.matmul(pp[i][0:ns, 0:nkv * 64], lhsT=hc[hci][:, kc, s0:s0 + ns],
                                           rhs=wv[:, kc, 0:nkv * 64], start=(kc == 0), stop=(kc == 7))
                        return ins
                    S.op("pe", f, reads=[wv_b, hc_b[hci]], writes=[pp_b[i]])
                    S.op("dve", lambda e, i=i, vi=vi, ns=ns, si=si: e.tensor_copy(
                        out=vst[vi][0:ns, si, :, 0:64], in_=pp[i][0:ns, 0:nkv * 64].rearrange("p (h d) -> p h d", d=64)),
                        reads=[pp_b[i]], writes=[vst_b[vi][si]])
                if n % 128 == 0:
                    S.dma("pool", self.vS[e0:e0 + n, 0:nkv * 65].rearrange("(s p) c -> p s c", p=128),
                          vst[vi][:, 0:nsub, :, :].rearrange("p s h d -> p s (h d)"),
                          reads=vst_b[vi][0:nsub], writes=[self.B("vS")])
                else:
                    S.dma("pool", self.vS[e0:e0 + n, 0:nkv * 65], vst[vi][0:n, 0, :, :].rearrange("p h d -> p (h d)"),
                          reads=vst_b[vi][0:1], writes=[self.B("vS")])
            vx = sb("vx", [128, r * nkb, nkv * 65], BF16); vx_b = Buf()
            for rho in range(r):
                for m in range(nkb):
                    row0 = H + rho + r * (128 * m - halo_f)
                    srcv = self.vS[row0:row0 + 127 * r + 1:r, 0:nkv * 65] if r > 1 else self.vS[row0:row0 + 128, 0:nkv * 65]
                    S.dma("sp", vx[:, rho * nkb + m, :], srcv, reads=[self.B("vS")], writes=[vx_b])
            S.emit()
            es2.close()
            nkmax = max(len(blocks_fn(b, nb)[0]) for b in range(nb))
            NSET = 2 if nkmax <= 4 else 1
            ps = [[es.enter_context(self.pst(tag + "ps%d_%d" % (a, j), [128, 512 if nkmax <= 4 else 1024], F32)) for j in range(2)] for a in range(NSET)]
            ps_b = [[Buf() for j in range(2)] for a in range(NSET)]
            pob = [[es.enter_context(self.pst(tag + "po%d_%d" % (a, j), [128, 512], F32)) for j in range(2)] for a in range(2)]
            po_b = [[Buf() for j in range(2)] for a in range(2)]
            sbt = [[sb("sbt%d_%d" % (a, j), [128, nkmax * 128], F32) for j in range(2)] for a in range(2)]
            sbt_b = [[Buf() for j in range(2)] for a in range(2)]
            et = [[sb("et%d_%d" % (a, j), [128, nkmax * 128], BF16) for j in range(2)] for a in range(2)]
            et_b = [[Buf() for j in range(2)] for a in range(2)]
            ost = [sb("ost%d" % i, [128, 4, 65], F32) for i in range(2)]
            ost_b = [[Buf() for h in range(4)] for i in range(2)]
            units = [(rho, b, hp) for rho in range(r) for b in range(nb) for hp in range(2)]

            def stage_f(u):
                rho, b, hp = units[u]
                a = u % NSET
                ms, slot0 = blocks_fn(b, nb)
                q0 = rho + r * 128 * b
                for j in range(2):
                    h = 2 * hp + j
                    mq = h // 2
                    pr = slice((h % 2) * 64, (h % 2) * 64 + 64)
                    qap = qT[mq][pr, q0:q0 + 127 * r + 1:r] if r > 1 else qT[mq][pr, q0:q0 + 128]
                    def f(e, a=a, j=j, ms=ms, mq=mq, pr=pr, qap=qap, rho=rho):
                        for x, m in enumerate(ms):
                            k0 = H + rho + r * (128 * m - halo_f)
                            kap = kT[mq][pr, k0:k0 + 127 * r + 1:r] if r > 1 else kT[mq][pr, k0:k0 + 128]
                            ins = e.matmul(ps[a][j][:, x * 128:(x + 1) * 128], lhsT=kap, rhs=qap, start=True, stop=True)
                        return ins
                    S.op("pe", f, reads=[qk_b], writes=[ps_b[a][j]])

            def stage_m(u):
                rho, b, hp = units[u]
                a = u % NSET
                a2 = u % 2
                ms, slot0 = blocks_fn(b, nb)
                nk = len(ms)
                oi = (u // 2) % 2
                q0 = rho + r * 128 * b
                for j in range(2):
                    h = 2 * hp + j
                    bias = tabs[:, h * nslots + slot0:h * nslots + slot0 + nk, :].rearrange("k s q -> k (s q)")
                    S.op("dve", lambda e, a=a, a2=a2, j=j, nk=nk, bias=bias: e.scalar_tensor_tensor(
                        out=sbt[a2][j][:, 0:nk * 128], in0=ps[a][j][:, 0:nk * 128], scalar=0.125, in1=bias,
                        op0=ALU.mult, op1=ALU.add), reads=[ps_b[a][j], tab_b], writes=[sbt_b[a2][j]])
                for j in range(2):
                    S.op("act", lambda e, a2=a2, j=j, nk=nk: e.activation(out=et[a2][j][:, 0:nk * 128], in_=sbt[a2][j][:, 0:nk * 128], func=AF.Exp),
                         reads=[sbt_b[a2][j]], writes=[et_b[a2][j]])
                for j in range(2):
                    h = 2 * hp + j
                    hv = (h // 2) if dup_k else h
                    def g(e, a2=a2, j=j, ms=ms, hv=hv, rho=rho, nk=nk):
                        for x, m in enumerate(ms):
                            ins = e.matmul(pob[a2][j][:, 0:65], lhsT=et[a2][j][:, x * 128:(x + 1) * 128],
                                           rhs=vx[:, rho * nkb + m, hv * 65:(hv + 1) * 65], start=(x == 0), stop=(x == nk - 1))
                        return ins
                    S.op("pe", g, reads=[et_b[a2][j], vx_b], writes=[po_b[a2][j]])
                for j in range(2):
                    h = 2 * hp + j
                    S.op("act", lambda e, a2=a2, j=j, h=h, oi=oi: e.activation(out=ost[oi][:, h, :], in_=pob[a2][j][:, 0:65], func=AF.Copy),
                         reads=[po_b[a2][j]], writes=[ost_b[oi][h]])
                if hp == 1:
                    dst = og[q0:q0 + 127 * r + 1:r, :] if r > 1 else og[q0:q0 + 128, :]
                    S.dma("sp", dst, ost[oi][:, :, :].rearrange("p h d -> p (h d)"), reads=ost_b[oi], writes=[self.B(og.name)])

            if NSET == 2:
                stage_f(0)
            for u in range(len(units)):
                if NSET == 2:
                    if u + 1 < len(units):
                        stage_f(u + 1)
                else:
                    stage_f(u)
                stage_m(u)
            S.emit()

    def phase_combine(self, tag, ogs, yT, sink=False):
        nc, S = self.nc, self.S
        ng = len(ogs)
        with ExitStack() as es:
            sb = lambda n, sh, dt: es.enter_context(self.sbt(tag + n, sh, dt))
            identf = sb("idf", [128, 128], F32); ident = sb("idb", [128, 128], BF16); id_b = Buf()
            S.dma("sp", identf[:, :], self.ident_f[:, :], writes=[id_b])
            S.op("dve", lambda e: e.tensor_copy(out=ident[:, :], in_=identf[:, :]), reads=[id_b], writes=[id_b])
            esk = sb("esk", [128, 4], F32); esk_b = Buf()
            if sink:
                S.dma("sp", esk[:, :], self.swa_sink.ap().partition_broadcast(128), writes=[esk_b])
                S.op("act", lambda e: e.activation(out=esk[:, :], in_=esk[:, :], func=AF.Exp), reads=[esk_b], writes=[esk_b])
            NT4 = 4
            ot = [[sb("ot%d_%d" % (i, g), [128, 4, 65], F32) for g in range(ng)] for i in range(NT4)]
            ot_b = [[Buf() for g in range(ng)] for i in range(NT4)]
            den = [sb("den%d" % i, [128, 4], F32) for i in range(NT4)]; den_b = [Buf() for _ in range(NT4)]
            y = [sb("y%d" % i, [128, 256], BF16) for i in range(NT4)]; y_b = [[Buf() for h in range(4)] for _ in range(NT4)]
            psT = [es.enter_context(self.pst(tag + "psT%d" % i, [128, 8, 128], BF16)) for i in range(NT4)]
            psT_b = [Buf() for _ in range(NT4)]
            yts = [sb("yts%d" % i, [128, 2, 512], BF16) for i in range(2)]; yts_b = [[Buf() for k in range(4)] for _ in range(2)]
            yTv = yT.ap().rearrange("(kc p) t -> p kc t", p=128)
            for grp in range(T // 512):
                j = grp % 2
                tts = [grp * 4 + k for k in range(4)]
                for k, tt in enumerate(tts):
                    for g in range(ng):
                        S.dma("sp", ot[k][g][:, :, :].rearrange("p h d -> p (h d)"), ogs[g][tt * 128:(tt + 1) * 128, :],
                              reads=[self.B(ogs[g].name)], writes=[ot_b[k][g]])
                for g in range(1, ng):
                    for k in range(4):
                        S.op("dve", lambda e, k=k, g=g: e.tensor_tensor(out=ot[k][0][:, :, :], in0=ot[k][0][:, :, :], in1=ot[k][g][:, :, :], op=ALU.add),
                             reads=[ot_b[k][0], ot_b[k][g]], writes=[ot_b[k][0]])
                if sink:
                    for k in range(4):
                        S.op("dve", lambda e, k=k: e.tensor_tensor(out=den[k][:, :], in0=ot[k][0][:, :, 64], in1=esk[:, :], op=ALU.add),
                             reads=[ot_b[k][0], esk_b], writes=[den_b[k]])
                    for k in range(4):
                        S.op("dve", lambda e, k=k: e.reciprocal(out=den[k][:, :], in_=den[k][:, :]), reads=[den_b[k]], writes=[den_b[k]])
                else:
                    for k in range(4):
                        S.op("dve", lambda e, k=k: e.reciprocal(out=den[k][:, :], in_=ot[k][0][:, :, 64]), reads=[ot_b[k][0]], writes=[den_b[k]])
                for h in range(4):
                    for k in range(4):
                        S.op("dve", lambda e, k=k, h=h: e.tensor_scalar(out=y[k][:, h * 64:(h + 1) * 64], in0=ot[k][0][:, h, 0:64],
                                                                       scalar1=den[k][:, h:h + 1], scalar2=None, op0=ALU.mult),
                             reads=[ot_b[k][0], den_b[k]], writes=[y_b[k][h]])
                for k in range(4):
                    def f_tr(e, k=k):
                        e.transpose(out=psT[k][:, 0, :], in_=y[k][:, 0:128], identity=ident[:, :])
                        return e.transpose(out=psT[k][:, 1, :], in_=y[k][:, 128:256], identity=ident[:, :])
                    S.op("pe", f_tr, reads=y_b[k] + [id_b], writes=[psT_b[k]])
                for k in range(4):
                    S.op("dve", lambda e, k=k, j=j: e.tensor_copy(out=yts[j][:, 0:2, k * 128:(k + 1) * 128], in_=psT[k][:, 0:2, :]),
                         reads=[psT_b[k]], writes=[yts_b[j][k]])
                c0 = grp * 512
                S.dma("pool", yTv[:, :, c0:c0 + 512], yts[j][:, :, :], reads=yts_b[j], writes=[self.B(yT.name)])
            S.emit()

    def phase_mla(self):
        nc, S = self.nc, self.S
        w_in = self.w_in
        with ExitStack() as es:
            sb = lambda n, sh, dt: es.enter_context(self.sbt("ml_" + n, sh, dt))
            wq = sb("wq", [128, 8, 256], BF16); wkv = sb("wkv", [128, 8, 128], BF16)
            wkr = sb("wkr", [128, 8, 96], BF16); wkrs = sb("wkrs", [128, 8, 96], BF16)
            w_b = Buf()
            self.wload(wq, w_in[:, 0:256], wb=w_b); self.wload(wkv, w_in[:, 256:384], wb=w_b)
            self.wload(wkr, w_in[:, 256:320], c0=0, wb=w_b); self.wload(wkr, w_in[:, 384:416], c0=64, wb=w_b)
            self.wload(wkrs, w_in[:, 256:320], c0=0, wb=w_b); self.wload(wkrs, w_in[:, 400:416], c0=64, wb=w_b)
            self.wload(wkrs, w_in[:, 384:400], c0=80, wb=w_b)
            uqf = sb("uqf", [128, 2, 384], F32); gq = sb("gq", [128, 2], F32)
            uq = sb("uq", [128, 2, 384], BF16); uqs = sb("uqs", [128, 2, 384], BF16)
            ukvf = sb("ukvf", [128, 512], F32); gkv = sb("gkv", [128, 1], F32)
            ukv = sb("ukv", [128, 512], BF16); wv4 = sb("wv4", [128, 256], BF16)
            u_b = Buf()
            S.dma("sp", uqf[:, :, :], self.mla_w_uq.ap().rearrange("(kc p) n -> p kc n", p=128), writes=[u_b])
            for kc in range(2):
                S.dma("sp", gq[:, kc:kc + 1], self.mla_q_norm[kc * 128:(kc + 1) * 128].rearrange("(p o) -> p o", o=1), writes=[u_b])
            S.dma("sp", ukvf[:, :], self.mla_w_ukv[:, :], writes=[u_b])
            S.dma("sp", gkv[:, :], self.mla_kv_norm.ap().rearrange("(p o) -> p o", o=1), writes=[u_b])
            for kc in range(2):
                S.op("dve", lambda e, kc=kc: e.tensor_scalar(out=uq[:, kc, :], in0=uqf[:, kc, :], scalar1=gq[:, kc:kc + 1],
                                                              scalar2=None, op0=ALU.mult), reads=[u_b], writes=[u_b])
            S.op("dve", lambda e: e.tensor_copy(out=uqs[:, :, :], in_=uq[:, :, :]), reads=[u_b], writes=[u_b])
            for h in range(4):
                S.op("dve", lambda e, h=h: e.tensor_copy(out=uqs[:, :, h * 96 + 64:h * 96 + 80], in_=uq[:, :, h * 96 + 80:h * 96 + 96]),
                     reads=[u_b], writes=[u_b])
                S.op("dve", lambda e, h=h: e.tensor_copy(out=uqs[:, :, h * 96 + 80:h * 96 + 96], in_=uq[:, :, h * 96 + 64:h * 96 + 80]),
                     reads=[u_b], writes=[u_b])
            S.op("dve", lambda e: e.tensor_scalar(out=ukv[:, :], in0=ukvf[:, :], scalar1=gkv[:, 0:1], scalar2=None, op0=ALU.mult),
                 reads=[u_b], writes=[u_b])
            for h in range(4):
                S.op("dve", lambda e, h=h: e.tensor_copy(out=wv4[:, h * 64:(h + 1) * 64], in_=ukv[:, h * 128 + 64:h * 128 + 128]),
                     reads=[u_b], writes=[u_b])
            identf = sb("idf", [128, 128], F32); ident = sb("idb", [128, 128], BF16); id_b = Buf()
            S.dma("sp", identf[:, :], self.ident_f[:, :], writes=[id_b])
            S.op("dve", lambda e: e.tensor_copy(out=ident[:, :], in_=identf[:, :]), reads=[id_b], writes=[id_b])
            eps6 = sb("eps6", [128, 1], F32); eps_b = Buf()
            S.op("dve", lambda e: e.memset(eps6[:, :], 1e-6), writes=[eps_b])
            kT = [sb("kT%d" % h, [96, SEQ], BF16) for h in range(4)]; kT_b = Buf()
            qT = [sb("qT%d" % h, [96, T], BF16) for h in range(4)]; qT_b = Buf()
            vx = sb("vx", [128, SEQ // 128, 260], BF16); vx_b = Buf()
            S.op("pool", lambda e: e.memset(vx[:, :, :], 1.0), writes=[vx_b])
            hc = [sb("hc%d" % i, [128, 8, 512], BF16) for i in range(2)]; hc_b = [Buf() for _ in range(2)]
            rt = [sb("rt%d" % i, [96, 2, 512], F32) for i in range(2)]; rt_b = [Buf() for _ in range(2)]
            junk = sb("junk", [128, 256], F32); junk_b = Buf()
            ss = [sb("ss%d" % i, [128, 3], F32) for i in range(2)]; ss_b = [Buf() for _ in range(2)]
            cn = [sb("cn%d" % i, [128, 256], BF16) for i in range(2)]; cn_b = [Buf() for _ in range(2)]
            cnT = [sb("cnT%d" % i, [128, 2, 512], BF16) for i in range(2)]; cnT_b = [Buf() for _ in range(2)]
            t1 = [sb("t1%d" % i, [96, 512], F32) for i in range(2)]; t1_b = [Buf() for _ in range(2)]
            t2 = [sb("t2%d" % i, [96, 512], F32) for i in range(2)]; t2_b = [Buf() for _ in range(2)]
            with ExitStack() as es2:
                NP = 6
                pp = [es2.enter_context(self.pst("ml_pp%d" % i, [128, 512], F32)) for i in range(NP)]
                pp_b = [Buf() for _ in range(NP)]
                psT = [es2.enter_context(self.pst("ml_psT%d" % i, [128, 2, 128], BF16)) for i in range(2)]
                psT_b = [Buf() for _ in range(2)]
                ctr = [0]
                def nextp():
                    i = ctr[0] % NP; ctr[0] += 1
                    return i

                def norm_tile(hci, s, wt, ncol, nfeat, si):
                    i = nextp()
                    def f(e):
                        for kc in range(8):
                            ins = e.matmul(pp[i][:, 0:ncol], lhsT=hc[hci][:, kc, s * 128:(s + 1) * 128], rhs=wt[:, kc, 0:ncol],
                                           start=(kc == 0), stop=(kc == 7))
                        return ins
                    S.op("pe", f, reads=[w_b, hc_b[hci]], writes=[pp_b[i]])
                    S.op("act", lambda e: e.activation(out=junk[:, 0:ncol], in_=pp[i][:, 0:ncol], func=AF.Square, accum_out=ss[si][:, 0:1]),
                         reads=[pp_b[i]], writes=[junk_b, ss_b[si]])
                    S.op("act", lambda e: e.activation(out=ss[si][:, 1:2], in_=ss[si][:, 0:1], func=AF.Sqrt, bias=eps6[:, 0:1], scale=1.0 / nfeat),
                         reads=[ss_b[si], eps_b], writes=[ss_b[si]])
                    S.op("dve", lambda e: e.reciprocal(out=ss[si][:, 2:3], in_=ss[si][:, 1:2]), reads=[ss_b[si]], writes=[ss_b[si]])
                    S.op("dve", lambda e: e.tensor_scalar(out=cn[si][:, 0:ncol], in0=pp[i][:, 0:ncol], scalar1=ss[si][:, 2:3], scalar2=None, op0=ALU.mult),
                         reads=[pp_b[i], ss_b[si]], writes=[cn_b[si]])

                def rope_rows(pa, pb, ri, dsts, dst_b, c0, ti):
                    S.op("dve", lambda e: e.tensor_tensor(out=t1[ti][64:96, :], in0=pp[pa][64:96, :], in1=rt[ri][64:96, 0, :], op=ALU.mult),
                         reads=[pp_b[pa], rt_b[ri]], writes=[t1_b[ti]])
                    S.op("dve", lambda e: e.tensor_tensor(out=t2[ti][64:96, :], in0=pp[pb][64:96, :], in1=rt[ri][64:96, 1, :], op=ALU.mult),
                         reads=[pp_b[pb], rt_b[ri]], writes=[t2_b[ti]])
                    for d in dsts:
                        S.op("dve", lambda e, d=d: e.tensor_tensor(out=d[64:96, c0:c0 + 512], in0=t1[ti][64:96, :], in1=t2[ti][64:96, :], op=ALU.add),
                             reads=[t1_b[ti], t2_b[ti]], writes=[dst_b])

                for c in range(SEQ // 512):
                    hci = c % 2
                    S.dma("sp", hc[hci][:, :, :], self.G_v[c // 8][:, :, (c % 8) * 512:(c % 8) * 512 + 512], reads=[self.B("G")], writes=[hc_b[hci]])
                    S.dma("sp", rt[hci][64:96, :, :], self.ropek[:, :, c * 512:(c + 1) * 512].rearrange("a p t -> p a t"),
                          reads=[self.B("ropek")], writes=[rt_b[hci]])
                    for s in range(4):
                        si = (c * 4 + s) % 2
                        norm_tile(hci, s, wkv, 128, 128.0, si)
                        def ftr(e, si=si):
                            return e.transpose(out=psT[si][:, 0, :], in_=cn[si][:, 0:128], identity=ident[:, :])
                        S.op("pe", ftr, reads=[cn_b[si], id_b], writes=[psT_b[si]])
                        S.op("dve", lambda e, si=si, s=s, hci=hci: e.tensor_copy(out=cnT[hci][:, 0, s * 128:(s + 1) * 128], in_=psT[si][:, 0, :]),
                             reads=[psT_b[si]], writes=[cnT_b[hci]])
                        i = nextp()
                        S.op("pe", lambda e, i=i, s=s, hci=hci: e.matmul(pp[i][:, 0:256], lhsT=cnT[hci][:, 0, s * 128:(s + 1) * 128], rhs=wv4[:, :],
                                                                         start=True, stop=True), reads=[cnT_b[hci], u_b], writes=[pp_b[i]])
                        S.op("dve", lambda e, i=i, blk=c * 4 + s: e.tensor_copy(
                            out=vx[:, blk, :].rearrange("p (h d) -> p h d", d=65)[:, :, 0:64],
                            in_=pp[i][:, 0:256].rearrange("p (h d) -> p h d", d=64)), reads=[pp_b[i]], writes=[vx_b])
                    for h in range(4):
                        i = nextp()
                        S.op("pe", lambda e, i=i, h=h, hci=hci: e.matmul(pp[i][0:64, :], lhsT=ukv[:, h * 128:h * 128 + 64], rhs=cnT[hci][:, 0, :],
                                                                         start=True, stop=True), reads=[cnT_b[hci], u_b], writes=[pp_b[i]])
                        S.op("act", lambda e, i=i, h=h, c=c: e.activation(out=kT[h][0:64, c * 512:(c + 1) * 512], in_=pp[i][0:64, :], func=AF.Copy),
                             reads=[pp_b[i]], writes=[kT_b])
                    pa, pb = nextp(), nextp()
                    for (pi, wt) in ((pa, wkr), (pb, wkrs)):
                        def f(e, pi=pi, wt=wt, hci=hci):
                            for kc in range(8):
                                ins = e.matmul(pp[pi][0:96, :], lhsT=wt[:, kc, :], rhs=hc[hci][:, kc, :], start=(kc == 0), stop=(kc == 7))
                            return ins
                        S.op("pe", f, reads=[w_b, hc_b[hci]], writes=[pp_b[pi]])
                    rope_rows(pa, pb, hci, kT, kT_b, c * 512, hci)
                for c in range(T // 512):
                    hci = c % 2
                    S.dma("sp", hc[hci][:, :, :], self.hTo_v[:, :, c * 512:(c + 1) * 512], reads=[self.B("hTo")], writes=[hc_b[hci]])
                    S.dma("sp", rt[hci][64:96, :, :], self.ropeq[:, :, c * 512:(c + 1) * 512].rearrange("a p t -> p a t"),
                          reads=[self.B("ropeq")], writes=[rt_b[hci]])
                    for s in range(4):
                        si = (c * 4 + s) % 2
                        norm_tile(hci, s, wq, 256, 256.0, si)
                        def ftr(e, si=si):
                            e.transpose(out=psT[si][:, 0, :], in_=cn[si][:, 0:128], identity=ident[:, :])
                            return e.transpose(out=psT[si][:, 1, :], in_=cn[si][:, 128:256], identity=ident[:, :])
                        S.op("pe", ftr, reads=[cn_b[si], id_b], writes=[psT_b[si]])
                        S.op("dve", lambda e, si=si, s=s, hci=hci: e.tensor_copy(out=cnT[hci][:, :, s * 128:(s + 1) * 128], in_=psT[si][:, :, :]),
                             reads=[psT_b[si]], writes=[cnT_b[hci]])
                    for h in range(4):
                        pa, pb = nextp(), nextp()
                        for (pi, wt) in ((pa, uq), (pb, uqs)):
                            def f(e, pi=pi, wt=wt, hci=hci, h=h):
                                for kc in range(2):
                                    ins = e.matmul(pp[pi][0:96, :], lhsT=wt[:, kc, h * 96:(h + 1) * 96], rhs=cnT[hci][:, kc, :],
                                                   start=(kc == 0), stop=(kc == 1))
                                return ins
                            S.op("pe", f, reads=[u_b, cnT_b[hci]], writes=[pp_b[pi]])
                        S.op("act", lambda e, pa=pa, h=h, c=c: e.activation(out=qT[h][0:64, c * 512:(c + 1) * 512], in_=pp[pa][0:64, :], func=AF.Copy),
                             reads=[pp_b[pa]], writes=[qT_b])
                        rope_rows(pa, pb, hci, [qT[h]], qT_b, c * 512, h % 2)
                S.emit()
            NB3 = 3
            ps = [es.enter_context(self.pst("ml_ps%d" % i, [128, 512], F32)) for i in range(NB3)]; ps_b = [Buf() for _ in range(NB3)]
            po = [es.enter_context(self.pst("ml_po%d" % i, [128, 512], F32)) for i in range(4)]; po_b = [Buf() for _ in range(4)]
            et = [sb("et%d" % i, [128, 512], BF16) for i in range(NB3)]; et_b = [Buf() for _ in range(NB3)]
            ost = [[sb("ost%d_%d" % (i, q), [128, 4, 65], F32) for q in range(4)] for i in range(2)]
            ost_b = [[[Buf() for h in range(4)] for q in range(4)] for i in range(2)]
            sc = 96.0 ** -0.5
            NKB = SEQ // 128
            its = [(qc, h, kb) for qc in range(T // 512) for h in range(4) for kb in range(NKB)]

            def emit_st(n):
                qc, h, kb = its[n]
                i = n % NB3
                S.op("pe", lambda e: e.matmul(ps[i][:, :], lhsT=kT[h][0:96, kb * 128:(kb + 1) * 128],
                                              rhs=qT[h][0:96, qc * 512:(qc + 1) * 512], start=True, stop=True),
                     reads=[kT_b, qT_b], writes=[ps_b[i]])

            emit_st(0); emit_st(1)
            for n, (qc, h, kb) in enumerate(its):
                i = n % NB3
                oi = qc % 2
                if n + 2 < len(its):
                    emit_st(n + 2)
                S.op("act", lambda e, i=i: e.activation(out=et[i][:, :], in_=ps[i][:, :], func=AF.Exp, scale=sc),
                     reads=[ps_b[i]], writes=[et_b[i]])
                def g(e, i=i, h=h, kb=kb):
                    for qs in range(4):
                        ins = e.matmul(po[qs][:, 0:65], lhsT=et[i][:, qs * 128:(qs + 1) * 128], rhs=vx[:, kb, h * 65:(h + 1) * 65],
                                       start=(kb == 0), stop=(kb == NKB - 1))
                    return ins
                S.op("pe", g, reads=[et_b[i], vx_b], writes=po_b)
                if kb == NKB - 1:
                    for qs in range(4):
                        S.op("dve", lambda e, qs=qs, h=h, oi=oi: e.tensor_copy(out=ost[oi][qs][:, h, :], in_=po[qs][:, 0:65]),
                             reads=[po_b[qs]], writes=[ost_b[oi][qs][h]])
                    if h == 3:
                        for qs in range(4):
                            r0 = qc * 512 + qs * 128
                            S.dma("sp", self.Og["A"][r0:r0 + 128, :], ost[oi][qs][:, :, :].rearrange("p h d -> p (h d)"),
                                  reads=ost_b[oi][qs], writes=[self.B("OgA")])
            S.emit()

    def phase_merge1(self):
        nc, S = self.nc, self.S
        if not hasattr(self, "mT"):
            self.mT = self.scratch("mT", [D, T], BF16)
        mTv = self.mT.ap().rearrange("(kc p) t -> p kc t", p=128)
        yTv = [y.ap().rearrange("(kc p) t -> p kc t", p=128) for y in self.yT]
        with ExitStack() as es:
            sb = lambda n, sh, dt: es.enter_context(self.sbt("m1_" + n, sh, dt))
            wg = sb("wg", [128, 8, 4096], BF16); wb = sb("wb", [128, 8, 1024], BF16); w_b = Buf()
            self.wload(wg, self.w_in[:, 4000:8096], wb=w_b)
            self.wload(wb, self.w_branch[:, :], wb=w_b)
            hc = [sb("hc%d" % i, [128, 8, 512], BF16) for i in range(2)]; hc_b = [Buf() for _ in range(2)]
            yc = [[sb("yc%d_%d" % (i, n), [128, 2, 512], BF16) for n in range(4)] for i in range(2)]
            yc_b = [[Buf() for n in range(4)] for i in range(2)]
            mt = [sb("mt%d" % i, [128, 8, 512], BF16) for i in range(2)]; mt_b = [Buf() for _ in range(2)]
            gt = [sb("gt%d" % i, [128, 512], F32) for i in range(2)]; gt_b = [Buf() for _ in range(2)]
            acc = [sb("acc%d" % i, [128, 512], F32) for i in range(2)]; acc_b = [Buf() for _ in range(2)]
            tmp = [sb("tmp%d" % i, [128, 512], F32) for i in range(2)]; tmp_b = [Buf() for _ in range(2)]
            pg = [es.enter_context(self.pst("m1_pg%d" % i, [128, 512], F32)) for i in range(2)]; pg_b = [Buf() for _ in range(2)]
            pb = [es.enter_context(self.pst("m1_pb%d" % i, [128, 512], F32)) for i in range(2)]; pb_b = [Buf() for _ in range(2)]
            it = 0
            for c in range(T // 512):
                ci = c % 2
                S.dma("sp", hc[ci][:, :, :], self.hTo_v[:, :, c * 512:(c + 1) * 512], reads=[self.B("hTo")], writes=[hc_b[ci]])
                for n in range(4):
                    S.dma("sp", yc[ci][n][:, :, :], yTv[n][:, :, c * 512:(c + 1) * 512], reads=[self.B(self.yT[n].name)], writes=[yc_b[ci][n]])
                for dc in range(8):
                    ai = (c * 8 + dc) % 2
                    for n in range(4):
                        i = it % 2; it += 1
                        def f(e, i=i, n=n, dc=dc, ci=ci):
                            for kc in range(8):
                                ins = e.matmul(pg[i][:, :], lhsT=wg[:, kc, n * 1024 + dc * 128:n * 1024 + dc * 128 + 128], rhs=hc[ci][:, kc, :],
                                               start=(kc == 0), stop=(kc == 7))
                            return ins
                        S.op("pe", f, reads=[w_b, hc_b[ci]], writes=[pg_b[i]])
                        S.op("act", lambda e, i=i: e.activation(out=gt[i][:, :], in_=pg[i][:, :], func=AF.Sigmoid), reads=[pg_b[i]], writes=[gt_b[i]])
                        def f2(e, i=i, n=n, dc=dc, ci=ci):
                            for kc in range(2):
                                ins = e.matmul(pb[i][:, :], lhsT=wb[:, n * 2 + kc, dc * 128:dc * 128 + 128], rhs=yc[ci][n][:, kc, :],
                                               start=(kc == 0), stop=(kc == 1))
                            return ins
                        S.op("pe", f2, reads=[w_b, yc_b[ci][n]], writes=[pb_b[i]])
                        if n == 0:
                            S.op("dve", lambda e, i=i, ai=ai: e.tensor_tensor(out=acc[ai][:, :], in0=pb[i][:, :], in1=gt[i][:, :], op=ALU.mult),
                                 reads=[pb_b[i], gt_b[i]], writes=[acc_b[ai]])
                        else:
                            S.op("dve", lambda e, i=i: e.tensor_tensor(out=tmp[i][:, :], in0=pb[i][:, :], in1=gt[i][:, :], op=ALU.mult),
                                 reads=[pb_b[i], gt_b[i]], writes=[tmp_b[i]])
                            if n < 3:
                                S.op("pool", lambda e, i=i, ai=ai: e.tensor_tensor(out=acc[ai][:, :], in0=acc[ai][:, :], in1=tmp[i][:, :], op=ALU.add),
                                     reads=[acc_b[ai], tmp_b[i]], writes=[acc_b[ai]])
                            else:
                                S.op("pool", lambda e, i=i, ai=ai, dc=dc, ci=ci: e.tensor_tensor(out=mt[ci][:, dc, :], in0=acc[ai][:, :], in1=tmp[i][:, :], op=ALU.add),
                                     reads=[acc_b[ai], tmp_b[i]], writes=[mt_b[ci]])
                S.dma("pool", mTv[:, :, c * 512:(c + 1) * 512], mt[ci][:, :, :], reads=[mt_b[ci]], writes=[self.B("mT")])
            S.emit()

    def phase_merge2(self):
        nc, S = self.nc, self.S
        mTv = self.mT.ap().rearrange("(kc p) t -> p kc t", p=128)
        h1Tv = self.h1T.ap().rearrange("(kc p) t -> p kc t", p=128)
        with ExitStack() as es:
            sb = lambda n, sh, dt: es.enter_context(self.sbt("m2_" + n, sh, dt))
            c = self.ln_consts(es, self.ln1_g.ap(), self.ln1_b.ap(), "m2_")
            wo = sb("wo", [128, 8, 1024], BF16); wpg = sb("wpg", [128, 8, 1024], BF16); wpl = sb("wpl", [128, 2, 1024], BF16); w_b = Buf()
            self.wload(wo, self.w_out[:, :], wb=w_b); self.wload(wpg, self.w_ple_gate[:, :], wb=w_b); self.wload(wpl, self.w_ple[:, :], wb=w_b)
            mc = [sb("mc%d" % i, [128, 8, 512], BF16) for i in range(2)]; mc_b = [Buf() for _ in range(2)]
            ha = [sb("ha%d" % i, [128, D], F32) for i in range(2)]; ha_b = [Buf() for _ in range(2)]
            r1 = [sb("r1%d" % i, [128, D], F32) for i in range(2)]; r1_b = [Buf() for _ in range(2)]
            h1 = [sb("h1%d" % i, [128, D], F32) for i in range(2)]; h1_b = [Buf() for _ in range(2)]
            hb = [sb("hb%d" % i, [128, D], BF16) for i in range(2)]; hb_b = [Buf() for _ in range(2)]
            hts = [sb("hts%d" % i, [128, 8, 512], BF16) for i in range(2)]; hts_b = [Buf() for _ in range(2)]
            pt = [sb("pt%d" % i, [128, 256], F32) for i in range(2)]; pt_b = [Buf() for _ in range(2)]
            ptb = [sb("ptb%d" % i, [128, 256], BF16) for i in range(2)]; ptb_b = [Buf() for _ in range(2)]
            pT = [sb("pT%d" % i, [128, 2, 128], BF16) for i in range(2)]; pT_b = [Buf() for _ in range(2)]
            sg = [sb("sg%d" % i, [128, D], F32) for i in range(2)]; sg_b = [Buf() for _ in range(2)]
            ps = lambda n, sh, dt: es.enter_context(self.pst("m2_" + n, sh, dt))
            po = [ps("po%d" % i, [128, 512], F32) for i in range(2)]; po_b = [Buf() for _ in range(2)]
            pq = [ps("pq%d" % i, [128, 512], F32) for i in range(2)]; pq_b = [Buf() for _ in range(2)]
            pp = [ps("pp%d" % i, [128, 512], F32) for i in range(2)]; pp_b = [Buf() for _ in range(2)]
            psT = [ps("psT", [128, 8, 128], BF16)]; psT_b = [Buf()]
            psP = [ps("psP", [128, 2, 128], BF16)]; psP_b = [Buf()]
            for tt in range(T // 128):
                i = tt % 2
                cc, s = tt // 4, tt % 4
                ci = cc % 2
                if s == 0:
                    S.dma("sp", mc[ci][:, :, :], mTv[:, :, cc * 512:(cc + 1) * 512], reads=[self.B("mT")], writes=[mc_b[ci]])
                S.dma("sp", ha[i][:, :], self.hA[tt * 128:(tt + 1) * 128, :], reads=[self.B("hA")], writes=[ha_b[i]])
                S.dma("sp", pt[i][:, :], self.p[tt * 128:(tt + 1) * 128, :], reads=[self.B("p")], writes=[pt_b[i]])
                for hf in range(2):
                    def f(e, hf=hf, ci=ci, s=s):
                        for dc in range(8):
                            ins = e.matmul(po[hf][:, :], lhsT=mc[ci][:, dc, s * 128:(s + 1) * 128], rhs=wo[:, dc, hf * 512:(hf + 1) * 512],
                                           start=(dc == 0), stop=(dc == 7))
                        return ins
                    S.op("pe", f, reads=[w_b, mc_b[ci]], writes=[po_b[hf]])
                    S.op("dve", lambda e, hf=hf, i=i: e.scalar_tensor_tensor(out=r1[i][:, hf * 512:(hf + 1) * 512], in0=ha[i][:, hf * 512:(hf + 1) * 512],
                                                                              scalar=ALPHA, in1=po[hf][:, :], op0=ALU.mult, op1=ALU.add),
                         reads=[ha_b[i], po_b[hf]], writes=[r1_b[i]])
                self.ln_ops("m2", r1[i], r1_b[i], h1[i], h1_b[i], c["g"], c["bb"], c["tmp"][i])
                S.op("act", lambda e, i=i: e.activation(out=hb[i][:, :], in_=h1[i][:, :], func=AF.Copy), reads=[h1_b[i]], writes=[hb_b[i]])
                self.transpose_ops(hb[i], hb_b[i], psT[0], psT_b[0], hts[ci], hts_b[ci], s * 128, c["ident"], c["ident_b"])
                S.op("act", lambda e, i=i: e.activation(out=ptb[i][:, :], in_=pt[i][:, :], func=AF.Copy), reads=[pt_b[i]], writes=[ptb_b[i]])
                self.transpose_ops(ptb[i], ptb_b[i], psP[0], psP_b[0], pT[i], pT_b[i], 0, c["ident"], c["ident_b"], nk=2)
                for hf in range(2):
                    def f(e, hf=hf, ci=ci, s=s):
                        for kc in range(8):
                            ins = e.matmul(pq[hf][:, :], lhsT=hts[ci][:, kc, s * 128:(s + 1) * 128], rhs=wpg[:, kc, hf * 512:(hf + 1) * 512],
                                           start=(kc == 0), stop=(kc == 7))
                        return ins
                    S.op("pe", f, reads=[w_b, hts_b[ci]], writes=[pq_b[hf]])
                    S.op("act", lambda e, hf=hf, i=i: e.activation(out=sg[i][:, hf * 512:(hf + 1) * 512], in_=pq[hf][:, :], func=AF.Sigmoid),
                         reads=[pq_b[hf]], writes=[sg_b[i]])
                    def f2(e, hf=hf, i=i):
                        for kc in range(2):
                            ins = e.matmul(pp[hf][:, :], lhsT=pT[i][:, kc, :], rhs=wpl[:, kc, hf * 512:(hf + 1) * 512], start=(kc == 0), stop=(kc == 1))
                        return ins
                    S.op("pe", f2, reads=[w_b, pT_b[i]], writes=[pp_b[hf]])
                    S.op("dve", lambda e, hf=hf, i=i: e.tensor_tensor(out=sg[i][:, hf * 512:(hf + 1) * 512], in0=pp[hf][:, :], in1=sg[i][:, hf * 512:(hf + 1) * 512], op=ALU.mult),
                         reads=[pp_b[hf], sg_b[i]], writes=[sg_b[i]])
                S.op("dve", lambda e, i=i: e.scalar_tensor_tensor(out=sg[i][:, :], in0=h1[i][:, :], scalar=ALPHA, in1=sg[i][:, :], op0=ALU.mult, op1=ALU.add),
                     reads=[h1_b[i], sg_b[i]], writes=[sg_b[i]])
                S.dma("pool", self.r2[tt * 128:(tt + 1) * 128, :], sg[i][:, :], reads=[sg_b[i]], writes=[self.B("r2")])
                if s == 3:
                    S.dma("pool", h1Tv[:, :, cc * 512:(cc + 1) * 512], hts[ci][:, :, :], reads=[hts_b[ci]], writes=[self.B("h1T")])
            S.emit()

    def phase_ffn(self):
        nc, S = self.nc, self.S
        h1Tv = self.h1T.ap().rearrange("(kc p) t -> p kc t", p=128)
        hTov = self.hT_o.ap().rearrange("(kc p) t -> p kc t", p=128)
        with ExitStack() as es:
            sb = lambda n, sh, dt: es.enter_context(self.sbt("ff_" + n, sh, dt))
            c = self.ln_consts(es, self.ln2_g.ap(), self.ln2_b.ap(), "ff_")
            w1 = sb("w1", [128, 8, 4096], BF16); w2 = sb("w2", [128, 32, 1024], BF16); w_b = Buf(); w2_b = Buf()
            self.wload(w1, self.w_ff1[:, :], wb=w_b); self.wload(w2, self.w_ff2[:, :], wb=w2_b)
            hc = [sb("hc%d" % i, [128, 8, 256], BF16) for i in range(2)]; hc_b = [Buf() for _ in range(2)]
            uT = sb("uT", [128, 32, 256], BF16); uT_b = Buf()
            rl = [sb("rl%d" % i, [128, 256], F32) for i in range(2)]; rl_b = [Buf() for _ in range(2)]
            rt = [sb("rt%d" % i, [128, D], F32) for i in range(2)]; rt_b = [Buf() for _ in range(2)]
            ot = [sb("ot%d" % i, [128, D], F32) for i in range(2)]; ot_b = [Buf() for _ in range(2)]
            ob = [sb("ob%d" % i, [128, D], BF16) for i in range(2)]; ob_b = [Buf() for _ in range(2)]
            hts = [sb("hts%d" % i, [128, 8, 256], BF16) for i in range(2)]; hts_b = [Buf() for _ in range(2)]
            ps = lambda n, sh, dt: es.enter_context(self.pst("ff_" + n, sh, dt))
            pu = [ps("pu%d" % i, [128, 512], F32) for i in range(2)]; pu_b = [Buf() for _ in range(2)]
            po = [ps("po%d" % i, [128, 512], F32) for i in range(2)]; po_b = [Buf() for _ in range(2)]
            psT = [ps("psT%d" % i, [128, 8, 128], BF16) for i in range(2)]; psT_b = [Buf() for _ in range(2)]
            it = 0
            for c2 in range(T // 256):
                ci = c2 % 2
                S.dma("sp", hc[ci][:, :, :], h1Tv[:, :, c2 * 256:(c2 + 1) * 256], reads=[self.B("h1T")], writes=[hc_b[ci]])
                for fc in range(32):
                    i = it % 2; it += 1
                    def f(e, i=i, fc=fc, ci=ci):
                        for kc in range(8):
                            ins = e.matmul(pu[i][:, 0:256], lhsT=w1[:, kc, fc * 128:(fc + 1) * 128], rhs=hc[ci][:, kc, :], start=(kc == 0), stop=(kc == 7))
                        return ins
                    S.op("pe", f, reads=[w_b, hc_b[ci]], writes=[pu_b[i]])
                    S.op("act", lambda e, i=i: e.activation(out=rl[i][:, :], in_=pu[i][:, 0:256], func=AF.Relu), reads=[pu_b[i]], writes=[rl_b[i]])
                    S.op("dve", lambda e, i=i, fc=fc: e.tensor_tensor(out=uT[:, fc, :], in0=rl[i][:, :], in1=rl[i][:, :], op=ALU.mult),
                         reads=[rl_b[i]], writes=[uT_b])
                for s in range(2):
                    tt = c2 * 2 + s
                    i = tt % 2
                    S.dma("sp", rt[i][:, :], self.r2[tt * 128:(tt + 1) * 128, :], reads=[self.B("r2")], writes=[rt_b[i]])
                    for hf in range(2):
                        def f(e, hf=hf, s=s):
                            for fc in range(32):
                                ins = e.matmul(po[hf][:, :], lhsT=uT[:, fc, s * 128:(s + 1) * 128], rhs=w2[:, fc, hf * 512:(hf + 1) * 512],
                                               start=(fc == 0), stop=(fc == 31))
                            return ins
                        S.op("pe", f, reads=[w2_b, uT_b], writes=[po_b[hf]])
                        S.op("dve", lambda e, hf=hf, i=i: e.tensor_tensor(out=rt[i][:, hf * 512:(hf + 1) * 512], in0=po[hf][:, :], in1=rt[i][:, hf * 512:(hf + 1) * 512], op=ALU.add),
                             reads=[po_b[hf], rt_b[i]], writes=[rt_b[i]])
                    self.ln_ops("ff", rt[i], rt_b[i], ot[i], ot_b[i], c["g"], c["bb"], c["tmp"][i])
                    S.dma("pool", self.hA_o[tt * 128:(tt + 1) * 128, :], ot[i][:, :], reads=[ot_b[i]], writes=[self.B("hA_out")])
                    S.op("act", lambda e, i=i: e.activation(out=ob[i][:, :], in_=ot[i][:, :], func=AF.Copy), reads=[ot_b[i]], writes=[ob_b[i]])
                    self.transpose_ops(ob[i], ob_b[i], psT[i], psT_b[i], hts[ci], hts_b[ci], s * 128, c["ident"], c["ident_b"])
                S.dma("pool", hTov[:, :, c2 * 256:(c2 + 1) * 256], hts[ci][:, :, :], reads=[hts_b[ci]], writes=[self.B("hT_out")])
            S.emit()


NEG = -30000.0


def _t5_bucket(rel):
    nb, max_exact = 16, 8
    n = np.abs(rel)
    large = max_exact + (np.log(np.maximum(n, 1).astype(np.float32) / max_exact)
                         / math.log(1024 / max_exact) * (nb - max_exact)).astype(np.int32)
    large = np.minimum(large, nb - 1)
    return np.where(rel > 0, nb, 0) + np.where(n < max_exact, n, large)


def _banded_tab(rel_bias, heads, r, half, kb0, dms, hf):
    L = T // r
    nb = L // 128
    k = np.arange(128)[:, None]; q = np.arange(128)[None, :]
    out = []
    for h in heads:
        for mode in range(3):
            b = {0: 1, 1: 0, 2: nb - 1}[mode]
            if nb == 2 and mode == 0:
                b = 0
            for dm in dms:
                m = b + dm
                f = kb0 + 128 * m + k
                rel = f - (128 * b + q)
                val = rel_bias[_t5_bucket(rel * r), h].astype(np.float32)
                ok = np.abs(rel) <= half
                if mode == 1 and hf == 0:
                    ok = ok & (f >= 0)
                if mode == 2 and hf == 1:
                    ok = ok & (f < L)
                if mode == 0:
                    pass
                out.append(np.where(ok, val, NEG).astype(np.float32))
    return np.stack(out)


def _na_tab(rpb, hf):
    k = np.arange(128)[:, None]; q = np.arange(128)[None, :]
    a_loc, kc = k // 64, k % 64
    r_loc, c = q // 64, q % 64
    col_start = np.clip(c - 8, 0, 48)
    col_ok = (kc >= col_start) & (kc < col_start + 16)
    dc = np.clip(kc - c, -15, 15) + 15

    def tile(h, i, j):
        if j < 0 or j >= 64:
            return np.full((128, 128), NEG, np.float32)
        r = 2 * i + r_loc; a = 2 * j + a_loc
        start = np.clip(r - 4, 0, 120)
        ok = (a >= start) & (a < start + 8) & col_ok
        dr = np.clip(a - r + 7, 0, 14)
        return np.where(ok, rpb[h, dr, dc], NEG).astype(np.float32)
    out = []
    for h in range(4):
        for dj in range(-2, 3):
            out.append(tile(h, 10, 10 + dj))
        for b in (0, 1, 30, 31):
            i = hf * 32 + b
            for dj in range(-3, 4):
                out.append(tile(h, i, i + dj))
    return np.stack(out)


def _rope_tab(pos):
    half = 16
    inv = (10000.0 ** (-np.arange(half, dtype=np.float32) / half)).astype(np.float32)
    ang = pos.astype(np.float32)[None, :] * inv[:, None]
    cos, sin = np.cos(ang).astype(np.float32), np.sin(ang).astype(np.float32)
    return np.stack([np.concatenate([cos, cos], 0), np.concatenate([-sin, sin], 0)]).astype(np.float32)


_PROGS = {}


def build_A():
    k = K()
    k.declare_A()
    k.phase_ln_emb()
    k.finish([k.B("hA_out"), k.B("hT_out")])
    return k


def build_B(debug=()):
    k = K(debug=debug)
    k.declare_B()
    edge = {0: 0, 1: 1, 30: 2, 31: 3}
    k.phase_mla()
    k.phase_combine("ca_", [k.Og["A"]], k.yT[0])
    k.phase_banded("sw_", 416, 672, 800, 2, 1, 128, lambda b, nb: ([b, b + 1, b + 2], 3 if b == 0 else (6 if b == nb - 1 else 0)),
                   9, k.tabB, k.Og["B"], True)
    k.phase_combine("cb_", [k.Og["B"]], k.yT[1], sink=True)
    k.phase_banded("na_", 928, 1184, 1440, 4, 1, 384,
                   lambda b, nb: (list(range(b, b + 7)), 5 + 7 * edge[b]) if b in edge else (list(range(b + 1, b + 6)), 0),
                   33, k.tabC, k.Og["C"], False)
    k.phase_combine("cc_", [k.Og["C"]], k.yT[2])
    for g, r in enumerate((1, 4, 16)):
        k.phase_banded("d%d_" % g, 1696 + g * 256, 2464 + g * 256, 3232 + g * 256, 4, r, 64,
                       lambda b, nb: ([b, b + 1], 2 if b == 0 else (4 if b == nb - 1 else 0)), 6, k.tabD[g], k.Og["D%d" % g], False)
    k.phase_combine("cd_", [k.Og["D0"], k.Og["D1"], k.Og["D2"]], k.yT[3])
    k.phase_merge1()
    k.phase_merge2()
    k.phase_ffn()
    outs = [k.B("hA_out"), k.B("hT_out")]
    if debug:
        outs += k.dump(debug)
    k.finish(outs)
    return k


def _f(a):
    return np.ascontiguousarray(np.asarray(a, dtype=np.float32))


def build_F():
    k = K()
    k.declare_F()
    k.phase_ln_emb()
    edge = {0: 0, 1: 1, 30: 2, 31: 3}
    for l in range(NL):
        k.allgather(l)
        k.set_layer(l)
        k.phase_mla()
        k.phase_combine("ca_", [k.Og["A"]], k.yT[0])
        k.phase_banded("sw_", 416, 672, 800, 2, 1, 128, lambda b, nb: ([b, b + 1, b + 2], 3 if b == 0 else (6 if b == nb - 1 else 0)),
                       9, k.tabB, k.Og["B"], True)
        k.phase_combine("cb_", [k.Og["B"]], k.yT[1], sink=True)
        k.phase_banded("na_", 928, 1184, 1440, 4, 1, 384,
                       lambda b, nb: (list(range(b, b + 7)), 5 + 7 * edge[b]) if b in edge else (list(range(b + 1, b + 6)), 0),
                       33, k.tabC, k.Og["C"], False)
        k.phase_combine("cc_", [k.Og["C"]], k.yT[2])
        for g, r in enumerate((1, 4, 16)):
            k.phase_banded("d%d_" % g, 1696 + g * 256, 2464 + g * 256, 3232 + g * 256, 4, r, 64,
                           lambda b, nb: ([b, b + 1], 2 if b == 0 else (4 if b == nb - 1 else 0)), 6, k.tabD[g], k.Og["D%d" % g], False)
        k.phase_combine("cd_", [k.Og["D0"], k.Og["D1"], k.Og["D2"]], k.yT[3])
        k.phase_merge1()
        k.phase_merge2()
        k.phase_ffn()
    k.finish([k.B("out")])
    return k


def kernel(**inputs):
    x = _f(inputs["x"]); p = _f(inputs["p"])
    rel_bias = _f(inputs["rel_bias"]); na_rpb = _f(inputs["na_rpb"])
    ident = np.eye(128, dtype=np.float32)
    if "F" not in _PROGS:
        _PROGS["F"] = build_F()
    ropek = _rope_tab(np.arange(SEQ))
    shared = {"ident": ident, "ln_emb_g": _f(inputs["ln_emb_g"]), "ln_emb_b": _f(inputs["ln_emb_b"]), "ropek": ropek}
    for l in range(NL):
        for n in ("w_in", "mla_q_norm", "mla_w_uq", "mla_kv_norm", "mla_w_ukv", "swa_sink", "w_out", "ln1_g", "ln1_b",
                  "w_ff1", "w_ff2", "w_ple", "w_ple_gate", "ln2_g", "ln2_b"):
            shared["%s_%d" % (n, l)] = _f(inputs[n][l])
        shared["w_branch_%d" % l] = _f(inputs["w_branch"][l]).reshape(4 * 256, D)
    maps = []
    for c in range(8):
        b, hf = c // 2, c % 2
        m = dict(shared)
        m["x"] = np.ascontiguousarray(x[b, hf * T:(hf + 1) * T])
        m["ropeq"] = _rope_tab(hf * T + np.arange(T))
        m["tabB"] = _banded_tab(rel_bias, range(4), 1, 128, -128, (0, 1, 2), hf)
        for g, r in enumerate((1, 4, 16)):
            m["tabD%d" % g] = _banded_tab(rel_bias, range(4 + 4 * g, 8 + 4 * g), r, 64, -64, (0, 1), hf)
        for l in range(NL):
            m["p_%d" % l] = np.ascontiguousarray(p[l, b, hf * T:(hf + 1) * T])
            m["tabC_%d" % l] = _na_tab(na_rpb[l], hf)
        maps.append(m)
    res = run_bass_kernel_spmd(_PROGS["F"].nc, maps, core_ids=list(range(8)))
    out = np.zeros((4, SEQ, D), np.float32)
    for c in range(8):
        b, hf = c // 2, c % 2
        out[b, hf * T:(hf + 1) * T] = np.asarray(res.results[c]["out"], dtype=np.float32)
    return out
```
